# Optimizing a Trainium2 kernel written in Bass

```python
import jax, jax.numpy as jnp
from jax import lax
import numpy as np

D_MODEL = 1024
BATCH = 16
SEQ = 256
DEPTH = 4
DEC_BATCH = 4
DEC_SEQ = 4096
PAST_LEN = 256

GRID_W = 64
N_HEADS_A = 4
DK_A = 64
DV_A = 128
QK_W = N_HEADS_A * DK_A
V_W = N_HEADS_A * DV_A
GLA_LOWRANK = 16
GATE_NORMALIZER = 16.0
GLA_CHUNK = 64
SC_W = 512
SC_KERNEL = 3
FN_GROUPS = 4
FN_GW = 128
FN_W = FN_GROUPS * FN_GW
N_BRANCH = 3
D_FF = ((8 * D_MODEL // 3 + 255) // 256) * 256
N_MOD = 6
EPS = 1e-6
IN_SIZES = (QK_W, QK_W, V_W, V_W, GLA_LOWRANK, GLA_LOWRANK, SC_W, SC_W, SC_W, FN_W, N_BRANCH * D_MODEL)
P_IN = sum(IN_SIZES)

kernel_name = 'hybrid_gla_conv_fourier_dit_step'


def rms_norm(x, g):
    xf = x.astype(jnp.float32)
    y = xf * lax.rsqrt(jnp.mean(xf * xf, axis=-1, keepdims=True) + EPS)
    return (y * g.astype(jnp.float32)).astype(x.dtype)


def gla_scan(q, k, v, log_a, s0):
    bsz, nh, L, _ = q.shape
    n = L // GLA_CHUNK

    def blk(t):
        return t.reshape(bsz, nh, n, GLA_CHUNK, t.shape[-1])

    q, k, v, log_a = blk(q), blk(k), blk(v), blk(log_a)
    b = jnp.cumsum(log_a, axis=3)
    b_last = b[:, :, :, -1:, :]
    q_in = q * jnp.exp(b)
    k_in = k * jnp.exp(-b)
    k_out = k * jnp.exp(b_last - b)
    mask = jnp.tril(jnp.ones((GLA_CHUNK, GLA_CHUNK), dtype=bool))
    att = jnp.where(mask, jnp.einsum('bhntd,bhnsd->bhnts', q_in, k_in), 0.0)
    o_intra = jnp.einsum('bhnts,bhnse->bhnte', att, v)
    u = jnp.einsum('bhnsd,bhnse->bhnde', k_out, v)
    g = jnp.exp(b_last[:, :, :, 0, :])

    def step(s, xs):
        g_n, u_n = xs
        return g_n[..., None] * s + u_n, s

    s_final, s_start = lax.scan(step, s0, (jnp.moveaxis(g, 2, 0), jnp.moveaxis(u, 2, 0)))
    s_start = jnp.moveaxis(s_start, 0, 2)
    o_inter = jnp.einsum('bhntd,bhnde->bhnte', q_in, s_start)
    return (o_intra + o_inter).reshape(bsz, nh, L, v.shape[-1]), s_final


def gla_bidir(q, k, v, la_f, la_b, s0_f, s0_b):
    o_f, s_f = gla_scan(q, k, v, la_f, s0_f)
    flip = lambda t: jnp.flip(t, axis=2)
    o_b, s_b = gla_scan(flip(q), flip(k), flip(v), flip(la_b), s0_b)
    return o_f + flip(o_b), s_f, s_b


def dwconv3(u, w, axis):
    n = u.shape[axis]
    pad = [(0, 0)] * u.ndim
    pad[axis] = (1, 1)
    up = jnp.pad(u, pad)
    return sum(w[i] * lax.slice_in_dim(up, i, i + n, axis=axis) for i in range(SC_KERNEL))


def token_mixing(h, s0_f, s0_b, rows, p):
    bsz, L, _ = h.shape
    f32 = jnp.float32
    pts = np.cumsum(IN_SIZES)[:-1].tolist()
    q, k, v, og, gkf, gkb, sb, sc, sx, fx, mg = jnp.split(h @ p['w_in'], pts, axis=-1)

    def heads(t):
        return t.reshape(bsz, L, N_HEADS_A, -1).transpose(0, 2, 1, 3).astype(f32)

    la_f = jax.nn.log_sigmoid((gkf @ p['w_gk_f'] + p['b_gk_f']).astype(f32)) / GATE_NORMALIZER
    la_b = jax.nn.log_sigmoid((gkb @ p['w_gk_b'] + p['b_gk_b']).astype(f32)) / GATE_NORMALIZER
    o, s_f, s_b = gla_bidir(heads(q) * DK_A ** -0.5, heads(k), heads(v), heads(la_f), heads(la_b),
                            s0_f.astype(f32), s0_b.astype(f32))
    o = o.transpose(0, 2, 1, 3)
    o = o * lax.rsqrt(jnp.mean(o * o, axis=-1, keepdims=True) + EPS) * p['gla_norm'].astype(f32)
    y_a = (o.reshape(bsz, L, V_W).astype(h.dtype) * jax.nn.silu(og)) @ p['w_a_out']

    u = sc * sx
    if rows is None:
        u = dwconv3(u, p['conv_w'], axis=1)
    else:
        u = dwconv3(u.reshape(bsz, rows, GRID_W, SC_W), p['conv_w'], axis=2).reshape(bsz, L, SC_W)
    y_b = (sb * u) @ p['w_b_out']

    fr = jnp.fft.fft2(fx.reshape(bsz, L, FN_GROUPS, FN_GW).astype(f32), axes=(1, 3), norm='ortho').real
    y_c = fr.reshape(bsz, L, FN_W).astype(h.dtype) @ p['w_c_out']

    g = jax.nn.sigmoid(mg.reshape(bsz, L, N_BRANCH, D_MODEL))
    y = g[:, :, 0] * y_a + g[:, :, 1] * y_b + g[:, :, 2] * y_c
    return y @ p['w_o'], s_f, s_b


def swiglu(h, w_up, w_down):
    gate, up = jnp.split(h @ w_up, 2, axis=-1)
    return (jax.nn.silu(gate) * up) @ w_down


def trunk_layer(x, cond, s0_f, s0_b, rows, p):
    m = jax.nn.silu(cond) @ p['w_ada'] + p['b_ada']
    sh1, sc1, g1, sh2, sc2, g2 = jnp.split(m[:, None, :], N_MOD, axis=-1)
    h = rms_norm(x, p['norm1']) * (1 + sc1) + sh1
    mix, s_f, s_b = token_mixing(h, s0_f, s0_b, rows, p)
    x = x + g1 * mix
    h = rms_norm(x, p['norm2']) * (1 + sc2) + sh2
    x = x + g2 * swiglu(h, p['w_up'], p['w_down'])
    return x, s_f, s_b


def setup_inputs(seed: int = 0) -> dict:
    key = jax.random.key(seed)
    ks = jax.random.split(key, 23)
    f = jnp.float32
    nrm = jax.random.normal

    def w(k, shape, fan_in):
        return nrm(k, shape, f) * fan_in ** -0.5

    def gain(k, shape):
        return 1.0 + 0.02 * nrm(k, shape, f)

    return {
        'x_prompt': nrm(ks[0], (BATCH, SEQ, D_MODEL), f),
        'x_sample': nrm(ks[1], (DEC_BATCH, DEC_SEQ, D_MODEL), f),
        'state_gla': nrm(ks[2], (DEC_BATCH, DEPTH, 2, N_HEADS_A, DK_A, DV_A), f),
        'c': nrm(ks[3], (DEC_BATCH, D_MODEL), f),
        'c_ctx': nrm(ks[4], (D_MODEL,), f),
        'w_ada': w(ks[5], (DEPTH, D_MODEL, N_MOD * D_MODEL), D_MODEL),
        'b_ada': 0.02 * nrm(ks[6], (DEPTH, N_MOD * D_MODEL), f),
        'norm1': gain(ks[7], (DEPTH, D_MODEL)),
        'norm2': gain(ks[8], (DEPTH, D_MODEL)),
        'w_in': w(ks[9], (DEPTH, D_MODEL, P_IN), D_MODEL),
        'w_gk_f': w(ks[10], (DEPTH, GLA_LOWRANK, QK_W), GLA_LOWRANK),
        'b_gk_f': 0.02 * nrm(ks[11], (DEPTH, QK_W), f),
        'w_gk_b': w(ks[12], (DEPTH, GLA_LOWRANK, QK_W), GLA_LOWRANK),
        'b_gk_b': 0.02 * nrm(ks[13], (DEPTH, QK_W), f),
        'gla_norm': gain(ks[14], (DEPTH, DV_A)),
        'w_a_out': w(ks[15], (DEPTH, V_W, D_MODEL), V_W),
        'conv_w': w(ks[16], (DEPTH, SC_KERNEL, SC_W), SC_KERNEL),
        'w_b_out': w(ks[17], (DEPTH, SC_W, D_MODEL), SC_W),
        'w_c_out': w(ks[18], (DEPTH, FN_W, D_MODEL), FN_W),
        'w_o': w(ks[19], (DEPTH, D_MODEL, D_MODEL), D_MODEL),
        'w_up': w(ks[20], (DEPTH, D_MODEL, 2 * D_FF), D_MODEL),
        'w_down': w(ks[21], (DEPTH, D_FF, D_MODEL), D_FF),
        'norm_f': gain(ks[22], (D_MODEL,)),
    }


def reference(x_prompt, x_sample, state_gla, c, c_ctx, w_ada, b_ada, norm1, norm2, w_in,
              w_gk_f, b_gk_f, w_gk_b, b_gk_b, gla_norm, w_a_out, conv_w, w_b_out, w_c_out,
              w_o, w_up, w_down, norm_f):
    xp, xs = x_prompt, x_sample
    rows = x_sample.shape[1] // GRID_W
    zero_state = jnp.zeros((x_prompt.shape[0], N_HEADS_A, DK_A, DV_A), jnp.float32)
    cond_ctx = c_ctx[None, :]
    ctx_states = []
    for l in range(DEPTH):
        p = {'w_ada': w_ada[l], 'b_ada': b_ada[l], 'norm1': norm1[l], 'norm2': norm2[l],
             'w_in': w_in[l], 'w_gk_f': w_gk_f[l], 'b_gk_f': b_gk_f[l], 'w_gk_b': w_gk_b[l],
             'b_gk_b': b_gk_b[l], 'gla_norm': gla_norm[l], 'w_a_out': w_a_out[l],
             'conv_w': conv_w[l], 'w_b_out': w_b_out[l], 'w_c_out': w_c_out[l], 'w_o': w_o[l],
             'w_up': w_up[l], 'w_down': w_down[l]}
        xp, s_f, s_b = trunk_layer(xp, cond_ctx, zero_state, zero_state, None, p)
        ctx_states.append(jnp.stack([s_f, s_b], axis=1))
        xs, _, _ = trunk_layer(xs, c, state_gla[:, l, 0], state_gla[:, l, 1], rows, p)
    new_state_gla = jnp.stack(ctx_states, axis=1).astype(x_prompt.dtype)
    y_prompt = rms_norm(xp, norm_f)
    y_sample = rms_norm(xs, norm_f)
    return (y_prompt, y_sample, new_state_gla)
```

```python
import contextlib
import math
import numpy as np
import ml_dtypes
import concourse.bass as bass
import concourse.mybir as mybir
from concourse.bass_utils import run_bass_kernel_spmd

F32 = mybir.dt.float32
BF16 = mybir.dt.bfloat16
AF = mybir.ActivationFunctionType
ALU = mybir.AluOpType

D = 1024
NCH = 8
TT = 512
LP = 256
NH, DK, DV = 4, 64, 128
QKW, VW = 256, 512
SCW, FNW, FNG = 512, 512, 4
DFF = 2816
NFF = 22
PIN = 6688
OFF_Q, OFF_K, OFF_V, OFF_OG, OFF_GKF, OFF_GKB = 0, 256, 512, 1024, 1536, 1552
OFF_SB, OFF_SC, OFF_SX, OFF_FX, OFF_MG = 1568, 2080, 2592, 3104, 3616
EPS = 1e-6
GRID_W = 64
GCH = 128

ENGS = ("pe", "act", "dve", "pool", "sp")
HND = {"pe": "tensor", "act": "scalar", "dve": "vector", "pool": "gpsimd", "sp": "sync"}


class Op:
    __slots__ = ("eng", "fn", "reads", "writes", "dma", "signal", "sigval", "key")

    def __init__(self, eng, fn, reads, writes, dma):
        self.eng, self.fn, self.reads, self.writes, self.dma = eng, fn, reads, writes, dma
        self.signal = False
        self.sigval = 0
        self.key = ("dma", dma) if dma is not None else ("eng", eng)


class Prog:
    SAME_ENGINE_SYNC = True

    def __init__(self, nc):
        self.nc = nc
        self.ops = []
        self.outer = contextlib.ExitStack()
        self.sems = {}
        self.sigcount = {}
        self.n = 0
        self.total = {e: 0 for e in ENGS}

    def sem(self, key):
        if key not in self.sems:
            nm = "s" + str(len(self.sems))
            self.sems[key] = self.outer.enter_context(self.nc.semaphore(nm))
            self.sigcount[key] = 0
        return self.sems[key]

    def alloc(self, stack, shape, dtype, psum=False):
        self.n += 1
        f = self.nc.psum_tensor if psum else self.nc.sbuf_tensor
        return stack.enter_context(f(f"t{self.n}", list(shape), dtype))

    def op(self, eng, fn, reads=(), writes=(), dma=None):
        self.ops.append(Op(eng, fn, tuple(reads), tuple(writes), dma))

    def mm(self, out, lhsT, rhs, start, stop, reads, writes):
        self.op("pe", lambda e: e.matmul(out, lhsT, rhs, start=start, stop=stop, skip_group_check=True),
                reads, writes)

    def act(self, out, in_, func, reads, writes, bias=0.0, scale=1.0):
        self.op("act", lambda e: e.activation(out=out, in_=in_, func=func, bias=bias, scale=scale), reads, writes)

    def tt(self, eng, out, in0, in1, op, reads, writes):
        self.op(eng, lambda e: e.tensor_tensor(out=out, in0=in0, in1=in1, op=op), reads, writes)

    def ts(self, eng, out, in0, s1, op0, reads, writes, s2=None, op1=None):
        if op1 is None:
            self.op(eng, lambda e: e.tensor_scalar(out=out, in0=in0, scalar1=s1, scalar2=None, op0=op0), reads, writes)
        else:
            self.op(eng, lambda e: e.tensor_scalar(out=out, in0=in0, scalar1=s1, scalar2=s2, op0=op0, op1=op1),
                    reads, writes)

    def stt(self, eng, out, in0, scalar, in1, op0, op1, reads, writes):
        self.op(eng, lambda e: e.scalar_tensor_tensor(out=out, in0=in0, scalar=scalar, in1=in1, op0=op0, op1=op1),
                reads, writes)

    def copy(self, eng, out, in_, reads, writes):
        if eng == "act":
            self.op("act", lambda e: e.activation(out=out, in_=in_, func=AF.Copy), reads, writes)
        else:
            self.op(eng, lambda e: e.tensor_copy(out=out, in_=in_), reads, writes)

    def memset(self, eng, ap, val, writes):
        self.op(eng, lambda e: e.memset(ap, val), (), writes)

    def dma(self, q, out, in_, reads, writes, slot):
        self.op(q, lambda e: e.dma_start(out=out, in_=in_), reads, writes, dma=slot)

    def flush(self):
        nc = self.nc
        ops = self.ops
        self.ops = []
        last_writer, readers = {}, {}
        pos_ctr = {}
        pos = []
        for o in ops:
            pos_ctr[o.key] = pos_ctr.get(o.key, 0) + 1
            pos.append(pos_ctr[o.key])
        seen = {e: {} for e in ENGS}
        deps_all = []
        for i, o in enumerate(ops):
            deps = set()
            for r in o.reads:
                j = last_writer.get(r)
                if j is not None:
                    deps.add(j)
            for w in o.writes:
                j = last_writer.get(w)
                if j is not None:
                    deps.add(j)
                deps.update(readers.get(w, ()))
            for r in o.reads:
                readers.setdefault(r, []).append(i)
            for w in o.writes:
                last_writer[w] = i
                readers[w] = []
            need = {}
            for j in deps:
                if j == i:
                    continue
                pj = ops[j]
                if pj.dma is None and pj.eng == o.eng and (o.eng == "pe" or not self.SAME_ENGINE_SYNC):
                    continue
                if need.get(pj.key, (0, 0))[0] < pos[j]:
                    need[pj.key] = (pos[j], j)
            final = []
            for key, (p, j) in need.items():
                if seen[o.eng].get(key, 0) >= p:
                    continue
                seen[o.eng][key] = p
                final.append(j)
                ops[j].signal = True
            deps_all.append(final)
        last_of = {}
        for i, o in enumerate(ops):
            if o.dma is None:
                last_of[o.eng] = i
        for i in last_of.values():
            ops[i].signal = True
        for o in ops:
            s = self.sem(o.key)
            if o.dma is not None:
                self.sigcount[o.key] += 16
                o.sigval = self.sigcount[o.key]
                o.signal = True
            elif o.signal:
                self.sigcount[o.key] += 1
                o.sigval = self.sigcount[o.key]
        per_eng = {e: [] for e in ENGS}
        for i, o in enumerate(ops):
            per_eng[o.eng].append(i)
            self.total[o.eng] += 1
        prev = getattr(self, "_barrier_prev", {})
        barrier = {k: v for k, v in self.sigcount.items() if v > prev.get(k, 0)}
        self._barrier_prev = dict(self.sigcount)
        sems = self.sems

        def make(e):
            def body(eng):
                for i in per_eng[e]:
                    o = ops[i]
                    for j in deps_all[i]:
                        eng.wait_ge(sems[ops[j].key], ops[j].sigval)
                    ins = o.fn(eng)
                    if o.signal:
                        ins.then_inc(sems[o.key], 16 if o.dma is not None else 1)
                for key, val in barrier.items():
                    if val > 0 and key != ("eng", e):
                        eng.wait_ge(sems[key], val)
            return body

        with nc.Block() as block:
            for e in ENGS:
                getattr(block, HND[e])(make(e))


def _bf(a):
    return np.ascontiguousarray(a.astype(np.float32)).astype(ml_dtypes.bfloat16)


def _consts(LS):
    s = np.arange(128)[:, None]
    t = np.arange(128)[None, :]
    same = (s // GCH) == (t // GCH)
    m_inc_f = (same & (s <= t)).astype(np.float32)
    m_str_f = (same & (s > t)).astype(np.float32)
    m_inc_b = (same & (s >= t)).astype(np.float32)
    m_str_b = (same & (s < t)).astype(np.float32)
    cm = np.stack([m_inc_f, m_str_f, m_inc_b, m_str_b], axis=1)
    am = np.stack([np.repeat(m_inc_f[:, None, :], 4, 1), np.repeat(m_inc_b[:, None, :], 4, 1)], axis=1)
    n = np.arange(128)
    ang = 2 * np.pi * np.outer(n, n) / 128.0
    cn = np.concatenate([np.cos(ang), np.sin(ang)], axis=1) / np.sqrt(128.0)

    def tab(L):
        l = np.arange(L, dtype=np.int64)
        kl = np.outer(l, l) % L
        a = 2 * np.pi * kl / float(L)
        return np.stack([np.cos(a), -np.sin(a)], axis=0) / np.sqrt(float(L))

    c256 = tab(LP)
    c256s = c256.reshape(2, 2, 128, LP).transpose(2, 1, 0, 3)
    cl = tab(LS)[:, :LS // 2, :LS // 2]
    cl = cl.reshape(2, LS // 256, 128, LS // 512, 256).transpose(3, 2, 1, 0, 4)
    lidx = np.arange(LS).reshape(LS // 128, 128).T
    cny = np.where(lidx % 2 == 0, 1.0, -1.0) / np.sqrt(float(LS))
    pp = np.arange(128)
    perm = ((pp[:, None] + pp[None, :]) == 128).astype(np.float32)
    e00 = np.zeros((128, 128), np.float32)
    e00[0, 0] = 1.0
    jperm = np.stack([perm, e00], axis=1)
    kk = np.arange(LS // 2)
    cpk = (np.where(kk % 2 == 0, 1.0, -1.0) / np.sqrt(float(LS)))[None, :]
    return (_bf(cm), am.astype(np.float32), _bf(cn), _bf(np.ascontiguousarray(c256s)), _bf(cl), _bf(cny),
            _bf(jperm), _bf(cpk), _bf(np.eye(128)))


class VecPack:
    def __init__(self):
        self.cols = []
        self.off = {}
        self.n = 0

    def add(self, name, arr):
        arr = np.asarray(arr, np.float32).reshape(128, -1)
        self.off[name] = self.n
        self.cols.append(arr)
        self.n += arr.shape[1]

    def build(self):
        return np.ascontiguousarray(np.concatenate(self.cols, axis=1))


def _fm(v):
    return np.asarray(v, np.float32).reshape(-1, 128).T


def _vec_layout(depth):
    off, n = {}, 0
    for l in range(depth):
        for nm, k in (("bada", 96), ("n1", 16), ("n2", 16), ("glan", 1), ("convw", 12)):
            off[(nm, l)] = n
            n += k
    off["nf"] = n
    n += 8
    return off, n


def _pack_vecs(depth, b_ada, norm1, norm2, gla_norm, conv_w, norm_f):
    off, n = _vec_layout(depth)
    out = np.zeros((128, n), np.float32)
    for l in range(depth):
        o = off[("bada", l)]
        out[:, o:o + 96] = np.repeat(_fm(b_ada[l]), 2, axis=1)
        o = off[("n1", l)]
        out[:, o:o + 16] = np.repeat(_fm(norm1[l]), 2, axis=1)
        o = off[("n2", l)]
        out[:, o:o + 16] = np.repeat(_fm(norm2[l]), 2, axis=1)
        o = off[("glan", l)]
        out[:, o:o + 1] = np.asarray(gla_norm[l], np.float32).reshape(128, 1)
        o = off[("convw", l)]
        for i in range(3):
            out[:, o + i * 4:o + i * 4 + 4] = _fm(conv_w[l, i])
    o = off["nf"]
    out[:, o:o + 8] = _fm(norm_f)
    return out


def build(depth, LS):
    T = 2 * LP + LS
    NT = T // TT
    nc = bass.Bass("TRN2", target_bir_lowering=False)
    voff, nvec = _vec_layout(depth)

    def din(name, shape, dt=F32):
        return nc.dram_tensor(name, list(shape), dt, kind="ExternalInput").ap()

    def dscr(name, shape, dt):
        return nc.dram_tensor(name, list(shape), dt).ap()

    x0 = din("x0", [D, T])
    state = din("state", [depth, 2, NH, DK, DV])
    cond = din("cond", [128, NCH, 2])
    vecs_d = din("vecs", [128, nvec])
    w_ada = din("w_ada", [depth, D, 6 * D])
    w_in = din("w_in", [depth, D, PIN])
    w_gk = [din("w_gk_f", [depth, 16, QKW]), din("w_gk_b", [depth, 16, QKW])]
    b_gk = [din("b_gk_f", [depth, QKW]), din("b_gk_b", [depth, QKW])]
    w_a_out = din("w_a_out", [depth, VW, D])
    w_b_out = din("w_b_out", [depth, SCW, D])
    w_c_out = din("w_c_out", [depth, FNW, D])
    w_o = din("w_o", [depth, D, D])
    w_up = din("w_up", [depth, D, 2 * DFF])
    w_down = din("w_down", [depth, DFF, D])
    cm_d = din("cmask", [128, 4, 128], BF16)
    am_d = din("amask", [128, 2, 4, 128])
    cn_d = din("cn", [128, 256], BF16)
    c256_d = din("c256", [128, 2, 2, LP], BF16)
    cl_d = din("cl", [LS // 512, 128, LS // 256, 2, 256], BF16)
    jp_d = din("jperm", [128, 2, 128], BF16)
    cpk_d = din("cpk", [1, LS // 2], BF16)
    id_d = din("ident", [128, 128], BF16)
    cny_d = din("cny", [128, LS // 128], BF16)
    y_out = nc.dram_tensor("y", [D, T], F32, kind="ExternalOutput").ap()
    ns_out = nc.dram_tensor("ns", [2, depth, 2, NH, DK, DV], F32, kind="ExternalOutput").ap()

    xs = dscr("xs", [D, T], F32)
    QT = dscr("QT", [NH, DK, T], BF16)
    KT = dscr("KT", [NH, DK, T], BF16)
    KTM = dscr("KTM", [T, QKW], BF16)
    VTM = dscr("VTM", [T, VW], BF16)
    GK = [dscr("GKF", [16, T], BF16), dscr("GKB", [16, T], BF16)]
    SOG = dscr("SOG", [VW, T], BF16)
    YBIN = dscr("YBIN", [SCW, T], BF16)
    AB = dscr("AB", [T, 1024], BF16)
    OF = dscr("OF", [VW, T], F32)
    OS = dscr("OS", [VW, T], F32)
    H1 = dscr("H1", [D, T], BF16)
    GA = dscr("GA", [VW, T], BF16)
    FR = dscr("FR", [FNW, T], BF16)

    P = Prog(nc)
    G = P.outer
    banks = [P.alloc(G, [128, 512], F32, psum=True) for _ in range(8)]
    bank_ctr = [0]

    nb_mod = [8]

    def nb():
        i = bank_ctr[0] % nb_mod[0]
        bank_ctr[0] += 1
        return banks[i], f"ps{i}"

    vecs = P.alloc(G, [128, nvec], F32)
    ones_bf = P.alloc(G, [128, 128], BF16)
    ident_bf = P.alloc(G, [128, 128], BF16)
    mod = P.alloc(G, [128, depth, 96], F32)
    a1 = P.alloc(G, [128, depth, 16], F32)
    a2 = P.alloc(G, [128, depth, 16], F32)
    condt = P.alloc(G, [128, NCH, 2], F32)
    scb = P.alloc(G, [128, NCH, 2], BF16)

    def vcol(name, l, i):
        o = voff[(name, l)] + i
        return vecs[:, o:o + 1]

    def modc(l, which, c, ci):
        o = (which * 8 + c) * 2 + ci
        return mod[:, l, o:o + 1]

    wv = w_ada.rearrange("l (kc p) n -> l p kc n", p=128)

    def ada_gen(l, wa, pb, pk):
        def issue(pc):
            wk = f"wa{pc % 3}"
            P.dma("pool", wa[pc % 3][:], wv[l, :, :, pc * 512:(pc + 1) * 512], (), [wk], wk)
        issue(0)
        for pc in range(12):
            if pc + 1 < 12:
                issue(pc + 1)
            wt = wa[pc % 3]
            wk = f"wa{pc % 3}"
            for jj in range(4):
                j = pc * 4 + jj
                for kc in range(NCH):
                    P.mm(pb[:, 2 * j:2 * j + 2], wt[:, kc, jj * 128:(jj + 1) * 128], scb[:, kc, :],
                         kc == 0, kc == NCH - 1, [wk, "scb"], [pk])
            yield
        o = voff[("bada", l)]
        P.tt("dve", mod[:, l, :], pb[:, 0:96], vecs[:, o:o + 96], ALU.add, [pk, "vecs"], [f"mod{l}"])
        o1 = voff[("n1", l)]
        P.stt("dve", a1[:, l, :], mod[:, l, 16:32], 1.0, vecs[:, o1:o1 + 16], ALU.add, ALU.mult,
              [f"mod{l}", "vecs"], [f"a1{l}"])
        o2 = voff[("n2", l)]
        P.stt("dve", a2[:, l, :], mod[:, l, 64:80], 1.0, vecs[:, o2:o2 + 16], ALU.add, ALU.mult,
              [f"mod{l}", "vecs"], [f"a2{l}"])
        yield

    with contextlib.ExitStack() as S:
        P.dma("sp", vecs[:], vecs_d, (), ["vecs"], "ld0")
        P.dma("sp", condt[:], cond, (), ["condt"], "ld5")
        P.dma("sp", ident_bf[:], id_d, (), ["ident"], "ld8")
        P.memset("pool", ones_bf[:], 1.0, ["ones"])
        P.act(scb[:], condt[:], AF.Silu, ["condt"], ["scb"])
        wa0 = [P.alloc(S, [128, NCH, 512], BF16) for _ in range(3)]
        pb0, pk0 = nb()
        for _ in ada_gen(0, wa0, pb0, pk0):
            pass
        P.flush()

    def ci_of(t):
        return 0 if t == 0 else 1

    def xsrc(l):
        return x0 if l == 0 else xs

    def xview(src, t):
        return src.rearrange("(c p) t -> p c t", p=128)[:, :, t * TT:(t + 1) * TT]

    def norm_parts(parts, xt, xkeys, hbuf, hkeys, rs, rsk, tmp, tmpk, avec, akey, shw, l, ci, sq_act=False):
        if 0 in parts:
            if sq_act:
                P.act(hbuf[:], xt[:], AF.Square, sorted(set(xkeys)), hkeys)
            else:
                for c in range(NCH):
                    P.tt("pool", hbuf[:, c, :], xt[:, c, :], xt[:, c, :], ALU.mult, [xkeys[c]], [hkeys[c]])
        if 1 in parts:
            pb, pk = nb()
            for c in range(NCH):
                P.mm(pb[:], ones_bf[:], hbuf[:, c, :], c == 0, c == NCH - 1, ["ones", hkeys[c]], [pk])
            P.act(rs[:], pb[:], AF.Ln, [pk], [rsk], bias=EPS, scale=1.0 / D)
            P.act(rs[:], rs[:], AF.Exp, [rsk], [rsk], scale=-0.5)
        for part in (2, 3):
            if part in parts:
                for c in range(4 * (part - 2), 4 * (part - 1)):
                    P.tt("dve", tmp[:, c % 2, :], xt[:, c, :], rs[:], ALU.mult, [xkeys[c], rsk], [tmpk + str(c % 2)])
                    P.ts("dve", hbuf[:, c, :], tmp[:, c % 2, :], avec[:, l, c * 2 + ci:c * 2 + ci + 1], ALU.mult,
                         [tmpk + str(c % 2), akey, f"mod{l}"], [hkeys[c]], s2=modc(l, shw, c, ci), op1=ALU.add)

    wslot = [0]

    def next_w():
        wslot[0] += 1
        return f"w{wslot[0] - 1}"

    def load_w(wt, src2d, kch, ncols, name, piece=512):
        v = src2d.rearrange("(kc p) n -> p kc n", p=128)
        keys = []
        for i, c0 in enumerate(range(0, ncols, piece)):
            c1 = min(ncols, c0 + piece)
            k = f"{name}{i}"
            P.dma("pool", wt[:, :, c0:c1], v[:, :, c0:c1], (), [k], next_w())
            keys.append(k)
        return keys

    for l in range(depth):
        last = l == depth - 1
        with contextlib.ExitStack() as S:
            wslot[0] = 0
            NA = OFF_MG
            win = P.alloc(S, [128, NCH, NA], BF16)
            cn = P.alloc(S, [128, 256], BF16)
            P.dma("sp", cn[:], cn_d, (), ["cn"], "ld3")
            wkeys = load_w(win, w_in[l][:, 0:NA], NCH, NA, "win")

            def wk_of(c0, c1):
                return sorted({wkeys[c // 512] for c in (c0, c1 - 1)})

            xt = P.alloc(S, [128, NCH, TT], F32)
            hb = [P.alloc(S, [128, NCH, TT], BF16) for _ in range(2)]
            rs = P.alloc(S, [128, TT], F32)
            tmp = P.alloc(S, [128, 2, TT], F32)
            st_q = P.alloc(S, [128, 2, TT], BF16)
            st_k = P.alloc(S, [128, 2, TT], BF16)
            st_ktm = P.alloc(S, [128, 4, QKW], BF16)
            st_vtm = P.alloc(S, [128, 4, VW], BF16)
            st_gk = [P.alloc(S, [16, TT], BF16) for _ in range(2)]
            st_sog = P.alloc(S, [128, 4, TT], BF16)
            st_yb = P.alloc(S, [128, 4, TT], BF16)
            fxT = P.alloc(S, [128, 4, TT], BF16)
            st_ab = P.alloc(S, [128, 4, 1024], BF16)
            sbf = P.alloc(S, [128, TT], F32)
            scf = P.alloc(S, [128, TT], F32)
            uu = P.alloc(S, [128, TT], F32)
            cv = P.alloc(S, [128, TT], F32)
            src = xsrc(l)
            P.dma("sp", xt[:], xview(src, 0), (), ["xt"], "xt")
            agen = None
            if l + 1 < depth:
                waA = [P.alloc(S, [128, NCH, 512], BF16) for _ in range(3)]
                nb_mod[0] = 7
                agen = ada_gen(l + 1, waA, banks[7], "ps7")

            def ada_step():
                if agen is not None:
                    next(agen, None)

            for t in range(NT):
                ci = ci_of(t)
                h = hb[t % 2]
                hk = f"h{t % 2}"
                hkeys = [f"{hk}_{c}" for c in range(NCH)]
                if t == 0:
                    norm_parts((0, 1, 2, 3), xt, ["xt"] * NCH, h, hkeys, rs, "rs", tmp, "tmp", a1, f"a1{l}", 0, l, ci,
                               sq_act=True)
                    P.dma("sp", xview(H1, 0), h[:], hkeys, (), "hst0")
                if t + 1 < NT:
                    P.dma("sp", xt[:], xview(src, t + 1), (), ["xt"], "xt")
                cols = slice(t * TT, (t + 1) * TT)

                def fm_group(c0, m, evac):
                    pb, pk = nb()
                    for kc in range(NCH):
                        P.mm(pb[0:m, :], win[:, kc, c0:c0 + m], h[:, kc, :], kc == 0, kc == NCH - 1,
                             wk_of(c0, c0 + m) + [hkeys[kc]], [pk])
                    evac(pb, pk)

                for pr2 in range(2):
                    fm_group(OFF_Q + pr2 * 128, 128,
                             lambda pb, pk, pr2=pr2: P.act(st_q[:, pr2, :], pb[:], AF.Copy, [pk], [f"st_q{pr2}"],
                                                           scale=DK ** -0.5))
                for pr2 in range(2):
                    fm_group(OFF_K + pr2 * 128, 128,
                             lambda pb, pk, pr2=pr2: P.copy("dve", st_k[:, pr2, :], pb[:], [pk], [f"st_k{pr2}"]))
                P.dma("sp", QT[:, :, cols].rearrange("h d t -> (h d) t").rearrange("(pr p) t -> p pr t", p=128),
                      st_q[:], ["st_q0", "st_q1"], ["QT"], "st_q")
                P.dma("sp", KT[:, :, cols].rearrange("h d t -> (h d) t").rearrange("(pr p) t -> p pr t", p=128),
                      st_k[:], ["st_k0", "st_k1"], ["KT"], "st_k")
                ada_step()
                if t + 1 < NT:
                    norm_parts((0,), xt, ["xt"] * NCH, hb[(t + 1) % 2], [f"h{(t + 1) % 2}_{c}" for c in range(NCH)],
                               rs, "rs", tmp, "tmp", a1, f"a1{l}", 0, l, ci_of(t + 1))
                for dr in range(2):
                    fm_group(OFF_GKF + 16 * dr, 16,
                             lambda pb, pk, dr=dr: P.copy("dve", st_gk[dr][:], pb[0:16, :], [pk], [f"st_gk{dr}"]))
                    P.dma("sp", GK[dr][:, cols], st_gk[dr][:], [f"st_gk{dr}"], [f"GK{dr}"], f"st_gk{dr}")
                for c in range(4):
                    fm_group(OFF_OG + c * 128, 128,
                             lambda pb, pk, c=c: P.act(st_sog[:, c, :], pb[:], AF.Silu, [pk], [f"st_sog{c}"]))
                P.dma("sp", SOG.rearrange("(c p) t -> p c t", p=128)[:, :, cols], st_sog[:], [f"st_sog{i}" for i in range(4)], ["SOG"],
                      "st_sog")
                for sb_ in range(4):
                    tk = slice(sb_ * 128, (sb_ + 1) * 128)
                    pb, pk = nb()
                    for pr2 in range(2):
                        P.mm(pb[:, pr2 * 128:(pr2 + 1) * 128], st_k[:, pr2, tk], ident_bf[:], True, True,
                             [f"st_k{pr2}", "ident"], [pk])
                    P.copy("act", st_ktm[:, sb_, :], pb[:, 0:QKW], [pk], ["st_ktm"])
                    pb, pk = nb()
                    for kc in range(NCH):
                        P.mm(pb[:], h[:, kc, tk], win[:, kc, OFF_V:OFF_V + VW], kc == 0, kc == NCH - 1,
                             wk_of(OFF_V, OFF_V + VW) + [hkeys[kc]], [pk])
                    P.copy("dve", st_vtm[:, sb_, :], pb[:], [pk], ["st_vtm"])
                if t + 1 < NT:
                    norm_parts((1, 2, 3), xt, ["xt"] * NCH, hb[(t + 1) % 2], [f"h{(t + 1) % 2}_{c}" for c in range(NCH)],
                               rs, "rs", tmp, "tmp", a1, f"a1{l}", 0, l, ci_of(t + 1))
                    P.dma("sp", xview(H1, t + 1), hb[(t + 1) % 2][:], [f"h{(t + 1) % 2}_{c}" for c in range(NCH)], (),
                          f"hst{(t + 1) % 2}")
                P.dma("sp", KTM[cols, :].rearrange("(s p) c -> p s c", p=128), st_ktm[:], ["st_ktm"], ["KTM"], "st_ktm")
                P.dma("sp", VTM[cols, :].rearrange("(s p) c -> p s c", p=128), st_vtm[:], ["st_vtm"], ["VTM"], "st_vtm")
                R = TT // GRID_W if t > 0 else 2
                W = TT // R
                for c in range(4):
                    fm_group(OFF_SB + c * 128, 128, lambda pb, pk: P.copy("act", sbf[:], pb[:], [pk], ["sbf"]))
                    fm_group(OFF_SC + c * 128, 128, lambda pb, pk: P.copy("act", scf[:], pb[:], [pk], ["scf"]))
                    fm_group(OFF_SX + c * 128, 128,
                             lambda pb, pk: P.tt("dve", uu[:], pb[:], scf[:], ALU.mult, [pk, "scf"], ["uu"]))
                    u3 = uu[:].rearrange("p (r w) -> p r w", w=W)
                    c3 = cv[:].rearrange("p (r w) -> p r w", w=W)
                    P.act(cv[:], uu[:], AF.Copy, ["uu", "vecs"], ["cv"], scale=vcol("convw", l, 4 + c))
                    P.stt("dve", c3[:, :, 1:W], u3[:, :, 0:W - 1], vcol("convw", l, 0 + c), c3[:, :, 1:W],
                          ALU.mult, ALU.add, ["uu", "cv", "vecs"], ["cv"])
                    P.stt("dve", c3[:, :, 0:W - 1], u3[:, :, 1:W], vcol("convw", l, 8 + c), c3[:, :, 0:W - 1],
                          ALU.mult, ALU.add, ["uu", "cv", "vecs"], ["cv"])
                    P.tt("pool", st_yb[:, c, :], sbf[:], cv[:], ALU.mult, ["sbf", "cv"], ["st_yb"])
                P.dma("sp", YBIN.rearrange("(c p) t -> p c t", p=128)[:, :, cols], st_yb[:], ["st_yb"], ["YBIN"], "st_yb")
                ada_step()
                for g in range(4):
                    fm_group(OFF_FX + g * 128, 128,
                             lambda pb, pk, g=g: P.copy("act", fxT[:, g, :], pb[:], [pk], [f"fxT{g}"]))
                for sb_ in range(4):
                    tk = slice(sb_ * 128, (sb_ + 1) * 128)
                    for half in range(2):
                        pb, pk = nb()
                        for gg in range(2):
                            g = half * 2 + gg
                            P.mm(pb[:, gg * 256:(gg + 1) * 256], fxT[:, g, tk], cn[:], True, True,
                                 [f"fxT{g}", "cn"], [pk])
                        P.copy("dve" if half else "act", st_ab[:, sb_, half * 512:(half + 1) * 512], pb[:], [pk],
                               ["st_ab"])
                P.dma("sp", AB[cols, :].rearrange("(s p) c -> p s c", p=128), st_ab[:], ["st_ab"], ["AB"], "st_ab")
            if agen is not None:
                for _ in agen:
                    pass
            nb_mod[0] = 8
            P.flush()

        for dr in range(2):
            with contextlib.ExitStack() as S:
                wslot[0] = 0
                cm = P.alloc(S, [128, 4, 128], BF16)
                am = P.alloc(S, [128, 2, 4, 128], F32)
                P.dma("sp", cm[:], cm_d, (), ["cm"], "ld1")
                P.dma("sp", am[:], am_d, (), ["am"], "ld2")
                wgk = P.alloc(S, [16, QKW], BF16)
                bgk = P.alloc(S, [1, QKW], BF16)
                ones_row = P.alloc(S, [1, 128], BF16)
                P.dma("pool", wgk[:], w_gk[dr][l], (), ["wgk"], next_w())
                P.dma("pool", bgk[:], b_gk[dr][l:l + 1, :], (), ["bgk"], next_w())
                P.memset("pool", ones_row[:], 1.0, ["ones_row"])
                qTb = [P.alloc(S, [64, NH, TT], BF16) for _ in range(3)]
                kTb = [P.alloc(S, [64, NH, TT], BF16) for _ in range(3)]
                ktmb = [P.alloc(S, [128, 4, QKW], BF16) for _ in range(3)]
                vtmb = [P.alloc(S, [128, 4, VW], BF16) for _ in range(3)]
                gkb = [P.alloc(S, [16, TT], BF16) for _ in range(3)]
                qinb = [P.alloc(S, [64, NH, TT], BF16) for _ in range(2)]
                kinb = [P.alloc(S, [64, NH, TT], BF16) for _ in range(2)]
                koutb = [P.alloc(S, [128, 4, QKW], BF16) for _ in range(2)]
                gcolb = [P.alloc(S, [64, NH, 8], F32) for _ in range(2)]
                e1 = P.alloc(S, [128, QKW], F32)
                sp_ = P.alloc(S, [128, 4, QKW], BF16)
                ebT = P.alloc(S, [64, NH, TT], F32)
                eiT = P.alloc(S, [64, NH, TT], F32)
                ec = P.alloc(S, [128, 4, QKW], F32)
                attm = [P.alloc(S, [128, NH, 128], BF16) for _ in range(2)]
                Sf = P.alloc(S, [64, NH, DV], F32)
                Sb = P.alloc(S, [64, NH, DV], BF16)
                osbb = [P.alloc(S, [128, NH, TT], F32) for _ in range(2)]
                if dr == 1:
                    oflb = [P.alloc(S, [128, NH, TT], F32) for _ in range(3)]
                minc = cm[:, 0 + 2 * dr, :]
                mstr = cm[:, 1 + 2 * dr, :]
                if dr == 0:
                    visits = [(0, [("p", 0, [0, 1]), ("p", 1, [2, 3])])]
                    visits += [(t, [("s", 0, [0, 1, 2, 3])]) for t in range(1, NT)]
                else:
                    visits = [(0, [("p", 0, [1, 0]), ("p", 1, [3, 2])])]
                    visits += [(t, [("s", 0, [3, 2, 1, 0])]) for t in range(NT - 1, 0, -1)]
                NV = len(visits)
                first_s = visits[1][0]
                last_s = visits[-1][0]
                obank = [banks[i] for i in range(4)]
                obk = [f"ps{i}" for i in range(4)]
                cr, pr_ = [0], [0]

                def cb():
                    i = 4 + cr[0] % 2
                    cr[0] += 1
                    return banks[i], f"ps{i}"

                def p1b():
                    i = 6 + pr_[0] % 2
                    pr_[0] += 1
                    return banks[i], f"ps{i}"

                def g_loads(i):
                    t = visits[i][0]
                    b = i % 3
                    cols = slice(t * TT, (t + 1) * TT)
                    P.dma("sp", gkb[b][:], GK[dr][:, cols], [f"GK{dr}"], [f"gk{b}"], f"gk{b}")
                    P.dma("sp", qTb[b][:], QT[:, :, cols].rearrange("h d t -> d h t"), ["QT"], [f"qT{b}"], f"qT{b}")
                    P.dma("sp", kTb[b][:], KT[:, :, cols].rearrange("h d t -> d h t"), ["KT"], [f"kT{b}"], f"kT{b}")
                    P.dma("sp", ktmb[b][:], KTM[cols, :].rearrange("(s p) c -> p s c", p=128), ["KTM"], [f"ktm{b}"], f"ktm{b}")
                    P.dma("sp", vtmb[b][:], VTM[cols, :].rearrange("(s p) c -> p s c", p=128), ["VTM"], [f"vtm{b}"], f"vtm{b}")
                    if dr == 1:
                        P.dma("sp", oflb[b][:], OF.rearrange("(h p) t -> p h t", p=128)[:, :, cols], ["OF"], [f"ofl{b}"], f"ofl{b}")

                def p1a(i, s_):
                    b = i % 2
                    b3 = i % 3
                    tk = slice(s_ * 128, (s_ + 1) * 128)
                    pb, pk = p1b()
                    P.mm(pb[:, 0:QKW], gkb[b3][:, tk], wgk[:], True, False, [f"gk{b3}", "wgk"], [pk])
                    P.mm(pb[:, 0:QKW], ones_row[:], bgk[:], False, True, ["ones_row", "bgk"], [pk])
                    P.act(e1[:], pb[:, 0:QKW], AF.Exp, [pk], ["e1"], scale=-1.0)
                    P.act(sp_[:, s_, :], e1[:], AF.Ln, ["e1"], [f"sp{s_}"], bias=1.0)

                def p1c(i, s_):
                    b = i % 2
                    b3 = i % 3
                    tk = slice(s_ * 128, (s_ + 1) * 128)
                    pb, pk = p1b()
                    for hh in range(NH):
                        P.mm(pb[0:64, hh * 128:(hh + 1) * 128], sp_[:, s_, hh * 64:(hh + 1) * 64], minc, True, True,
                             [f"sp{s_}", "cm"], [pk])
                    pv = pb[0:64, :].rearrange("p (h t) -> p h t", h=NH)
                    P.act(ebT[:, :, tk], pv, AF.Exp, [pk], [f"ebT{s_}"], scale=-1.0 / 16)
                    P.act(eiT[:, :, tk], pv, AF.Exp, [pk], [f"eiT{s_}"], scale=1.0 / 16)
                    pb, pk = p1b()
                    P.mm(pb[:, 0:QKW], mstr, sp_[:, s_, :], True, True, [f"sp{s_}", "cm"], [pk])
                    P.act(ec[:, s_, :], pb[:, 0:QKW], AF.Exp, [pk], [f"ec{s_}"], scale=-1.0 / 16)
                    P.tt("pool", qinb[b][:, :, tk], qTb[b3][:, :, tk], ebT[:, :, tk], ALU.mult, [f"qT{b3}", f"ebT{s_}"],
                         [f"qin{b}_{s_}"])
                    P.tt("pool", kinb[b][:, :, tk], kTb[b3][:, :, tk], eiT[:, :, tk], ALU.mult, [f"kT{b3}", f"eiT{s_}"],
                         [f"kin{b}_{s_}"])
                    P.tt("pool", koutb[b][:, s_, :], ktmb[b3][:, s_, :], ec[:, s_, :], ALU.mult, [f"ktm{b3}", f"ec{s_}"],
                         [f"kout{b}_{s_}"])
                    tl = s_ * 128 + (127 if dr == 0 else 0)
                    P.copy("pool", gcolb[b][:, :, s_:s_ + 1], ebT[:, :, tl:tl + 1], [f"ebT{s_}"], [f"gcol{b}_{s_}"])

                def p2_att(i, s_):
                    b = i % 2
                    tk = slice(s_ * 128, (s_ + 1) * 128)
                    pb, pk = p1b()
                    for hh in range(NH):
                        P.mm(pb[:, hh * 128:(hh + 1) * 128], kinb[b][:, hh, tk], qinb[b][:, hh, tk], True, True,
                             [f"kin{b}_{s_}", f"qin{b}_{s_}"], [pk])
                    P.tt("dve", attm[s_ % 2][:], pb[:].rearrange("p (h t) -> p h t", h=NH), am[:, dr, :, :], ALU.mult,
                         [pk, "am"], [f"attm{s_ % 2}"])

                def p2_o(i, s_):
                    b = i % 2
                    b3 = i % 3
                    tk = slice(s_ * 128, (s_ + 1) * 128)
                    for hh in range(NH):
                        P.mm(obank[hh][:, tk], vtmb[b3][:, s_, hh * DV:(hh + 1) * DV], attm[s_ % 2][:, hh, :], s_ == 0,
                             False, [f"vtm{b3}", f"attm{s_ % 2}"], [obk[hh]])

                def chain_step(i, ch):
                    b = i % 2
                    b3 = i % 3
                    ck = slice(ch * 128, (ch + 1) * 128)
                    s_ = ch
                    pr = slice(0, 128)
                    ub = []
                    for p2_ in range(2):
                        heads = (2 * p2_, 2 * p2_ + 1)
                        for hh in heads:
                            P.mm(obank[hh][:, ck], Sb[:, hh, :], qinb[b][:, hh, ck], False, True,
                                 [f"Sb{p2_}", f"qin{b}_{s_}"], [obk[hh]])
                        pb, pk = cb()
                        for hh in heads:
                            P.mm(pb[0:64, (hh % 2) * DV:(hh % 2 + 1) * DV], koutb[b][pr, s_, hh * 64:(hh + 1) * 64],
                                 vtmb[b3][pr, s_, hh * DV:(hh + 1) * DV], True, True, [f"kout{b}_{s_}", f"vtm{b3}"], [pk])
                        ub.append((pb, pk))
                    for p2_ in range(2):
                        heads = (2 * p2_, 2 * p2_ + 1)
                        pb, pk = ub[p2_]
                        for hh in heads:
                            P.stt("dve", Sf[:, hh, :], Sf[:, hh, :], gcolb[b][:, hh, ch:ch + 1],
                                  pb[0:64, (hh % 2) * DV:(hh % 2 + 1) * DV], ALU.mult, ALU.add,
                                  [f"Sf{hh}", f"gcol{b}_{ch}", pk], [f"Sf{hh}"])
                        P.copy("dve", Sb[:, 2 * p2_:2 * p2_ + 2, :], Sf[:, 2 * p2_:2 * p2_ + 2, :],
                               [f"Sf{hh}" for hh in heads], [f"Sb{p2_}"])

                g_loads(0)
                if NV > 1:
                    g_loads(1)
                for s_ in range(4):
                    p1a(0, s_)
                for s_ in range(4):
                    p1c(0, s_)
                for i, (t, parts) in enumerate(visits):
                    b = i % 2
                    cols = slice(t * TT, (t + 1) * TT)
                    if i + 2 < NV:
                        g_loads(i + 2)
                    p2_att(i, 0)
                    p2_att(i, 1)
                    p2_o(i, 0)
                    p2_att(i, 2)
                    p2_o(i, 1)
                    p2_att(i, 3)
                    p2_o(i, 2)
                    p2_o(i, 3)
                    k = 0
                    for (kind, sidx, chunks) in parts:
                        first_visit = (kind == "p") or (t == first_s)
                        last_visit = (kind == "p") or (t == last_s)
                        if first_visit:
                            if kind == "p":
                                P.memset("dve", Sf[:], 0.0, [f"Sf{hh}" for hh in range(NH)])
                                P.memset("dve", Sb[:], 0.0, ["Sb0", "Sb1"])
                            else:
                                P.dma("sp", Sf[:], state[l, dr].rearrange("h d e -> d h e"), (),
                                      [f"Sf{hh}" for hh in range(NH)], "Sf")
                                P.copy("dve", Sb[:], Sf[:], [f"Sf{hh}" for hh in range(NH)], ["Sb0", "Sb1"])
                        for ch in chunks:
                            chain_step(i, ch)
                            if i + 1 < NV:
                                p1a(i + 1, k)
                                if k >= 1:
                                    p1c(i + 1, k - 1)
                            k += 1
                        if last_visit and kind == "p":
                            P.dma("sp", ns_out[sidx, l, dr].rearrange("h d e -> d h e"), Sf[:],
                                  [f"Sf{hh}" for hh in range(NH)], ["ns"], "Sfo")
                    osb = osbb[b]
                    if i + 1 < NV:
                        p1c(i + 1, 3)
                    if dr == 0:
                        for hh in range(NH):
                            P.copy("act" if hh % 2 else "dve", osb[:, hh, :], obank[hh][:], [obk[hh]], [f"osb{b}_{hh}"])
                        P.dma("sp", OF.rearrange("(h p) t -> p h t", p=128)[:, :, cols], osb[:],
                              [f"osb{b}_{hh}" for hh in range(NH)], ["OF"], f"osb{b}")
                    else:
                        for hh in range(NH):
                            P.tt("dve", osb[:, hh, :], obank[hh][:], oflb[i % 3][:, hh, :], ALU.add, [obk[hh], f"ofl{i % 3}"],
                                 [f"osb{b}_{hh}"])
                        P.dma("sp", OS.rearrange("(h p) t -> p h t", p=128)[:, :, cols], osb[:],
                              [f"osb{b}_{hh}" for hh in range(NH)], ["OS"], f"osb{b}")
                P.flush()

        with contextlib.ExitStack() as S:
            KTW = 256
            NLC = LS // 128
            c256 = P.alloc(S, [128, 2, 2, LP], BF16)
            P.dma("sp", c256[:], c256_d, (), ["c256"], "ld4")
            abp = P.alloc(S, [128, 2, 1024], BF16)
            frp = P.alloc(S, [128, 4, LP], BF16)
            abs_ = P.alloc(S, [128, NLC, 1024], BF16)
            AST = min(8, NLC)
            for i in range(0, NLC, AST):
                P.dma("sp", abs_[:, i:i + AST, :],
                      AB[2 * LP + i * 128:2 * LP + (i + AST) * 128, :].rearrange("(lc p) c -> p lc c", p=128),
                      ["AB"], [f"abs{i}"], f"abs{i}")
            NL2 = NLC // 2
            tabs = [P.alloc(S, [128, NL2, 2, KTW], BF16) for _ in range(2)]
            jp = P.alloc(S, [128, 2, 128], BF16)
            cpk = P.alloc(S, [1, LS // 2], BF16)
            P.dma("sp", jp[:], jp_d, (), ["jp"], "ld6")
            P.dma("sp", cpk[:], cpk_d, (), ["cpk"], "ld7")
            osl = [P.alloc(S, [128, NH, TT], F32) for _ in range(2)]
            sgl = [P.alloc(S, [128, NH, TT], BF16) for _ in range(2)]
            gal = [P.alloc(S, [128, NH, TT], BF16) for _ in range(2)]
            sq4 = P.alloc(S, [128, 2, TT], BF16)
            rs4 = P.alloc(S, [128, 2, TT], F32)

            def tail_loads(t):
                b = t % 2
                cols = slice(t * TT, (t + 1) * TT)
                P.dma("sp", osl[b][:], OS.rearrange("(h p) t -> p h t", p=128)[:, :, cols], ["OS"],
                      [f"osl{b}_{h_}" for h_ in range(NH)], f"osl{b}")
                P.dma("sp", sgl[b][:], SOG.rearrange("(h p) t -> p h t", p=128)[:, :, cols], ["SOG"], [f"sgl{b}"], f"sgl{b}")

            def tail_a(u):
                t, hh = divmod(u, NH)
                b = t % 2
                if hh == 2 and t + 1 < NT:
                    tail_loads(t + 1)
                P.act(sq4[:, u % 2, :], osl[b][:, hh, :], AF.Square, [f"osl{b}_{hh}"], [f"sq4{u % 2}"])

            def tail_b(u):
                t, hh = divmod(u, NH)
                b = t % 2
                pb, pk = nb()
                P.mm(pb[:], ones_bf[:], sq4[:, u % 2, :], True, True, ["ones", f"sq4{u % 2}"], [pk])
                P.act(rs4[:, u % 2, :], pb[:], AF.Ln, [pk], [f"rs4{u % 2}"], bias=EPS, scale=1.0 / DV)
                P.act(rs4[:, u % 2, :], rs4[:, u % 2, :], AF.Exp, [f"rs4{u % 2}"], [f"rs4{u % 2}"], scale=-0.5)
                P.tt("pool", osl[b][:, hh, :], osl[b][:, hh, :], rs4[:, u % 2, :], ALU.mult,
                     [f"osl{b}_{hh}", f"rs4{u % 2}"], [f"osl{b}_{hh}"])

            def tail_c(u):
                t, hh = divmod(u, NH)
                b = t % 2
                P.stt("dve", gal[b][:, hh, :], osl[b][:, hh, :], vcol("glan", l, 0), sgl[b][:, hh, :],
                      ALU.mult, ALU.mult, [f"osl{b}_{hh}", f"sgl{b}", "vecs"], [f"gal{b}_{hh}"])
                if hh == NH - 1:
                    cols = slice(t * TT, (t + 1) * TT)
                    P.dma("sp", GA.rearrange("(h p) t -> p h t", p=128)[:, :, cols], gal[b][:],
                          [f"gal{b}_{h_}" for h_ in range(NH)], ["GA"], f"gal{b}")

            NU = NT * NH
            tstate = [0]

            def tail_step():
                n = tstate[0]
                if n - 2 >= 0 and n - 2 < NU:
                    tail_c(n - 2)
                if n - 1 >= 0 and n - 1 < NU:
                    tail_b(n - 1)
                if n < NU:
                    tail_a(n)
                tstate[0] += 1

            tail_loads(0)
            frl = [P.alloc(S, [128, 4, KTW + 1], BF16) for _ in range(2)]
            frh = [P.alloc(S, [128, 4, KTW], BF16) for _ in range(2)]
            qsb = [P.alloc(S, [128, KTW], F32) for _ in range(2)]
            cny = P.alloc(S, [128, NLC], BF16)
            P.dma("sp", cny[:], cny_d, (), ["cny"], "ld5")
            P.dma("sp", tabs[0][:], cl_d[0], (), ["tab0"], "tab0")
            for sq_i in range(2):
                c0 = sq_i * LP
                P.dma("sp", abp[:], AB[c0:c0 + LP, :].rearrange("(lc p) c -> p lc c", p=128), ["AB"], ["abp"], "abp")
                for g in range(4):
                    pb, pk = nb()
                    n = 0
                    for lc in range(2):
                        for cs in range(2):
                            P.mm(pb[:, 0:LP], abp[:, lc, g * 256 + cs * 128:g * 256 + cs * 128 + 128],
                                 c256[:, lc, cs, :], n == 0, n == 3, ["abp", "c256"], [pk])
                            n += 1
                    P.copy("act" if g % 2 else "dve", frp[:, g, :], pb[:, 0:LP], [pk], ["frp"])
                    tail_step()
                P.dma("sp", FR.rearrange("(g p) t -> p g t", p=128)[:, :, c0:c0 + LP], frp[:], ["frp"], ["FR"], "frp")
            def akey(lc):
                return f"abs{(lc // AST) * AST}"

            for lc in range(NL2):
                for half in range(2):
                    cs_ = slice(half * 512, (half + 1) * 512)
                    pb, pk = nb()
                    P.mm(pb[:], jp[:, 0, :], abs_[:, NLC - 1 - lc, cs_], True, lc == 0, ["jp", akey(NLC - 1 - lc)], [pk])
                    if lc > 0:
                        P.mm(pb[:], jp[:, 1, :], abs_[:, NLC - lc, cs_], False, True, ["jp", akey(NLC - lc)], [pk])
                    v = pb[:].rearrange("p (g cs m) -> p g cs m", g=2, cs=2)
                    dst = abs_[:, lc, cs_].rearrange("p (g cs m) -> p g cs m", g=2, cs=2)
                    P.tt("dve", dst[:, :, 0, :], dst[:, :, 0, :], v[:, :, 0, :], ALU.add, [pk, akey(lc)], [akey(lc)])
                    P.tt("dve", dst[:, :, 1, :], dst[:, :, 1, :], v[:, :, 1, :], ALU.subtract, [pk, akey(lc)], [akey(lc)])
                if lc % 2 == 1:
                    tail_step()
            NKT2 = LS // (2 * KTW)
            for kt in range(NKT2):
                tb = tabs[kt % 2]
                tbk = f"tab{kt % 2}"
                if kt + 1 < NKT2:
                    P.dma("sp", tabs[(kt + 1) % 2][:], cl_d[kt + 1], (), [f"tab{(kt + 1) % 2}"], f"tab{(kt + 1) % 2}")
                lo, hi = frl[kt % 2], frh[kt % 2]
                lok, hik = f"frl{kt % 2}", f"frh{kt % 2}"
                for g in range(4):
                    pP, pPk = nb()
                    for lc in range(NL2):
                        P.mm(pP[:, 0:KTW], abs_[:, lc, g * 256:g * 256 + 128], tb[:, lc, 0, :], lc == 0, False,
                             [akey(lc), tbk], [pPk])
                    P.mm(pP[:, 0:KTW], abs_[0:1, NL2, g * 256:g * 256 + 128], cpk[0:1, kt * KTW:(kt + 1) * KTW], False, True,
                         [akey(NL2), "cpk"], [pPk])
                    pQ, pQk = nb()
                    for lc in range(NL2):
                        P.mm(pQ[:, 0:KTW], abs_[:, lc, g * 256 + 128:g * 256 + 256], tb[:, lc, 1, :], lc == 0,
                             lc == NL2 - 1, [akey(lc), tbk], [pQk])
                    q = qsb[g % 2]
                    qk = f"qsb{g % 2}"
                    P.copy("act", q[:], pQ[:, 0:KTW], [pQk], [qk])
                    P.tt("dve", lo[:, g, 0:KTW], pP[:, 0:KTW], q[:], ALU.add, [pPk, qk], [lok + f"_{g}"])
                    if kt == 0:
                        P.tt("dve", hi[:, g, KTW - 2::-1], pP[:, 1:KTW], q[:, 1:KTW], ALU.subtract, [pPk, qk], [hik + f"_{g}"])
                    else:
                        P.tt("dve", hi[:, g, ::-1], pP[:, 0:KTW], q[:], ALU.subtract, [pPk, qk], [hik + f"_{g}"])
                    tail_step()
                k0 = kt * KTW
                nlo = KTW
                if kt == NKT2 - 1:
                    pN, pNk = nb()
                    for g in range(4):
                        for lc in range(NL2):
                            P.mm(pN[:, g:g + 1], abs_[:, lc, g * 256:g * 256 + 128], cny[:, lc:lc + 1], lc == 0,
                                 False, [akey(lc), "cny"], [pNk])
                        P.mm(pN[:, g:g + 1], abs_[0:1, NL2, g * 256:g * 256 + 128], cpk[0:1, 0:1], False, True,
                             [akey(NL2), "cpk"], [pNk])
                    P.copy("act", lo[:, :, KTW:KTW + 1], pN[:, 0:4].rearrange("p (g o) -> p g o", o=1), [pNk], [lok + "_n"])
                    nlo = KTW + 1
                FRv = FR.rearrange("(g p) t -> p g t", p=128)
                P.dma("sp", FRv[:, :, 2 * LP + k0:2 * LP + k0 + nlo], lo[:, :, 0:nlo],
                      [lok + f"_{g}" for g in range(4)] + ([lok + "_n"] if nlo > KTW else []), ["FR"], lok)
                if kt == 0:
                    P.dma("sp", FRv[:, :, 2 * LP + LS - (KTW - 1):2 * LP + LS], hi[:, :, 0:KTW - 1],
                          [hik + f"_{g}" for g in range(4)], ["FR"], hik)
                else:
                    P.dma("sp", FRv[:, :, 2 * LP + LS - k0 - (KTW - 1):2 * LP + LS - k0 + 1], hi[:, :, 0:KTW],
                          [hik + f"_{g}" for g in range(4)], ["FR"], hik)
            while tstate[0] < NU + 2:
                tail_step()
            P.flush()

        with contextlib.ExitStack() as S:
            wslot[0] = 0
            wmg = P.alloc(S, [128, NCH, 3 * D], BF16)
            wao = P.alloc(S, [128, 4, D], BF16)
            wbo = P.alloc(S, [128, 4, D], BF16)
            wco = P.alloc(S, [128, 4, D], BF16)
            woo = P.alloc(S, [128, NCH, D], BF16)
            P.dma("pool", wao[:], w_a_out[l].rearrange("(kc p) n -> p kc n", p=128), (), ["wao"], next_w())
            P.dma("pool", wbo[:], w_b_out[l].rearrange("(kc p) n -> p kc n", p=128), (), ["wbo"], next_w())
            P.dma("pool", wco[:], w_c_out[l].rearrange("(kc p) n -> p kc n", p=128), (), ["wco"], next_w())
            mgk = load_w(wmg, w_in[l][:, OFF_MG:PIN], NCH, 3 * D, "wmg")
            P.dma("pool", woo[:], w_o[l].rearrange("(kc p) n -> p kc n", p=128), (), ["woo"], next_w())
            xtb = [P.alloc(S, [128, NCH, TT], F32) for _ in range(2)]
            hb2 = [P.alloc(S, [128, NCH, TT], BF16) for _ in range(2)]
            gatb = [P.alloc(S, [128, 4, TT], BF16) for _ in range(2)]
            ybtb = [P.alloc(S, [128, 4, TT], BF16) for _ in range(2)]
            frtb = [P.alloc(S, [128, 4, TT], BF16) for _ in range(2)]
            rs = P.alloc(S, [128, TT], F32)
            tmp = P.alloc(S, [128, 2, TT], F32)
            gsb = [P.alloc(S, [128, TT], BF16) for _ in range(3)]
            tb3 = [P.alloc(S, [128, TT], F32) for _ in range(3)]
            Y = P.alloc(S, [128, NCH, TT], BF16)
            src = xsrc(l)

            def b_loads(t):
                b = t % 2
                cols = slice(t * TT, (t + 1) * TT)
                P.dma("sp", hb2[b][:], xview(H1, t), (), [f"h{b}_{c}" for c in range(NCH)], f"hld{b}")
                P.dma("sp", xtb[b][:], xview(src, t), (), [f"xt{b}_{c}" for c in range(NCH)], f"xt{b}")
                P.dma("sp", gatb[b][:], GA.rearrange("(c p) t -> p c t", p=128)[:, :, cols], ["GA"], [f"gat{b}"], f"gat{b}")
                P.dma("sp", ybtb[b][:], YBIN.rearrange("(c p) t -> p c t", p=128)[:, :, cols], ["YBIN"], [f"ybt{b}"], f"ybt{b}")
                P.dma("sp", frtb[b][:], FR.rearrange("(c p) t -> p c t", p=128)[:, :, cols], ["FR"], [f"frt{b}"], f"frt{b}")

            def b_prologue(t, parts=(0, 1, 2, 3)):
                b = t % 2
                norm_parts(parts, xtb[b], [f"xt{b}_{c}" for c in range(NCH)], hb2[b], [f"h{b}_{c}" for c in range(NCH)],
                           rs, "rs", tmp, "tmp", a1, f"a1{l}", 0, l, ci_of(t), sq_act=(t == 0))

            b_loads(0)
            for t in range(NT):
                ci = ci_of(t)
                b = t % 2
                xt, h, gat, ybt, frt = xtb[b], hb2[b], gatb[b], ybtb[b], frtb[b]
                xk = [f"xt{b}_{c}" for c in range(NCH)]
                hkeys = [f"h{b}_{c}" for c in range(NCH)]
                if t + 1 < NT:
                    b_loads(t + 1)
                for j in range(NCH):
                    js = slice(j * 128, (j + 1) * 128)
                    ybanks = []
                    for (wt, wkk, it, itk) in ((wao, "wao", gat, f"gat{b}"), (wbo, "wbo", ybt, f"ybt{b}"),
                                               (wco, "wco", frt, f"frt{b}")):
                        pb, pk = nb()
                        for kc in range(4):
                            P.mm(pb[:], wt[:, kc, js], it[:, kc, :], kc == 0, kc == 3, [wkk, itk], [pk])
                        ybanks.append((pb, pk))
                    for bb in range(3):
                        pb, pk = nb()
                        c0 = bb * D + j * 128
                        for kc in range(NCH):
                            P.mm(pb[:], wmg[:, kc, c0:c0 + 128], h[:, kc, :], kc == 0, kc == NCH - 1,
                                 [mgk[c0 // 512], hkeys[kc]], [pk])
                        P.act(gsb[bb][:], pb[:], AF.Sigmoid, [pk], [f"gsb{bb}"])
                        P.tt("dve", tb3[bb][:], ybanks[bb][0][:], gsb[bb][:], ALU.mult, [ybanks[bb][1], f"gsb{bb}"],
                             [f"tb3{bb}"])
                    P.tt("pool", tb3[0][:], tb3[0][:], tb3[1][:], ALU.add, ["tb30", "tb31"], ["tb30"])
                    P.tt("pool", Y[:, j, :], tb3[0][:], tb3[2][:], ALU.add, ["tb30", "tb32"], [f"Y{j}"])
                for j in range(NCH):
                    js = slice(j * 128, (j + 1) * 128)
                    pb, pk = nb()
                    for kc in range(NCH):
                        P.mm(pb[:], woo[:, kc, js], Y[:, kc, :], kc == 0, kc == NCH - 1, ["woo", f"Y{kc}"], [pk])
                    P.stt("dve", xt[:, j, :], pb[:], modc(l, 2, j, ci), xt[:, j, :], ALU.mult, ALU.add,
                          [pk, xk[j], f"mod{l}"], [xk[j]])
                P.dma("sp", xview(xs, t), xt[:], xk, ["xs"], f"xto{b}")
            P.flush()

        with contextlib.ExitStack() as S:
            wslot[0] = 0
            wup = P.alloc(S, [128, NCH, 2 * DFF], BF16)
            upk = load_w(wup, w_up[l], NCH, 2 * DFF, "wup")
            wdn = P.alloc(S, [128, NFF, D], BF16)
            dnk = load_w(wdn, w_down[l], NFF, D, "wdn", piece=256)
            xt = P.alloc(S, [128, NCH, TT], F32)
            hb2 = [P.alloc(S, [128, NCH, TT], BF16) for _ in range(2)]
            rs = P.alloc(S, [128, TT], F32)
            hid = P.alloc(S, [128, NFF, TT], BF16)
            sgb = P.alloc(S, [128, 2, TT], BF16)
            xe = P.alloc(S, [128, 2, TT], F32)
            xo = P.alloc(S, [128, 2, TT], F32)
            xkeys = [f"xt{c}" for c in range(NCH)]
            xsv = xs.rearrange("(c p) t -> p c t", p=128)

            def c_prologue(t, parts=(0, 1, 2, 3)):
                b = t % 2
                norm_parts(parts, xt, xkeys, hb2[b], [f"h{b}_{c}" for c in range(NCH)], rs, "rs", xo, "xo",
                           a2, f"a2{l}", 3, l, ci_of(t), sq_act=(t == 0))

            P.dma("sp", xt[:], xview(xs, 0), (), xkeys, "xt")
            c_prologue(0)
            for t in range(NT):
                ci = ci_of(t)
                b = t % 2
                h = hb2[b]
                hkeys = [f"h{b}_{c}" for c in range(NCH)]
                if t + 1 < NT:
                    P.dma("sp", xt[:], xview(xs, t + 1), (), xkeys, "xt")
                for j in range(NFF):
                    pg, pgk = nb()
                    c0 = j * 128
                    for kc in range(NCH):
                        P.mm(pg[:], wup[:, kc, c0:c0 + 128], h[:, kc, :], kc == 0, kc == NCH - 1,
                             [upk[c0 // 512], hkeys[kc]], [pgk])
                    pu, puk = nb()
                    c1 = DFF + j * 128
                    for kc in range(NCH):
                        P.mm(pu[:], wup[:, kc, c1:c1 + 128], h[:, kc, :], kc == 0, kc == NCH - 1,
                             [upk[c1 // 512], hkeys[kc]], [puk])
                    P.act(sgb[:, j % 2, :], pg[:], AF.Silu, [pgk], [f"sg{j % 2}"])
                    P.tt("dve", hid[:, j, :], pu[:], sgb[:, j % 2, :], ALU.mult, [puk, f"sg{j % 2}"], [f"hid{j}"])
                    if t + 1 < NT and j in (6, 8, 10, 12):
                        c_prologue(t + 1, ((j - 6) // 2,))
                cols = slice(t * TT, (t + 1) * TT)
                P.dma("sp", xe[:, 0, :], xsv[:, 0, cols], (), ["xe0"], "xe0")
                for j in range(NCH):
                    js = slice(j * 128, (j + 1) * 128)
                    pb, pk = nb()
                    for kc in range(NFF):
                        P.mm(pb[:], wdn[:, kc, js], hid[:, kc, :], kc == 0, kc == NFF - 1,
                             [dnk[(j * 128) // 256], f"hid{kc}"], [pk])
                    if j + 1 < NCH:
                        P.dma("sp", xe[:, (j + 1) % 2, :], xsv[:, j + 1, cols], (), [f"xe{(j + 1) % 2}"], f"xe{(j + 1) % 2}")
                    P.stt("dve", xo[:, j % 2, :], pb[:], modc(l, 5, j, ci), xe[:, j % 2, :], ALU.mult, ALU.add,
                          [pk, f"xe{j % 2}", f"mod{l}"], [f"xo{j % 2}"])
                    P.dma("sp", xsv[:, j, cols], xo[:, j % 2, :], [f"xo{j % 2}"], (), f"xo{j % 2}")
            P.flush()

    with contextlib.ExitStack() as S:
        xb = [P.alloc(S, [128, NCH, TT], F32) for _ in range(2)]
        sqb = [P.alloc(S, [128, NCH, TT], BF16) for _ in range(2)]
        rsb = [P.alloc(S, [128, TT], F32) for _ in range(2)]
        o = voff["nf"]
        P.dma("sp", xb[0][:], xview(xs, 0), (), [f"x0_{c}" for c in range(NCH)], "xt0")
        for t in range(NT):
            b = t % 2
            xk = [f"x{b}_{c}" for c in range(NCH)]
            if t + 1 < NT:
                P.dma("sp", xb[1 - b][:], xview(xs, t + 1), (), [f"x{1 - b}_{c}" for c in range(NCH)], f"xt{1 - b}")
            P.act(sqb[b][:], xb[b][:], AF.Square, xk, [f"sq{b}"])
            pb, pk = nb()
            for c in range(NCH):
                P.mm(pb[:], ones_bf[:], sqb[b][:, c, :], c == 0, c == NCH - 1, ["ones", f"sq{b}"], [pk])
            P.act(rsb[b][:], pb[:], AF.Ln, [pk], [f"rs{b}"], bias=EPS, scale=1.0 / D)
            P.act(rsb[b][:], rsb[b][:], AF.Exp, [f"rs{b}"], [f"rs{b}"], scale=-0.5)
            for c in range(NCH):
                P.stt("dve", xb[b][:, c, :], xb[b][:, c, :], vecs[:, o + c:o + c + 1], rsb[b][:], ALU.mult, ALU.mult,
                      [xk[c], f"rs{b}", "vecs"], [xk[c]])
            P.dma("sp", xview(y_out, t), xb[b][:], xk, ["y"], f"xto{b}")
        P.flush()
    P.outer.close()
    return nc, P


_CACHE = {}


def run(inputs, depth, LS, n_cores, seq_of_core, sample_owner=None):
    f32 = lambda a: np.ascontiguousarray(np.asarray(a, dtype=np.float32))
    key = (depth, LS)
    if key not in _CACHE:
        _CACHE[key] = (build(depth, LS), _consts(LS))
    (nc, P), (cmk, amk, cnk, c256k, clk, cnyk, jpk, cpkk, identk) = _CACHE[key]
    xp, xsm = f32(inputs["x_prompt"]), f32(inputs["x_sample"])
    vec = _pack_vecs(depth, f32(inputs["b_ada"]), f32(inputs["norm1"]), f32(inputs["norm2"]),
                     f32(inputs["gla_norm"]), f32(inputs["conv_w"]), f32(inputs["norm_f"]))
    shared = {k: f32(inputs[k]) for k in ("w_ada", "w_in", "w_gk_f", "w_gk_b", "b_gk_f", "b_gk_b", "w_a_out",
                                           "w_b_out", "w_c_out", "w_o", "w_up", "w_down")}
    shared.update({"vecs": vec, "cmask": cmk, "amask": amk, "cn": cnk, "c256": c256k, "cl": clk, "cny": cnyk, "jperm": jpk, "cpk": cpkk, "ident": identk})
    in_maps = []
    voff, _ = _vec_layout(depth)
    for c in range(n_cores):
        s = seq_of_core(c)
        dup = (sample_owner is not None) and (not sample_owner(c))
        xs_c = np.zeros_like(xsm[s]) if dup else xsm[s]
        xall = np.concatenate([xp[2 * c], xp[2 * c + 1], xs_c], axis=0)
        c_s = np.zeros_like(f32(inputs["c"])[s]) if dup else f32(inputs["c"])[s]
        cnd = np.stack([_fm(f32(inputs["c_ctx"])), _fm(c_s)], axis=-1)
        m = dict(shared)
        if dup:
            v2 = vec.copy()
            for l in range(depth):
                o = voff[("bada", l)]
                v2[:, o + 1:o + 96:2] = 0.0
            m["vecs"] = v2
        m["x0"] = np.ascontiguousarray(xall.T)
        m["state"] = np.zeros_like(f32(inputs["state_gla"])[s]) if dup else f32(inputs["state_gla"])[s]
        m["cond"] = np.ascontiguousarray(cnd)
        in_maps.append(m)
    res = run_bass_kernel_spmd(nc, in_maps, core_ids=list(range(n_cores)))
    return res.results


def kernel(**inputs):
    depth, LS, n_cores = 4, 4096, 8
    res = run(inputs, depth, LS, n_cores, lambda c: c // 2, sample_owner=lambda c: c % 2 == 0)
    B = 2 * n_cores
    y_prompt = np.empty((B, LP, D), np.float32)
    y_sample = np.empty((n_cores // 2, LS, D), np.float32)
    ns = np.empty((B, depth, 2, NH, DK, DV), np.float32)
    for c in range(n_cores):
        y = np.asarray(res[c]["y"], np.float32)
        y_prompt[2 * c] = y[:, 0:LP].T
        y_prompt[2 * c + 1] = y[:, LP:2 * LP].T
        if c % 2 == 0:
            y_sample[c // 2] = y[:, 2 * LP:].T
        nsc = np.asarray(res[c]["ns"], np.float32)
        ns[2 * c] = nsc[0]
        ns[2 * c + 1] = nsc[1]
    return (y_prompt, y_sample, ns)
```

```python
import contextlib
import math
import numpy as np
import ml_dtypes
import concourse.bass as bass
import concourse.mybir as mybir
from concourse.bass_utils import run_bass_kernel_spmd

F32 = mybir.dt.float32
BF16 = mybir.dt.bfloat16
AF = mybir.ActivationFunctionType
ALU = mybir.AluOpType

D = 1024
NCH = 8
TT = 512
LP = 256
NH, DK, DV = 4, 64, 128
QKW, VW = 256, 512
SCW, FNW, FNG = 512, 512, 4
DFF = 2816
NFF = 22
PIN = 6688
OFF_Q, OFF_K, OFF_V, OFF_OG, OFF_GKF, OFF_GKB = 0, 256, 512, 1024, 1536, 1552
OFF_SB, OFF_SC, OFF_SX, OFF_FX, OFF_MG = 1568, 2080, 2592, 3104, 3616
EPS = 1e-6
GRID_W = 64
GCH = 128

ENGS = ("pe", "act", "dve", "pool", "sp")
HND = {"pe": "tensor", "act": "scalar", "dve": "vector", "pool": "gpsimd", "sp": "sync"}


class Op:
    __slots__ = ("eng", "fn", "reads", "writes", "dma", "signal", "sigval", "key")

    def __init__(self, eng, fn, reads, writes, dma):
        self.eng, self.fn, self.reads, self.writes, self.dma = eng, fn, reads, writes, dma
        self.signal = False
        self.sigval = 0
        self.key = ("dma", dma) if dma is not None else ("eng", eng)


class Prog:
    SAME_ENGINE_SYNC = True

    def __init__(self, nc):
        self.nc = nc
        self.ops = []
        self.outer = contextlib.ExitStack()
        self.sems = {}
        self.sigcount = {}
        self.n = 0
        self.total = {e: 0 for e in ENGS}

    def sem(self, key):
        if key not in self.sems:
            nm = "s" + str(len(self.sems))
            self.sems[key] = self.outer.enter_context(self.nc.semaphore(nm))
            self.sigcount[key] = 0
        return self.sems[key]

    def alloc(self, stack, shape, dtype, psum=False):
        self.n += 1
        f = self.nc.psum_tensor if psum else self.nc.sbuf_tensor
        return stack.enter_context(f(f"t{self.n}", list(shape), dtype))

    def op(self, eng, fn, reads=(), writes=(), dma=None):
        self.ops.append(Op(eng, fn, tuple(reads), tuple(writes), dma))

    def mm(self, out, lhsT, rhs, start, stop, reads, writes):
        self.op("pe", lambda e: e.matmul(out, lhsT, rhs, start=start, stop=stop, skip_group_check=True),
                reads, writes)

    def act(self, out, in_, func, reads, writes, bias=0.0, scale=1.0):
        self.op("act", lambda e: e.activation(out=out, in_=in_, func=func, bias=bias, scale=scale), reads, writes)

    def tt(self, eng, out, in0, in1, op, reads, writes):
        self.op(eng, lambda e: e.tensor_tensor(out=out, in0=in0, in1=in1, op=op), reads, writes)

    def ts(self, eng, out, in0, s1, op0, reads, writes, s2=None, op1=None):
        if op1 is None:
            self.op(eng, lambda e: e.tensor_scalar(out=out, in0=in0, scalar1=s1, scalar2=None, op0=op0), reads, writes)
        else:
            self.op(eng, lambda e: e.tensor_scalar(out=out, in0=in0, scalar1=s1, scalar2=s2, op0=op0, op1=op1),
                    reads, writes)

    def stt(self, eng, out, in0, scalar, in1, op0, op1, reads, writes):
        self.op(eng, lambda e: e.scalar_tensor_tensor(out=out, in0=in0, scalar=scalar, in1=in1, op0=op0, op1=op1),
                reads, writes)

    def copy(self, eng, out, in_, reads, writes):
        if eng == "act":
            self.op("act", lambda e: e.activation(out=out, in_=in_, func=AF.Copy), reads, writes)
        else:
            self.op(eng, lambda e: e.tensor_copy(out=out, in_=in_), reads, writes)

    def memset(self, eng, ap, val, writes):
        self.op(eng, lambda e: e.memset(ap, val), (), writes)

    def dma(self, q, out, in_, reads, writes, slot):
        self.op(q, lambda e: e.dma_start(out=out, in_=in_), reads, writes, dma=slot)

    def flush(self):
        nc = self.nc
        ops = self.ops
        self.ops = []
        last_writer, readers = {}, {}
        pos_ctr = {}
        pos = []
        for o in ops:
            pos_ctr[o.key] = pos_ctr.get(o.key, 0) + 1
            pos.append(pos_ctr[o.key])
        seen = {e: {} for e in ENGS}
        deps_all = []
        for i, o in enumerate(ops):
            deps = set()
            for r in o.reads:
                j = last_writer.get(r)
                if j is not None:
                    deps.add(j)
            for w in o.writes:
                j = last_writer.get(w)
                if j is not None:
                    deps.add(j)
                deps.update(readers.get(w, ()))
            for r in o.reads:
                readers.setdefault(r, []).append(i)
            for w in o.writes:
                last_writer[w] = i
                readers[w] = []
            need = {}
            for j in deps:
                if j == i:
                    continue
                pj = ops[j]
                if pj.dma is None and pj.eng == o.eng and (o.eng == "pe" or not self.SAME_ENGINE_SYNC):
                    continue
                if need.get(pj.key, (0, 0))[0] < pos[j]:
                    need[pj.key] = (pos[j], j)
            final = []
            for key, (p, j) in need.items():
                if seen[o.eng].get(key, 0) >= p:
                    continue
                seen[o.eng][key] = p
                final.append(j)
                ops[j].signal = True
            deps_all.append(final)
        last_of = {}
        for i, o in enumerate(ops):
            if o.dma is None:
                last_of[o.eng] = i
        for i in last_of.values():
            ops[i].signal = True
        for o in ops:
            s = self.sem(o.key)
            if o.dma is not None:
                self.sigcount[o.key] += 16
                o.sigval = self.sigcount[o.key]
                o.signal = True
            elif o.signal:
                self.sigcount[o.key] += 1
                o.sigval = self.sigcount[o.key]
        per_eng = {e: [] for e in ENGS}
        for i, o in enumerate(ops):
            per_eng[o.eng].append(i)
            self.total[o.eng] += 1
        prev = getattr(self, "_barrier_prev", {})
        barrier = {k: v for k, v in self.sigcount.items() if v > prev.get(k, 0)}
        self._barrier_prev = dict(self.sigcount)
        sems = self.sems

        def make(e):
            def body(eng):
                for i in per_eng[e]:
                    o = ops[i]
                    for j in deps_all[i]:
                        eng.wait_ge(sems[ops[j].key], ops[j].sigval)
                    ins = o.fn(eng)
                    if o.signal:
                        ins.then_inc(sems[o.key], 16 if o.dma is not None else 1)
                for key, val in barrier.items():
                    if val > 0 and key != ("eng", e):
                        eng.wait_ge(sems[key], val)
            return body

        with nc.Block() as block:
            for e in ENGS:
                getattr(block, HND[e])(make(e))


def _bf(a):
    return np.ascontiguousarray(a.astype(np.float32)).astype(ml_dtypes.bfloat16)


def _consts(LS):
    s = np.arange(128)[:, None]
    t = np.arange(128)[None, :]
    same = (s // GCH) == (t // GCH)
    m_inc_f = (same & (s <= t)).astype(np.float32)
    m_str_f = (same & (s > t)).astype(np.float32)
    m_inc_b = (same & (s >= t)).astype(np.float32)
    m_str_b = (same & (s < t)).astype(np.float32)
    cm = np.stack([m_inc_f, m_str_f, m_inc_b, m_str_b], axis=1)
    am = np.stack([np.repeat(m_inc_f[:, None, :], 4, 1), np.repeat(m_inc_b[:, None, :], 4, 1)], axis=1)
    n = np.arange(128)
    ang = 2 * np.pi * np.outer(n, n) / 128.0
    cn = np.concatenate([np.cos(ang), np.sin(ang)], axis=1) / np.sqrt(128.0)

    def tab(L):
        l = np.arange(L, dtype=np.int64)
        kl = np.outer(l, l) % L
        a = 2 * np.pi * kl / float(L)
        return np.stack([np.cos(a), -np.sin(a)], axis=0) / np.sqrt(float(L))

    c256 = tab(LP)
    c256s = c256.reshape(2, 2, 128, LP).transpose(2, 1, 0, 3)
    cl = tab(LS)[:, :LS // 2, :LS // 2]
    cl = cl.reshape(2, LS // 256, 128, LS // 512, 256).transpose(3, 2, 1, 0, 4)
    lidx = np.arange(LS).reshape(LS // 128, 128).T
    cny = np.where(lidx % 2 == 0, 1.0, -1.0) / np.sqrt(float(LS))
    pp = np.arange(128)
    perm = ((pp[:, None] + pp[None, :]) == 128).astype(np.float32)
    e00 = np.zeros((128, 128), np.float32)
    e00[0, 0] = 1.0
    jperm = np.stack([perm, e00], axis=1)
    kk = np.arange(LS // 2)
    cpk = (np.where(kk % 2 == 0, 1.0, -1.0) / np.sqrt(float(LS)))[None, :]
    return (_bf(cm), am.astype(np.float32), _bf(cn), _bf(np.ascontiguousarray(c256s)), _bf(cl), _bf(cny),
            _bf(jperm), _bf(cpk), _bf(np.eye(128)))


class VecPack:
    def __init__(self):
        self.cols = []
        self.off = {}
        self.n = 0

    def add(self, name, arr):
        arr = np.asarray(arr, np.float32).reshape(128, -1)
        self.off[name] = self.n
        self.cols.append(arr)
        self.n += arr.shape[1]

    def build(self):
        return np.ascontiguousarray(np.concatenate(self.cols, axis=1))


def _fm(v):
    return np.asarray(v, np.float32).reshape(-1, 128).T


def _vec_layout(depth):
    off, n = {}, 0
    for l in range(depth):
        for nm, k in (("bada", 96), ("n1", 16), ("n2", 16), ("glan", 1), ("convw", 12)):
            off[(nm, l)] = n
            n += k
    off["nf"] = n
    n += 8
    return off, n


def _pack_vecs(depth, b_ada, norm1, norm2, gla_norm, conv_w, norm_f):
    off, n = _vec_layout(depth)
    out = np.zeros((128, n), np.float32)
    for l in range(depth):
        o = off[("bada", l)]
        out[:, o:o + 96] = np.repeat(_fm(b_ada[l]), 2, axis=1)
        o = off[("n1", l)]
        out[:, o:o + 16] = np.repeat(_fm(norm1[l]), 2, axis=1)
        o = off[("n2", l)]
        out[:, o:o + 16] = np.repeat(_fm(norm2[l]), 2, axis=1)
        o = off[("glan", l)]
        out[:, o:o + 1] = np.asarray(gla_norm[l], np.float32).reshape(128, 1)
        o = off[("convw", l)]
        for i in range(3):
            out[:, o + i * 4:o + i * 4 + 4] = _fm(conv_w[l, i])
    o = off["nf"]
    out[:, o:o + 8] = _fm(norm_f)
    return out


def build(depth, LS):
    T = 2 * LP + LS
    NT = T // TT
    nc = bass.Bass("TRN2", target_bir_lowering=False)
    voff, nvec = _vec_layout(depth)

    def din(name, shape, dt=F32):
        return nc.dram_tensor(name, list(shape), dt, kind="ExternalInput").ap()

    def dscr(name, shape, dt):
        return nc.dram_tensor(name, list(shape), dt).ap()

    x0 = din("x0", [D, T])
    state = din("state", [depth, 2, NH, DK, DV])
    cond = din("cond", [128, NCH, 2])
    vecs_d = din("vecs", [128, nvec])
    w_ada = din("w_ada", [depth, D, 6 * D])
    w_in = din("w_in", [depth, D, PIN])
    w_gk = [din("w_gk_f", [depth, 16, QKW]), din("w_gk_b", [depth, 16, QKW])]
    b_gk = [din("b_gk_f", [depth, QKW]), din("b_gk_b", [depth, QKW])]
    w_a_out = din("w_a_out", [depth, VW, D])
    w_b_out = din("w_b_out", [depth, SCW, D])
    w_c_out = din("w_c_out", [depth, FNW, D])
    w_o = din("w_o", [depth, D, D])
    w_up = din("w_up", [depth, D, 2 * DFF])
    w_down = din("w_down", [depth, DFF, D])
    cm_d = din("cmask", [128, 4, 128], BF16)
    am_d = din("amask", [128, 2, 4, 128])
    cn_d = din("cn", [128, 256], BF16)
    c256_d = din("c256", [128, 2, 2, LP], BF16)
    cl_d = din("cl", [LS // 512, 128, LS // 256, 2, 256], BF16)
    jp_d = din("jperm", [128, 2, 128], BF16)
    cpk_d = din("cpk", [1, LS // 2], BF16)
    id_d = din("ident", [128, 128], BF16)
    cny_d = din("cny", [128, LS // 128], BF16)
    y_out = nc.dram_tensor("y", [D, T], F32, kind="ExternalOutput").ap()
    ns_out = nc.dram_tensor("ns", [2, depth, 2, NH, DK, DV], F32, kind="ExternalOutput").ap()

    xs = dscr("xs", [D, T], F32)
    QT = dscr("QT", [NH, DK, T], BF16)
    KT = dscr("KT", [NH, DK, T], BF16)
    KTM = dscr("KTM", [T, QKW], BF16)
    VTM = dscr("VTM", [T, VW], BF16)
    GK = [dscr("GKF", [16, T], BF16), dscr("GKB", [16, T], BF16)]
    SOG = dscr("SOG", [VW, T], BF16)
    YBIN = dscr("YBIN", [SCW, T], BF16)
    AB = dscr("AB", [T, 1024], BF16)
    OF = dscr("OF", [VW, T], F32)
    OS = dscr("OS", [VW, T], F32)
    H1 = dscr("H1", [D, T], BF16)
    GA = dscr("GA", [VW, T], BF16)
    FR = dscr("FR", [FNW, T], BF16)

    P = Prog(nc)
    G = P.outer
    banks = [P.alloc(G, [128, 512], F32, psum=True) for _ in range(8)]
    bank_ctr = [0]

    nb_mod = [8]

    def nb():
        i = bank_ctr[0] % nb_mod[0]
        bank_ctr[0] += 1
        return banks[i], f"ps{i}"

    vecs = P.alloc(G, [128, nvec], F32)
    ones_bf = P.alloc(G, [128, 128], BF16)
    ident_bf = P.alloc(G, [128, 128], BF16)
    mod = P.alloc(G, [128, depth, 96], F32)
    a1 = P.alloc(G, [128, depth, 16], F32)
    a2 = P.alloc(G, [128, depth, 16], F32)
    condt = P.alloc(G, [128, NCH, 2], F32)
    scb = P.alloc(G, [128, NCH, 2], BF16)

    def vcol(name, l, i):
        o = voff[(name, l)] + i
        return vecs[:, o:o + 1]

    def modc(l, which, c, ci):
        o = (which * 8 + c) * 2 + ci
        return mod[:, l, o:o + 1]

    wv = w_ada.rearrange("l (kc p) n -> l p kc n", p=128)

    def ada_gen(l, wa, pb, pk):
        def issue(pc):
            wk = f"wa{pc % 3}"
            P.dma("pool", wa[pc % 3][:], wv[l, :, :, pc * 512:(pc + 1) * 512], (), [wk], wk)
        issue(0)
        for pc in range(12):
            if pc + 1 < 12:
                issue(pc + 1)
            wt = wa[pc % 3]
            wk = f"wa{pc % 3}"
            for jj in range(4):
                j = pc * 4 + jj
                for kc in range(NCH):
                    P.mm(pb[:, 2 * j:2 * j + 2], wt[:, kc, jj * 128:(jj + 1) * 128], scb[:, kc, :],
                         kc == 0, kc == NCH - 1, [wk, "scb"], [pk])
            yield
        o = voff[("bada", l)]
        P.tt("dve", mod[:, l, :], pb[:, 0:96], vecs[:, o:o + 96], ALU.add, [pk, "vecs"], [f"mod{l}"])
        o1 = voff[("n1", l)]
        P.stt("dve", a1[:, l, :], mod[:, l, 16:32], 1.0, vecs[:, o1:o1 + 16], ALU.add, ALU.mult,
              [f"mod{l}", "vecs"], [f"a1{l}"])
        o2 = voff[("n2", l)]
        P.stt("dve", a2[:, l, :], mod[:, l, 64:80], 1.0, vecs[:, o2:o2 + 16], ALU.add, ALU.mult,
              [f"mod{l}", "vecs"], [f"a2{l}"])
        yield

    with contextlib.ExitStack() as S:
        P.dma("sp", vecs[:], vecs_d, (), ["vecs"], "ld0")
        P.dma("sp", condt[:], cond, (), ["condt"], "ld5")
        P.dma("sp", ident_bf[:], id_d, (), ["ident"], "ld8")
        P.memset("pool", ones_bf[:], 1.0, ["ones"])
        P.act(scb[:], condt[:], AF.Silu, ["condt"], ["scb"])
        wa0 = [P.alloc(S, [128, NCH, 512], BF16) for _ in range(3)]
        pb0, pk0 = nb()
        for _ in ada_gen(0, wa0, pb0, pk0):
            pass
        P.flush()

    def ci_of(t):
        return 0 if t == 0 else 1

    def xsrc(l):
        return x0 if l == 0 else xs

    def xview(src, t):
        return src.rearrange("(c p) t -> p c t", p=128)[:, :, t * TT:(t + 1) * TT]

    def norm_parts(parts, xt, xkeys, hbuf, hkeys, rs, rsk, tmp, tmpk, avec, akey, shw, l, ci, sq_act=False):
        if 0 in parts:
            if sq_act:
                P.act(hbuf[:], xt[:], AF.Square, sorted(set(xkeys)), hkeys)
            else:
                for c in range(NCH):
                    P.tt("pool", hbuf[:, c, :], xt[:, c, :], xt[:, c, :], ALU.mult, [xkeys[c]], [hkeys[c]])
        if 1 in parts:
            pb, pk = nb()
            for c in range(NCH):
                P.mm(pb[:], ones_bf[:], hbuf[:, c, :], c == 0, c == NCH - 1, ["ones", hkeys[c]], [pk])
            P.act(rs[:], pb[:], AF.Ln, [pk], [rsk], bias=EPS, scale=1.0 / D)
            P.act(rs[:], rs[:], AF.Exp, [rsk], [rsk], scale=-0.5)
        for part in (2, 3):
            if part in parts:
                for c in range(4 * (part - 2), 4 * (part - 1)):
                    P.tt("dve", tmp[:, c % 2, :], xt[:, c, :], rs[:], ALU.mult, [xkeys[c], rsk], [tmpk + str(c % 2)])
                    P.ts("dve", hbuf[:, c, :], tmp[:, c % 2, :], avec[:, l, c * 2 + ci:c * 2 + ci + 1], ALU.mult,
                         [tmpk + str(c % 2), akey, f"mod{l}"], [hkeys[c]], s2=modc(l, shw, c, ci), op1=ALU.add)

    wslot = [0]

    def next_w():
        wslot[0] += 1
        return f"w{wslot[0] - 1}"

    def load_w(wt, src2d, kch, ncols, name, piece=512):
        v = src2d.rearrange("(kc p) n -> p kc n", p=128)
        keys = []
        for i, c0 in enumerate(range(0, ncols, piece)):
            c1 = min(ncols, c0 + piece)
            k = f"{name}{i}"
            P.dma("pool", wt[:, :, c0:c1], v[:, :, c0:c1], (), [k], next_w())
            keys.append(k)
        return keys

    for l in range(depth):
        last = l == depth - 1
        with contextlib.ExitStack() as S:
            wslot[0] = 0
            NA = OFF_MG
            win = P.alloc(S, [128, NCH, NA], BF16)
            cn = P.alloc(S, [128, 256], BF16)
            P.dma("sp", cn[:], cn_d, (), ["cn"], "ld3")
            wkeys = load_w(win, w_in[l][:, 0:NA], NCH, NA, "win")

            def wk_of(c0, c1):
                return sorted({wkeys[c // 512] for c in (c0, c1 - 1)})

            xt = P.alloc(S, [128, NCH, TT], F32)
            hb = [P.alloc(S, [128, NCH, TT], BF16) for _ in range(2)]
            rs = P.alloc(S, [128, TT], F32)
            tmp = P.alloc(S, [128, 2, TT], F32)
            st_q = P.alloc(S, [128, 2, TT], BF16)
            st_k = P.alloc(S, [128, 2, TT], BF16)
            st_ktm = P.alloc(S, [128, 4, QKW], BF16)
            st_vtm = P.alloc(S, [128, 4, VW], BF16)
            st_gk = P.alloc(S, [32, TT], BF16)
            st_sog = P.alloc(S, [128, 4, TT], BF16)
            st_yb = P.alloc(S, [128, 4, TT], BF16)
            fxT = P.alloc(S, [128, 4, TT], BF16)
            st_ab = P.alloc(S, [128, 4, 1024], BF16)
            sbf = P.alloc(S, [128, TT], F32)
            scf = P.alloc(S, [128, TT], F32)
            uu = P.alloc(S, [128, TT], F32)
            cv = P.alloc(S, [128, TT], F32)
            src = xsrc(l)
            P.dma("sp", xt[:], xview(src, 0), (), ["xt"], "xt")
            agen = None
            if l + 1 < depth:
                waA = [P.alloc(S, [128, NCH, 512], BF16) for _ in range(3)]
                nb_mod[0] = 7
                agen = ada_gen(l + 1, waA, banks[7], "ps7")

            def ada_step():
                if agen is not None:
                    next(agen, None)

            for t in range(NT):
                ci = ci_of(t)
                h = hb[t % 2]
                hk = f"h{t % 2}"
                hkeys = [f"{hk}_{c}" for c in range(NCH)]
                if t == 0:
                    norm_parts((0, 1, 2, 3), xt, ["xt"] * NCH, h, hkeys, rs, "rs", tmp, "tmp", a1, f"a1{l}", 0, l, ci,
                               sq_act=True)
                    P.dma("sp", xview(H1, 0), h[:], hkeys, (), "hst0")
                if t + 1 < NT:
                    P.dma("sp", xt[:], xview(src, t + 1), (), ["xt"], "xt")
                cols = slice(t * TT, (t + 1) * TT)

                def fm_group(c0, m, evac):
                    pb, pk = nb()
                    for kc in range(NCH):
                        P.mm(pb[0:m, :], win[:, kc, c0:c0 + m], h[:, kc, :], kc == 0, kc == NCH - 1,
                             wk_of(c0, c0 + m) + [hkeys[kc]], [pk])
                    evac(pb, pk)

                for pr2 in range(2):
                    fm_group(OFF_Q + pr2 * 128, 128,
                             lambda pb, pk, pr2=pr2: P.act(st_q[:, pr2, :], pb[:], AF.Copy, [pk], [f"st_q{pr2}"],
                                                           scale=DK ** -0.5))
                for pr2 in range(2):
                    fm_group(OFF_K + pr2 * 128, 128,
                             lambda pb, pk, pr2=pr2: P.copy("dve", st_k[:, pr2, :], pb[:], [pk], [f"st_k{pr2}"]))
                P.dma("sp", QT[:, :, cols].rearrange("h d t -> (h d) t").rearrange("(pr p) t -> p pr t", p=128),
                      st_q[:], ["st_q0", "st_q1"], ["QT"], "st_q")
                P.dma("sp", KT[:, :, cols].rearrange("h d t -> (h d) t").rearrange("(pr p) t -> p pr t", p=128),
                      st_k[:], ["st_k0", "st_k1"], ["KT"], "st_k")
                ada_step()
                if t + 1 < NT:
                    norm_parts((0,), xt, ["xt"] * NCH, hb[(t + 1) % 2], [f"h{(t + 1) % 2}_{c}" for c in range(NCH)],
                               rs, "rs", tmp, "tmp", a1, f"a1{l}", 0, l, ci_of(t + 1))
                fm_group(OFF_GKF, 32, lambda pb, pk: P.copy("dve", st_gk[:], pb[0:32, :], [pk], ["st_gk"]))
                for dr in range(2):
                    P.dma("sp", GK[dr][:, cols], st_gk[16 * dr:16 * dr + 16, :], ["st_gk"], [f"GK{dr}"], f"st_gk{dr}")
                for c in range(4):
                    fm_group(OFF_OG + c * 128, 128,
                             lambda pb, pk, c=c: P.act(st_sog[:, c, :], pb[:], AF.Silu, [pk], [f"st_sog{c}"]))
                P.dma("sp", SOG.rearrange("(c p) t -> p c t", p=128)[:, :, cols], st_sog[:], [f"st_sog{i}" for i in range(4)], ["SOG"],
                      "st_sog")
                for sb_ in range(4):
                    tk = slice(sb_ * 128, (sb_ + 1) * 128)
                    pb, pk = nb()
                    for pr2 in range(2):
                        P.mm(pb[:, pr2 * 128:(pr2 + 1) * 128], st_k[:, pr2, tk], ident_bf[:], True, True,
                             [f"st_k{pr2}", "ident"], [pk])
                    P.copy("act", st_ktm[:, sb_, :], pb[:, 0:QKW], [pk], ["st_ktm"])
                    pb, pk = nb()
                    for kc in range(NCH):
                        P.mm(pb[:], h[:, kc, tk], win[:, kc, OFF_V:OFF_V + VW], kc == 0, kc == NCH - 1,
                             wk_of(OFF_V, OFF_V + VW) + [hkeys[kc]], [pk])
                    P.copy("dve", st_vtm[:, sb_, :], pb[:], [pk], ["st_vtm"])
                if t + 1 < NT:
                    norm_parts((1, 2, 3), xt, ["xt"] * NCH, hb[(t + 1) % 2], [f"h{(t + 1) % 2}_{c}" for c in range(NCH)],
                               rs, "rs", tmp, "tmp", a1, f"a1{l}", 0, l, ci_of(t + 1))
                    P.dma("sp", xview(H1, t + 1), hb[(t + 1) % 2][:], [f"h{(t + 1) % 2}_{c}" for c in range(NCH)], (),
                          f"hst{(t + 1) % 2}")
                P.dma("sp", KTM[cols, :].rearrange("(s p) c -> p s c", p=128), st_ktm[:], ["st_ktm"], ["KTM"], "st_ktm")
                P.dma("sp", VTM[cols, :].rearrange("(s p) c -> p s c", p=128), st_vtm[:], ["st_vtm"], ["VTM"], "st_vtm")
                R = TT // GRID_W if t > 0 else 2
                W = TT // R
                for c in range(4):
                    fm_group(OFF_SB + c * 128, 128, lambda pb, pk: P.copy("act", sbf[:], pb[:], [pk], ["sbf"]))
                    fm_group(OFF_SC + c * 128, 128, lambda pb, pk: P.copy("act", scf[:], pb[:], [pk], ["scf"]))
                    fm_group(OFF_SX + c * 128, 128,
                             lambda pb, pk: P.tt("dve", uu[:], pb[:], scf[:], ALU.mult, [pk, "scf"], ["uu"]))
                    u3 = uu[:].rearrange("p (r w) -> p r w", w=W)
                    c3 = cv[:].rearrange("p (r w) -> p r w", w=W)
                    P.act(cv[:], uu[:], AF.Copy, ["uu", "vecs"], ["cv"], scale=vcol("convw", l, 4 + c))
                    P.stt("dve", c3[:, :, 1:W], u3[:, :, 0:W - 1], vcol("convw", l, 0 + c), c3[:, :, 1:W],
                          ALU.mult, ALU.add, ["uu", "cv", "vecs"], ["cv"])
                    P.stt("dve", c3[:, :, 0:W - 1], u3[:, :, 1:W], vcol("convw", l, 8 + c), c3[:, :, 0:W - 1],
                          ALU.mult, ALU.add, ["uu", "cv", "vecs"], ["cv"])
                    P.tt("pool", st_yb[:, c, :], sbf[:], cv[:], ALU.mult, ["sbf", "cv"], ["st_yb"])
                P.dma("sp", YBIN.rearrange("(c p) t -> p c t", p=128)[:, :, cols], st_yb[:], ["st_yb"], ["YBIN"], "st_yb")
                ada_step()
                for g in range(4):
                    fm_group(OFF_FX + g * 128, 128,
                             lambda pb, pk, g=g: P.copy("act", fxT[:, g, :], pb[:], [pk], [f"fxT{g}"]))
                for sb_ in range(4):
                    tk = slice(sb_ * 128, (sb_ + 1) * 128)
                    for half in range(2):
                        pb, pk = nb()
                        for gg in range(2):
                            g = half * 2 + gg
                            P.mm(pb[:, gg * 256:(gg + 1) * 256], fxT[:, g, tk], cn[:], True, True,
                                 [f"fxT{g}", "cn"], [pk])
                        P.copy("dve" if half else "act", st_ab[:, sb_, half * 512:(half + 1) * 512], pb[:], [pk],
                               ["st_ab"])
                P.dma("sp", AB[cols, :].rearrange("(s p) c -> p s c", p=128), st_ab[:], ["st_ab"], ["AB"], "st_ab")
            if agen is not None:
                for _ in agen:
                    pass
            nb_mod[0] = 8
            P.flush()

        for dr in range(2):
            with contextlib.ExitStack() as S:
                wslot[0] = 0
                cm = P.alloc(S, [128, 4, 128], BF16)
                am = P.alloc(S, [128, 2, 4, 128], F32)
                P.dma("sp", cm[:], cm_d, (), ["cm"], "ld1")
                P.dma("sp", am[:], am_d, (), ["am"], "ld2")
                wgk = P.alloc(S, [16, QKW], BF16)
                bgk = P.alloc(S, [1, QKW], BF16)
                ones_row = P.alloc(S, [1, 128], BF16)
                P.dma("pool", wgk[:], w_gk[dr][l], (), ["wgk"], next_w())
                P.dma("pool", bgk[:], b_gk[dr][l:l + 1, :], (), ["bgk"], next_w())
                P.memset("pool", ones_row[:], 1.0, ["ones_row"])
                qTb = [P.alloc(S, [64, NH, TT], BF16) for _ in range(3)]
                kTb = [P.alloc(S, [64, NH, TT], BF16) for _ in range(3)]
                ktmb = [P.alloc(S, [128, 4, QKW], BF16) for _ in range(3)]
                vtmb = [P.alloc(S, [128, 4, VW], BF16) for _ in range(3)]
                gkb = [P.alloc(S, [16, TT], BF16) for _ in range(3)]
                qinb = [P.alloc(S, [64, NH, TT], BF16) for _ in range(2)]
                kinb = [P.alloc(S, [64, NH, TT], BF16) for _ in range(2)]
                koutb = [P.alloc(S, [128, 4, QKW], BF16) for _ in range(2)]
                gcolb = [P.alloc(S, [64, NH, 8], F32) for _ in range(2)]
                e1 = P.alloc(S, [128, QKW], F32)
                sp_ = P.alloc(S, [128, 4, QKW], BF16)
                ebT = P.alloc(S, [64, NH, TT], F32)
                eiT = P.alloc(S, [64, NH, TT], F32)
                ec = P.alloc(S, [128, 4, QKW], F32)
                attm = [P.alloc(S, [128, NH, 128], BF16) for _ in range(2)]
                Sf = P.alloc(S, [64, NH, DV], F32)
                Sb = P.alloc(S, [64, NH, DV], BF16)
                osbb = [P.alloc(S, [128, NH, TT], F32) for _ in range(2)]
                if dr == 1:
                    oflb = [P.alloc(S, [128, NH, TT], F32) for _ in range(3)]
                minc = cm[:, 0 + 2 * dr, :]
                mstr = cm[:, 1 + 2 * dr, :]
                if dr == 0:
                    visits = [(0, [("p", 0, [0, 1]), ("p", 1, [2, 3])])]
                    visits += [(t, [("s", 0, [0, 1, 2, 3])]) for t in range(1, NT)]
                else:
                    visits = [(0, [("p", 0, [1, 0]), ("p", 1, [3, 2])])]
                    visits += [(t, [("s", 0, [3, 2, 1, 0])]) for t in range(NT - 1, 0, -1)]
                NV = len(visits)
                first_s = visits[1][0]
                last_s = visits[-1][0]
                obank = [banks[i] for i in range(4)]
                obk = [f"ps{i}" for i in range(4)]
                cr, pr_ = [0], [0]

                def cb():
                    i = 4 + cr[0] % 2
                    cr[0] += 1
                    return banks[i], f"ps{i}"

                def p1b():
                    i = 6 + pr_[0] % 2
                    pr_[0] += 1
                    return banks[i], f"ps{i}"

                def g_loads(i):
                    t = visits[i][0]
                    b = i % 3
                    cols = slice(t * TT, (t + 1) * TT)
                    P.dma("sp", gkb[b][:], GK[dr][:, cols], [f"GK{dr}"], [f"gk{b}"], f"gk{b}")
                    P.dma("sp", qTb[b][:], QT[:, :, cols].rearrange("h d t -> d h t"), ["QT"], [f"qT{b}"], f"qT{b}")
                    P.dma("sp", kTb[b][:], KT[:, :, cols].rearrange("h d t -> d h t"), ["KT"], [f"kT{b}"], f"kT{b}")
                    P.dma("sp", ktmb[b][:], KTM[cols, :].rearrange("(s p) c -> p s c", p=128), ["KTM"], [f"ktm{b}"], f"ktm{b}")
                    P.dma("sp", vtmb[b][:], VTM[cols, :].rearrange("(s p) c -> p s c", p=128), ["VTM"], [f"vtm{b}"], f"vtm{b}")
                    if dr == 1:
                        P.dma("sp", oflb[b][:], OF.rearrange("(h p) t -> p h t", p=128)[:, :, cols], ["OF"], [f"ofl{b}"], f"ofl{b}")

                def p1a(i, s_):
                    b = i % 2
                    b3 = i % 3
                    tk = slice(s_ * 128, (s_ + 1) * 128)
                    pb, pk = p1b()
                    P.mm(pb[:, 0:QKW], gkb[b3][:, tk], wgk[:], True, False, [f"gk{b3}", "wgk"], [pk])
                    P.mm(pb[:, 0:QKW], ones_row[:], bgk[:], False, True, ["ones_row", "bgk"], [pk])
                    P.act(e1[:], pb[:, 0:QKW], AF.Exp, [pk], ["e1"], scale=-1.0)
                    P.act(sp_[:, s_, :], e1[:], AF.Ln, ["e1"], [f"sp{s_}"], bias=1.0)

                def p1c(i, s_):
                    b = i % 2
                    b3 = i % 3
                    tk = slice(s_ * 128, (s_ + 1) * 128)
                    pb, pk = p1b()
                    for hh in range(NH):
                        P.mm(pb[0:64, hh * 128:(hh + 1) * 128], sp_[:, s_, hh * 64:(hh + 1) * 64], minc, True, True,
                             [f"sp{s_}", "cm"], [pk])
                    pv = pb[0:64, :].rearrange("p (h t) -> p h t", h=NH)
                    P.act(ebT[:, :, tk], pv, AF.Exp, [pk], [f"ebT{s_}"], scale=-1.0 / 16)
                    P.act(eiT[:, :, tk], pv, AF.Exp, [pk], [f"eiT{s_}"], scale=1.0 / 16)
                    pb, pk = p1b()
                    P.mm(pb[:, 0:QKW], mstr, sp_[:, s_, :], True, True, [f"sp{s_}", "cm"], [pk])
                    P.act(ec[:, s_, :], pb[:, 0:QKW], AF.Exp, [pk], [f"ec{s_}"], scale=-1.0 / 16)
                    P.tt("pool", qinb[b][:, :, tk], qTb[b3][:, :, tk], ebT[:, :, tk], ALU.mult, [f"qT{b3}", f"ebT{s_}"],
                         [f"qin{b}_{s_}"])
                    P.tt("pool", kinb[b][:, :, tk], kTb[b3][:, :, tk], eiT[:, :, tk], ALU.mult, [f"kT{b3}", f"eiT{s_}"],
                         [f"kin{b}_{s_}"])
                    P.tt("pool", koutb[b][:, s_, :], ktmb[b3][:, s_, :], ec[:, s_, :], ALU.mult, [f"ktm{b3}", f"ec{s_}"],
                         [f"kout{b}_{s_}"])
                    tl = s_ * 128 + (127 if dr == 0 else 0)
                    P.copy("pool", gcolb[b][:, :, s_:s_ + 1], ebT[:, :, tl:tl + 1], [f"ebT{s_}"], [f"gcol{b}_{s_}"])

                def p2_att(i, s_):
                    b = i % 2
                    tk = slice(s_ * 128, (s_ + 1) * 128)
                    pb, pk = p1b()
                    for hh in range(NH):
                        P.mm(pb[:, hh * 128:(hh + 1) * 128], kinb[b][:, hh, tk], qinb[b][:, hh, tk], True, True,
                             [f"kin{b}_{s_}", f"qin{b}_{s_}"], [pk])
                    P.tt("dve", attm[s_ % 2][:], pb[:].rearrange("p (h t) -> p h t", h=NH), am[:, dr, :, :], ALU.mult,
                         [pk, "am"], [f"attm{s_ % 2}"])

                def p2_o(i, s_):
                    b = i % 2
                    b3 = i % 3
                    tk = slice(s_ * 128, (s_ + 1) * 128)
                    for hh in range(NH):
                        P.mm(obank[hh][:, tk], vtmb[b3][:, s_, hh * DV:(hh + 1) * DV], attm[s_ % 2][:, hh, :], s_ == 0,
                             False, [f"vtm{b3}", f"attm{s_ % 2}"], [obk[hh]])

                def chain_step(i, ch):
                    b = i % 2
                    b3 = i % 3
                    ck = slice(ch * 128, (ch + 1) * 128)
                    s_ = ch
                    pr = slice(0, 128)
                    ub = []
                    for p2_ in range(2):
                        heads = (2 * p2_, 2 * p2_ + 1)
                        for hh in heads:
                            P.mm(obank[hh][:, ck], Sb[:, hh, :], qinb[b][:, hh, ck], False, True,
                                 [f"Sb{p2_}", f"qin{b}_{s_}"], [obk[hh]])
                        pb, pk = cb()
                        for hh in heads:
                            P.mm(pb[0:64, (hh % 2) * DV:(hh % 2 + 1) * DV], koutb[b][pr, s_, hh * 64:(hh + 1) * 64],
                                 vtmb[b3][pr, s_, hh * DV:(hh + 1) * DV], True, True, [f"kout{b}_{s_}", f"vtm{b3}"], [pk])
                        ub.append((pb, pk))
                    for p2_ in range(2):
                        heads = (2 * p2_, 2 * p2_ + 1)
                        pb, pk = ub[p2_]
                        for hh in heads:
                            P.stt("dve", Sf[:, hh, :], Sf[:, hh, :], gcolb[b][:, hh, ch:ch + 1],
                                  pb[0:64, (hh % 2) * DV:(hh % 2 + 1) * DV], ALU.mult, ALU.add,
                                  [f"Sf{hh}", f"gcol{b}_{ch}", pk], [f"Sf{hh}"])
                        P.copy("dve", Sb[:, 2 * p2_:2 * p2_ + 2, :], Sf[:, 2 * p2_:2 * p2_ + 2, :],
                               [f"Sf{hh}" for hh in heads], [f"Sb{p2_}"])

                g_loads(0)
                if NV > 1:
                    g_loads(1)
                for s_ in range(4):
                    p1a(0, s_)
                for s_ in range(4):
                    p1c(0, s_)
                for i, (t, parts) in enumerate(visits):
                    b = i % 2
                    cols = slice(t * TT, (t + 1) * TT)
                    if i + 2 < NV:
                        g_loads(i + 2)
                    p2_att(i, 0)
                    p2_att(i, 1)
                    p2_o(i, 0)
                    p2_att(i, 2)
                    p2_o(i, 1)
                    p2_att(i, 3)
                    p2_o(i, 2)
                    p2_o(i, 3)
                    k = 0
                    for (kind, sidx, chunks) in parts:
                        first_visit = (kind == "p") or (t == first_s)
                        last_visit = (kind == "p") or (t == last_s)
                        if first_visit:
                            if kind == "p":
                                P.memset("dve", Sf[:], 0.0, [f"Sf{hh}" for hh in range(NH)])
                                P.memset("dve", Sb[:], 0.0, ["Sb0", "Sb1"])
                            else:
                                P.dma("sp", Sf[:], state[l, dr].rearrange("h d e -> d h e"), (),
                                      [f"Sf{hh}" for hh in range(NH)], "Sf")
                                P.copy("dve", Sb[:], Sf[:], [f"Sf{hh}" for hh in range(NH)], ["Sb0", "Sb1"])
                        for ch in chunks:
                            chain_step(i, ch)
                            if i + 1 < NV:
                                p1a(i + 1, k)
                                if k >= 1:
                                    p1c(i + 1, k - 1)
                            k += 1
                        if last_visit and kind == "p":
                            P.dma("sp", ns_out[sidx, l, dr].rearrange("h d e -> d h e"), Sf[:],
                                  [f"Sf{hh}" for hh in range(NH)], ["ns"], "Sfo")
                    osb = osbb[b]
                    if i + 1 < NV:
                        p1c(i + 1, 3)
                    if dr == 0:
                        for hh in range(NH):
                            P.copy("act" if hh % 2 else "dve", osb[:, hh, :], obank[hh][:], [obk[hh]], [f"osb{b}_{hh}"])
                        P.dma("sp", OF.rearrange("(h p) t -> p h t", p=128)[:, :, cols], osb[:],
                              [f"osb{b}_{hh}" for hh in range(NH)], ["OF"], f"osb{b}")
                    else:
                        for hh in range(NH):
                            P.tt("dve", osb[:, hh, :], obank[hh][:], oflb[i % 3][:, hh, :], ALU.add, [obk[hh], f"ofl{i % 3}"],
                                 [f"osb{b}_{hh}"])
                        P.dma("sp", OS.rearrange("(h p) t -> p h t", p=128)[:, :, cols], osb[:],
                              [f"osb{b}_{hh}" for hh in range(NH)], ["OS"], f"osb{b}")
                P.flush()

        with contextlib.ExitStack() as S:
            KTW = 256
            NLC = LS // 128
            c256 = P.alloc(S, [128, 2, 2, LP], BF16)
            P.dma("sp", c256[:], c256_d, (), ["c256"], "ld4")
            abp = P.alloc(S, [128, 2, 1024], BF16)
            frp = P.alloc(S, [128, 4, LP], BF16)
            abs_ = P.alloc(S, [128, NLC, 1024], BF16)
            AST = min(8, NLC)
            for i in range(0, NLC, AST):
                P.dma("sp", abs_[:, i:i + AST, :],
                      AB[2 * LP + i * 128:2 * LP + (i + AST) * 128, :].rearrange("(lc p) c -> p lc c", p=128),
                      ["AB"], [f"abs{i}"], f"abs{i}")
            NL2 = NLC // 2
            tabs = [P.alloc(S, [128, NL2, 2, KTW], BF16) for _ in range(2)]
            jp = P.alloc(S, [128, 2, 128], BF16)
            cpk = P.alloc(S, [1, LS // 2], BF16)
            P.dma("sp", jp[:], jp_d, (), ["jp"], "ld6")
            P.dma("sp", cpk[:], cpk_d, (), ["cpk"], "ld7")
            osl = [P.alloc(S, [128, NH, TT], F32) for _ in range(2)]
            sgl = [P.alloc(S, [128, NH, TT], BF16) for _ in range(2)]
            gal = [P.alloc(S, [128, NH, TT], BF16) for _ in range(2)]
            sq4 = P.alloc(S, [128, 2, TT], BF16)
            rs4 = P.alloc(S, [128, 2, TT], F32)

            def tail_loads(t):
                b = t % 2
                cols = slice(t * TT, (t + 1) * TT)
                P.dma("sp", osl[b][:], OS.rearrange("(h p) t -> p h t", p=128)[:, :, cols], ["OS"],
                      [f"osl{b}_{h_}" for h_ in range(NH)], f"osl{b}")
                P.dma("sp", sgl[b][:], SOG.rearrange("(h p) t -> p h t", p=128)[:, :, cols], ["SOG"], [f"sgl{b}"], f"sgl{b}")

            def tail_a(u):
                t, hh = divmod(u, NH)
                b = t % 2
                if hh == 2 and t + 1 < NT:
                    tail_loads(t + 1)
                P.act(sq4[:, u % 2, :], osl[b][:, hh, :], AF.Square, [f"osl{b}_{hh}"], [f"sq4{u % 2}"])

            def tail_b(u):
                t, hh = divmod(u, NH)
                b = t % 2
                pb, pk = nb()
                P.mm(pb[:], ones_bf[:], sq4[:, u % 2, :], True, True, ["ones", f"sq4{u % 2}"], [pk])
                P.act(rs4[:, u % 2, :], pb[:], AF.Ln, [pk], [f"rs4{u % 2}"], bias=EPS, scale=1.0 / DV)
                P.act(rs4[:, u % 2, :], rs4[:, u % 2, :], AF.Exp, [f"rs4{u % 2}"], [f"rs4{u % 2}"], scale=-0.5)
                P.tt("pool", osl[b][:, hh, :], osl[b][:, hh, :], rs4[:, u % 2, :], ALU.mult,
                     [f"osl{b}_{hh}", f"rs4{u % 2}"], [f"osl{b}_{hh}"])

            def tail_c(u):
                t, hh = divmod(u, NH)
                b = t % 2
                P.stt("dve", gal[b][:, hh, :], osl[b][:, hh, :], vcol("glan", l, 0), sgl[b][:, hh, :],
                      ALU.mult, ALU.mult, [f"osl{b}_{hh}", f"sgl{b}", "vecs"], [f"gal{b}_{hh}"])
                if hh == NH - 1:
                    cols = slice(t * TT, (t + 1) * TT)
                    P.dma("sp", GA.rearrange("(h p) t -> p h t", p=128)[:, :, cols], gal[b][:],
                          [f"gal{b}_{h_}" for h_ in range(NH)], ["GA"], f"gal{b}")

            NU = NT * NH
            tstate = [0]

            def tail_step():
                n = tstate[0]
                if n - 2 >= 0 and n - 2 < NU:
                    tail_c(n - 2)
                if n - 1 >= 0 and n - 1 < NU:
                    tail_b(n - 1)
                if n < NU:
                    tail_a(n)
                tstate[0] += 1

            tail_loads(0)
            frl = [P.alloc(S, [128, 4, KTW + 1], BF16) for _ in range(2)]
            frh = [P.alloc(S, [128, 4, KTW], BF16) for _ in range(2)]
            qsb = [P.alloc(S, [128, KTW], F32) for _ in range(2)]
            cny = P.alloc(S, [128, NLC], BF16)
            P.dma("sp", cny[:], cny_d, (), ["cny"], "ld5")
            P.dma("sp", tabs[0][:], cl_d[0], (), ["tab0"], "tab0")
            for sq_i in range(2):
                c0 = sq_i * LP
                P.dma("sp", abp[:], AB[c0:c0 + LP, :].rearrange("(lc p) c -> p lc c", p=128), ["AB"], ["abp"], "abp")
                for g in range(4):
                    pb, pk = nb()
                    n = 0
                    for lc in range(2):
                        for cs in range(2):
                            P.mm(pb[:, 0:LP], abp[:, lc, g * 256 + cs * 128:g * 256 + cs * 128 + 128],
                                 c256[:, lc, cs, :], n == 0, n == 3, ["abp", "c256"], [pk])
                            n += 1
                    P.copy("act" if g % 2 else "dve", frp[:, g, :], pb[:, 0:LP], [pk], ["frp"])
                    tail_step()
                P.dma("sp", FR.rearrange("(g p) t -> p g t", p=128)[:, :, c0:c0 + LP], frp[:], ["frp"], ["FR"], "frp")
            def akey(lc):
                return f"abs{(lc // AST) * AST}"

            for lc in range(NL2):
                for half in range(2):
                    cs_ = slice(half * 512, (half + 1) * 512)
                    pb, pk = nb()
                    P.mm(pb[:], jp[:, 0, :], abs_[:, NLC - 1 - lc, cs_], True, lc == 0, ["jp", akey(NLC - 1 - lc)], [pk])
                    if lc > 0:
                        P.mm(pb[:], jp[:, 1, :], abs_[:, NLC - lc, cs_], False, True, ["jp", akey(NLC - lc)], [pk])
                    v = pb[:].rearrange("p (g cs m) -> p g cs m", g=2, cs=2)
                    dst = abs_[:, lc, cs_].rearrange("p (g cs m) -> p g cs m", g=2, cs=2)
                    P.tt("dve", dst[:, :, 0, :], dst[:, :, 0, :], v[:, :, 0, :], ALU.add, [pk, akey(lc)], [akey(lc)])
                    P.tt("dve", dst[:, :, 1, :], dst[:, :, 1, :], v[:, :, 1, :], ALU.subtract, [pk, akey(lc)], [akey(lc)])
                if lc % 2 == 1:
                    tail_step()
            NKT2 = LS // (2 * KTW)
            for kt in range(NKT2):
                tb = tabs[kt % 2]
                tbk = f"tab{kt % 2}"
                if kt + 1 < NKT2:
                    P.dma("sp", tabs[(kt + 1) % 2][:], cl_d[kt + 1], (), [f"tab{(kt + 1) % 2}"], f"tab{(kt + 1) % 2}")
                lo, hi = frl[kt % 2], frh[kt % 2]
                lok, hik = f"frl{kt % 2}", f"frh{kt % 2}"
                for g in range(4):
                    pP, pPk = nb()
                    for lc in range(NL2):
                        P.mm(pP[:, 0:KTW], abs_[:, lc, g * 256:g * 256 + 128], tb[:, lc, 0, :], lc == 0, False,
                             [akey(lc), tbk], [pPk])
                    P.mm(pP[:, 0:KTW], abs_[0:1, NL2, g * 256:g * 256 + 128], cpk[0:1, kt * KTW:(kt + 1) * KTW], False, True,
                         [akey(NL2), "cpk"], [pPk])
                    pQ, pQk = nb()
                    for lc in range(NL2):
                        P.mm(pQ[:, 0:KTW], abs_[:, lc, g * 256 + 128:g * 256 + 256], tb[:, lc, 1, :], lc == 0,
                             lc == NL2 - 1, [akey(lc), tbk], [pQk])
                    q = qsb[g % 2]
                    qk = f"qsb{g % 2}"
                    P.copy("act", q[:], pQ[:, 0:KTW], [pQk], [qk])
                    P.tt("dve", lo[:, g, 0:KTW], pP[:, 0:KTW], q[:], ALU.add, [pPk, qk], [lok + f"_{g}"])
                    if kt == 0:
                        P.tt("dve", hi[:, g, KTW - 2::-1], pP[:, 1:KTW], q[:, 1:KTW], ALU.subtract, [pPk, qk], [hik + f"_{g}"])
                    else:
                        P.tt("dve", hi[:, g, ::-1], pP[:, 0:KTW], q[:], ALU.subtract, [pPk, qk], [hik + f"_{g}"])
                    tail_step()
                k0 = kt * KTW
                nlo = KTW
                if kt == NKT2 - 1:
                    pN, pNk = nb()
                    for g in range(4):
                        for lc in range(NL2):
                            P.mm(pN[:, g:g + 1], abs_[:, lc, g * 256:g * 256 + 128], cny[:, lc:lc + 1], lc == 0,
                                 False, [akey(lc), "cny"], [pNk])
                        P.mm(pN[:, g:g + 1], abs_[0:1, NL2, g * 256:g * 256 + 128], cpk[0:1, 0:1], False, True,
                             [akey(NL2), "cpk"], [pNk])
                    P.copy("act", lo[:, :, KTW:KTW + 1], pN[:, 0:4].rearrange("p (g o) -> p g o", o=1), [pNk], [lok + "_n"])
                    nlo = KTW + 1
                FRv = FR.rearrange("(g p) t -> p g t", p=128)
                P.dma("sp", FRv[:, :, 2 * LP + k0:2 * LP + k0 + nlo], lo[:, :, 0:nlo],
                      [lok + f"_{g}" for g in range(4)] + ([lok + "_n"] if nlo > KTW else []), ["FR"], lok)
                if kt == 0:
                    P.dma("sp", FRv[:, :, 2 * LP + LS - (KTW - 1):2 * LP + LS], hi[:, :, 0:KTW - 1],
                          [hik + f"_{g}" for g in range(4)], ["FR"], hik)
                else:
                    P.dma("sp", FRv[:, :, 2 * LP + LS - k0 - (KTW - 1):2 * LP + LS - k0 + 1], hi[:, :, 0:KTW],
                          [hik + f"_{g}" for g in range(4)], ["FR"], hik)
            while tstate[0] < NU + 2:
                tail_step()
            P.flush()

        with contextlib.ExitStack() as S:
            wslot[0] = 0
            wmg = P.alloc(S, [128, NCH, 3 * D], BF16)
            wao = P.alloc(S, [128, 4, D], BF16)
            wbo = P.alloc(S, [128, 4, D], BF16)
            wco = P.alloc(S, [128, 4, D], BF16)
            woo = P.alloc(S, [128, NCH, D], BF16)
            P.dma("pool", wao[:], w_a_out[l].rearrange("(kc p) n -> p kc n", p=128), (), ["wao"], next_w())
            P.dma("pool", wbo[:], w_b_out[l].rearrange("(kc p) n -> p kc n", p=128), (), ["wbo"], next_w())
            P.dma("pool", wco[:], w_c_out[l].rearrange("(kc p) n -> p kc n", p=128), (), ["wco"], next_w())
            mgk = load_w(wmg, w_in[l][:, OFF_MG:PIN], NCH, 3 * D, "wmg")
            P.dma("pool", woo[:], w_o[l].rearrange("(kc p) n -> p kc n", p=128), (), ["woo"], next_w())
            xtb = [P.alloc(S, [128, NCH, TT], F32) for _ in range(2)]
            hb2 = [P.alloc(S, [128, NCH, TT], BF16) for _ in range(2)]
            gatb = [P.alloc(S, [128, 4, TT], BF16) for _ in range(2)]
            ybtb = [P.alloc(S, [128, 4, TT], BF16) for _ in range(2)]
            frtb = [P.alloc(S, [128, 4, TT], BF16) for _ in range(2)]
            rs = P.alloc(S, [128, TT], F32)
            tmp = P.alloc(S, [128, 2, TT], F32)
            gsb = [P.alloc(S, [128, TT], BF16) for _ in range(3)]
            tb3 = [P.alloc(S, [128, TT], F32) for _ in range(3)]
            Y = P.alloc(S, [128, NCH, TT], BF16)
            src = xsrc(l)

            def b_loads(t):
                b = t % 2
                cols = slice(t * TT, (t + 1) * TT)
                P.dma("sp", hb2[b][:], xview(H1, t), (), [f"h{b}_{c}" for c in range(NCH)], f"hld{b}")
                P.dma("sp", xtb[b][:], xview(src, t), (), [f"xt{b}_{c}" for c in range(NCH)], f"xt{b}")
                P.dma("sp", gatb[b][:], GA.rearrange("(c p) t -> p c t", p=128)[:, :, cols], ["GA"], [f"gat{b}"], f"gat{b}")
                P.dma("sp", ybtb[b][:], YBIN.rearrange("(c p) t -> p c t", p=128)[:, :, cols], ["YBIN"], [f"ybt{b}"], f"ybt{b}")
                P.dma("sp", frtb[b][:], FR.rearrange("(c p) t -> p c t", p=128)[:, :, cols], ["FR"], [f"frt{b}"], f"frt{b}")

            def b_prologue(t, parts=(0, 1, 2, 3)):
                b = t % 2
                norm_parts(parts, xtb[b], [f"xt{b}_{c}" for c in range(NCH)], hb2[b], [f"h{b}_{c}" for c in range(NCH)],
                           rs, "rs", tmp, "tmp", a1, f"a1{l}", 0, l, ci_of(t), sq_act=(t == 0))

            b_loads(0)
            for t in range(NT):
                ci = ci_of(t)
                b = t % 2
                xt, h, gat, ybt, frt = xtb[b], hb2[b], gatb[b], ybtb[b], frtb[b]
                xk = [f"xt{b}_{c}" for c in range(NCH)]
                hkeys = [f"h{b}_{c}" for c in range(NCH)]
                if t + 1 < NT:
                    b_loads(t + 1)
                for j in range(NCH):
                    js = slice(j * 128, (j + 1) * 128)
                    ybanks = []
                    for (wt, wkk, it, itk) in ((wao, "wao", gat, f"gat{b}"), (wbo, "wbo", ybt, f"ybt{b}"),
                                               (wco, "wco", frt, f"frt{b}")):
                        pb, pk = nb()
                        for kc in range(4):
                            P.mm(pb[:], wt[:, kc, js], it[:, kc, :], kc == 0, kc == 3, [wkk, itk], [pk])
                        ybanks.append((pb, pk))
                    for bb in range(3):
                        pb, pk = nb()
                        c0 = bb * D + j * 128
                        for kc in range(NCH):
                            P.mm(pb[:], wmg[:, kc, c0:c0 + 128], h[:, kc, :], kc == 0, kc == NCH - 1,
                                 [mgk[c0 // 512], hkeys[kc]], [pk])
                        P.act(gsb[bb][:], pb[:], AF.Sigmoid, [pk], [f"gsb{bb}"])
                        P.tt("dve", tb3[bb][:], ybanks[bb][0][:], gsb[bb][:], ALU.mult, [ybanks[bb][1], f"gsb{bb}"],
                             [f"tb3{bb}"])
                    P.tt("pool", tb3[0][:], tb3[0][:], tb3[1][:], ALU.add, ["tb30", "tb31"], ["tb30"])
                    P.tt("pool", Y[:, j, :], tb3[0][:], tb3[2][:], ALU.add, ["tb30", "tb32"], [f"Y{j}"])
                for j in range(NCH):
                    js = slice(j * 128, (j + 1) * 128)
                    pb, pk = nb()
                    for kc in range(NCH):
                        P.mm(pb[:], woo[:, kc, js], Y[:, kc, :], kc == 0, kc == NCH - 1, ["woo", f"Y{kc}"], [pk])
                    P.stt("dve", xt[:, j, :], pb[:], modc(l, 2, j, ci), xt[:, j, :], ALU.mult, ALU.add,
                          [pk, xk[j], f"mod{l}"], [xk[j]])
                P.dma("sp", xview(xs, t), xt[:], xk, ["xs"], f"xto{b}")
            P.flush()

        with contextlib.ExitStack() as S:
            wslot[0] = 0
            wup = P.alloc(S, [128, NCH, 2 * DFF], BF16)
            upk = load_w(wup, w_up[l], NCH, 2 * DFF, "wup")
            wdn = P.alloc(S, [128, NFF, D], BF16)
            dnk = load_w(wdn, w_down[l], NFF, D, "wdn", piece=256)
            xt = P.alloc(S, [128, NCH, TT], F32)
            hb2 = [P.alloc(S, [128, NCH, TT], BF16) for _ in range(2)]
            rs = P.alloc(S, [128, TT], F32)
            hid = P.alloc(S, [128, NFF, TT], BF16)
            sgb = P.alloc(S, [128, 2, TT], BF16)
            xe = P.alloc(S, [128, 2, TT], F32)
            xo = P.alloc(S, [128, 2, TT], F32)
            xkeys = [f"xt{c}" for c in range(NCH)]
            xsv = xs.rearrange("(c p) t -> p c t", p=128)

            def c_prologue(t, parts=(0, 1, 2, 3)):
                b = t % 2
                norm_parts(parts, xt, xkeys, hb2[b], [f"h{b}_{c}" for c in range(NCH)], rs, "rs", xo, "xo",
                           a2, f"a2{l}", 3, l, ci_of(t), sq_act=(t == 0))

            P.dma("sp", xt[:], xview(xs, 0), (), xkeys, "xt")
            c_prologue(0)
            for t in range(NT):
                ci = ci_of(t)
                b = t % 2
                h = hb2[b]
                hkeys = [f"h{b}_{c}" for c in range(NCH)]
                if t + 1 < NT:
                    P.dma("sp", xt[:], xview(xs, t + 1), (), xkeys, "xt")
                for j in range(NFF):
                    pg, pgk = nb()
                    c0 = j * 128
                    for kc in range(NCH):
                        P.mm(pg[:], wup[:, kc, c0:c0 + 128], h[:, kc, :], kc == 0, kc == NCH - 1,
                             [upk[c0 // 512], hkeys[kc]], [pgk])
                    pu, puk = nb()
                    c1 = DFF + j * 128
                    for kc in range(NCH):
                        P.mm(pu[:], wup[:, kc, c1:c1 + 128], h[:, kc, :], kc == 0, kc == NCH - 1,
                             [upk[c1 // 512], hkeys[kc]], [puk])
                    P.act(sgb[:, j % 2, :], pg[:], AF.Silu, [pgk], [f"sg{j % 2}"])
                    P.tt("dve", hid[:, j, :], pu[:], sgb[:, j % 2, :], ALU.mult, [puk, f"sg{j % 2}"], [f"hid{j}"])
                    if t + 1 < NT and j in (6, 8, 10, 12):
                        c_prologue(t + 1, ((j - 6) // 2,))
                cols = slice(t * TT, (t + 1) * TT)
                P.dma("sp", xe[:, 0, :], xsv[:, 0, cols], (), ["xe0"], "xe0")
                for j in range(NCH):
                    js = slice(j * 128, (j + 1) * 128)
                    pb, pk = nb()
                    for kc in range(NFF):
                        P.mm(pb[:], wdn[:, kc, js], hid[:, kc, :], kc == 0, kc == NFF - 1,
                             [dnk[(j * 128) // 256], f"hid{kc}"], [pk])
                    if j + 1 < NCH:
                        P.dma("sp", xe[:, (j + 1) % 2, :], xsv[:, j + 1, cols], (), [f"xe{(j + 1) % 2}"], f"xe{(j + 1) % 2}")
                    P.stt("dve", xo[:, j % 2, :], pb[:], modc(l, 5, j, ci), xe[:, j % 2, :], ALU.mult, ALU.add,
                          [pk, f"xe{j % 2}", f"mod{l}"], [f"xo{j % 2}"])
                    P.dma("sp", xsv[:, j, cols], xo[:, j % 2, :], [f"xo{j % 2}"], (), f"xo{j % 2}")
            P.flush()

    with contextlib.ExitStack() as S:
        xb = [P.alloc(S, [128, NCH, TT], F32) for _ in range(2)]
        sqb = [P.alloc(S, [128, NCH, TT], BF16) for _ in range(2)]
        rsb = [P.alloc(S, [128, TT], F32) for _ in range(2)]
        o = voff["nf"]
        P.dma("sp", xb[0][:], xview(xs, 0), (), [f"x0_{c}" for c in range(NCH)], "xt0")
        for t in range(NT):
            b = t % 2
            xk = [f"x{b}_{c}" for c in range(NCH)]
            if t + 1 < NT:
                P.dma("sp", xb[1 - b][:], xview(xs, t + 1), (), [f"x{1 - b}_{c}" for c in range(NCH)], f"xt{1 - b}")
            P.act(sqb[b][:], xb[b][:], AF.Square, xk, [f"sq{b}"])
            pb, pk = nb()
            for c in range(NCH):
                P.mm(pb[:], ones_bf[:], sqb[b][:, c, :], c == 0, c == NCH - 1, ["ones", f"sq{b}"], [pk])
            P.act(rsb[b][:], pb[:], AF.Ln, [pk], [f"rs{b}"], bias=EPS, scale=1.0 / D)
            P.act(rsb[b][:], rsb[b][:], AF.Exp, [f"rs{b}"], [f"rs{b}"], scale=-0.5)
            for c in range(NCH):
                P.stt("dve", xb[b][:, c, :], xb[b][:, c, :], vecs[:, o + c:o + c + 1], rsb[b][:], ALU.mult, ALU.mult,
                      [xk[c], f"rs{b}", "vecs"], [xk[c]])
            P.dma("sp", xview(y_out, t), xb[b][:], xk, ["y"], f"xto{b}")
        P.flush()
    P.outer.close()
    return nc, P


_CACHE = {}


def run(inputs, depth, LS, n_cores, seq_of_core, sample_owner=None):
    f32 = lambda a: np.ascontiguousarray(np.asarray(a, dtype=np.float32))
    key = (depth, LS)
    if key not in _CACHE:
        _CACHE[key] = (build(depth, LS), _consts(LS))
    (nc, P), (cmk, amk, cnk, c256k, clk, cnyk, jpk, cpkk, identk) = _CACHE[key]
    xp, xsm = f32(inputs["x_prompt"]), f32(inputs["x_sample"])
    vec = _pack_vecs(depth, f32(inputs["b_ada"]), f32(inputs["norm1"]), f32(inputs["norm2"]),
                     f32(inputs["gla_norm"]), f32(inputs["conv_w"]), f32(inputs["norm_f"]))
    shared = {k: f32(inputs[k]) for k in ("w_ada", "w_in", "w_gk_f", "w_gk_b", "b_gk_f", "b_gk_b", "w_a_out",
                                           "w_b_out", "w_c_out", "w_o", "w_up", "w_down")}
    shared.update({"vecs": vec, "cmask": cmk, "amask": amk, "cn": cnk, "c256": c256k, "cl": clk, "cny": cnyk, "jperm": jpk, "cpk": cpkk, "ident": identk})
    in_maps = []
    voff, _ = _vec_layout(depth)
    for c in range(n_cores):
        s = seq_of_core(c)
        dup = (sample_owner is not None) and (not sample_owner(c))
        xs_c = np.zeros_like(xsm[s]) if dup else xsm[s]
        xall = np.concatenate([xp[2 * c], xp[2 * c + 1], xs_c], axis=0)
        c_s = np.zeros_like(f32(inputs["c"])[s]) if dup else f32(inputs["c"])[s]
        cnd = np.stack([_fm(f32(inputs["c_ctx"])), _fm(c_s)], axis=-1)
        m = dict(shared)
        if dup:
            v2 = vec.copy()
            for l in range(depth):
                o = voff[("bada", l)]
                v2[:, o + 1:o + 96:2] = 0.0
            m["vecs"] = v2
        m["x0"] = np.ascontiguousarray(xall.T)
        m["state"] = np.zeros_like(f32(inputs["state_gla"])[s]) if dup else f32(inputs["state_gla"])[s]
        m["cond"] = np.ascontiguousarray(cnd)
        in_maps.append(m)
    res = run_bass_kernel_spmd(nc, in_maps, core_ids=list(range(n_cores)))
    return res.results


def kernel(**inputs):
    depth, LS, n_cores = 4, 4096, 8
    res = run(inputs, depth, LS, n_cores, lambda c: c // 2, sample_owner=lambda c: c % 2 == 0)
    B = 2 * n_cores
    y_prompt = np.empty((B, LP, D), np.float32)
    y_sample = np.empty((n_cores // 2, LS, D), np.float32)
    ns = np.empty((B, depth, 2, NH, DK, DV), np.float32)
    for c in range(n_cores):
        y = np.asarray(res[c]["y"], np.float32)
        y_prompt[2 * c] = y[:, 0:LP].T
        y_prompt[2 * c + 1] = y[:, LP:2 * LP].T
        if c % 2 == 0:
            y_sample[c // 2] = y[:, 2 * LP:].T
        nsc = np.asarray(res[c]["ns"], np.float32)
        ns[2 * c] = nsc[0]
        ns[2 * c + 1] = nsc[1]
    return (y_prompt, y_sample, ns)
```

```python
import contextlib
import math
import numpy as np
import ml_dtypes
import concourse.bass as bass
import concourse.mybir as mybir
from concourse.bass_utils import run_bass_kernel_spmd

F32 = mybir.dt.float32
BF16 = mybir.dt.bfloat16
AF = mybir.ActivationFunctionType
ALU = mybir.AluOpType

D = 1024
NCH = 8
TT = 512
LP = 256
NH, DK, DV = 4, 64, 128
QKW, VW = 256, 512
SCW, FNW, FNG = 512, 512, 4
DFF = 2816
NFF = 22
PIN = 6688
OFF_Q, OFF_K, OFF_V, OFF_OG, OFF_GKF, OFF_GKB = 0, 256, 512, 1024, 1536, 1552
OFF_SB, OFF_SC, OFF_SX, OFF_FX, OFF_MG = 1568, 2080, 2592, 3104, 3616
EPS = 1e-6
GRID_W = 64
GCH = 128

ENGS = ("pe", "act", "dve", "pool", "sp")
HND = {"pe": "tensor", "act": "scalar", "dve": "vector", "pool": "gpsimd", "sp": "sync"}


class Op:
    __slots__ = ("eng", "fn", "reads", "writes", "dma", "signal", "sigval", "key")

    def __init__(self, eng, fn, reads, writes, dma):
        self.eng, self.fn, self.reads, self.writes, self.dma = eng, fn, reads, writes, dma
        self.signal = False
        self.sigval = 0
        self.key = ("dma", dma) if dma is not None else ("eng", eng)


class Prog:
    SAME_ENGINE_SYNC = True

    def __init__(self, nc):
        self.nc = nc
        self.ops = []
        self.outer = contextlib.ExitStack()
        self.sems = {}
        self.sigcount = {}
        self.n = 0
        self.total = {e: 0 for e in ENGS}

    def sem(self, key):
        if key not in self.sems:
            nm = "s" + str(len(self.sems))
            self.sems[key] = self.outer.enter_context(self.nc.semaphore(nm))
            self.sigcount[key] = 0
        return self.sems[key]

    def alloc(self, stack, shape, dtype, psum=False):
        self.n += 1
        f = self.nc.psum_tensor if psum else self.nc.sbuf_tensor
        return stack.enter_context(f(f"t{self.n}", list(shape), dtype))

    def op(self, eng, fn, reads=(), writes=(), dma=None):
        self.ops.append(Op(eng, fn, tuple(reads), tuple(writes), dma))

    def mm(self, out, lhsT, rhs, start, stop, reads, writes):
        self.op("pe", lambda e: e.matmul(out, lhsT, rhs, start=start, stop=stop, skip_group_check=True),
                reads, writes)

    def act(self, out, in_, func, reads, writes, bias=0.0, scale=1.0):
        self.op("act", lambda e: e.activation(out=out, in_=in_, func=func, bias=bias, scale=scale), reads, writes)

    def tt(self, eng, out, in0, in1, op, reads, writes):
        self.op(eng, lambda e: e.tensor_tensor(out=out, in0=in0, in1=in1, op=op), reads, writes)

    def ts(self, eng, out, in0, s1, op0, reads, writes, s2=None, op1=None):
        if op1 is None:
            self.op(eng, lambda e: e.tensor_scalar(out=out, in0=in0, scalar1=s1, scalar2=None, op0=op0), reads, writes)
        else:
            self.op(eng, lambda e: e.tensor_scalar(out=out, in0=in0, scalar1=s1, scalar2=s2, op0=op0, op1=op1),
                    reads, writes)

    def stt(self, eng, out, in0, scalar, in1, op0, op1, reads, writes):
        self.op(eng, lambda e: e.scalar_tensor_tensor(out=out, in0=in0, scalar=scalar, in1=in1, op0=op0, op1=op1),
                reads, writes)

    def copy(self, eng, out, in_, reads, writes):
        if eng == "act":
            self.op("act", lambda e: e.activation(out=out, in_=in_, func=AF.Copy), reads, writes)
        else:
            self.op(eng, lambda e: e.tensor_copy(out=out, in_=in_), reads, writes)

    def memset(self, eng, ap, val, writes):
        self.op(eng, lambda e: e.memset(ap, val), (), writes)

    def dma(self, q, out, in_, reads, writes, slot):
        self.op(q, lambda e: e.dma_start(out=out, in_=in_), reads, writes, dma=slot)

    def flush(self):
        nc = self.nc
        ops = self.ops
        self.ops = []
        last_writer, readers = {}, {}
        pos_ctr = {}
        pos = []
        for o in ops:
            pos_ctr[o.key] = pos_ctr.get(o.key, 0) + 1
            pos.append(pos_ctr[o.key])
        seen = {e: {} for e in ENGS}
        deps_all = []
        for i, o in enumerate(ops):
            deps = set()
            for r in o.reads:
                j = last_writer.get(r)
                if j is not None:
                    deps.add(j)
            for w in o.writes:
                j = last_writer.get(w)
                if j is not None:
                    deps.add(j)
                deps.update(readers.get(w, ()))
            for r in o.reads:
                readers.setdefault(r, []).append(i)
            for w in o.writes:
                last_writer[w] = i
                readers[w] = []
            need = {}
            for j in deps:
                if j == i:
                    continue
                pj = ops[j]
                if pj.dma is None and pj.eng == o.eng and (o.eng == "pe" or not self.SAME_ENGINE_SYNC):
                    continue
                if need.get(pj.key, (0, 0))[0] < pos[j]:
                    need[pj.key] = (pos[j], j)
            final = []
            for key, (p, j) in need.items():
                if seen[o.eng].get(key, 0) >= p:
                    continue
                seen[o.eng][key] = p
                final.append(j)
                ops[j].signal = True
            deps_all.append(final)
        last_of = {}
        for i, o in enumerate(ops):
            if o.dma is None:
                last_of[o.eng] = i
        for i in last_of.values():
            ops[i].signal = True
        for o in ops:
            s = self.sem(o.key)
            if o.dma is not None:
                self.sigcount[o.key] += 16
                o.sigval = self.sigcount[o.key]
                o.signal = True
            elif o.signal:
                self.sigcount[o.key] += 1
                o.sigval = self.sigcount[o.key]
        per_eng = {e: [] for e in ENGS}
        for i, o in enumerate(ops):
            per_eng[o.eng].append(i)
            self.total[o.eng] += 1
        prev = getattr(self, "_barrier_prev", {})
        barrier = {k: v for k, v in self.sigcount.items() if v > prev.get(k, 0)}
        self._barrier_prev = dict(self.sigcount)
        sems = self.sems

        def make(e):
            def body(eng):
                for i in per_eng[e]:
                    o = ops[i]
                    for j in deps_all[i]:
                        eng.wait_ge(sems[ops[j].key], ops[j].sigval)
                    ins = o.fn(eng)
                    if o.signal:
                        ins.then_inc(sems[o.key], 16 if o.dma is not None else 1)
                for key, val in barrier.items():
                    if val > 0 and key != ("eng", e):
                        eng.wait_ge(sems[key], val)
            return body

        with nc.Block() as block:
            for e in ENGS:
                getattr(block, HND[e])(make(e))


def _bf(a):
    return np.ascontiguousarray(a.astype(np.float32)).astype(ml_dtypes.bfloat16)


def _consts(LS):
    s = np.arange(128)[:, None]
    t = np.arange(128)[None, :]
    same = (s // GCH) == (t // GCH)
    m_inc_f = (same & (s <= t)).astype(np.float32)
    m_str_f = (same & (s > t)).astype(np.float32)
    m_inc_b = (same & (s >= t)).astype(np.float32)
    m_str_b = (same & (s < t)).astype(np.float32)
    cm = np.stack([m_inc_f, m_str_f, m_inc_b, m_str_b], axis=1)
    am = np.stack([np.repeat(m_inc_f[:, None, :], 4, 1), np.repeat(m_inc_b[:, None, :], 4, 1)], axis=1)
    n = np.arange(128)
    ang = 2 * np.pi * np.outer(n, n) / 128.0
    cn = np.concatenate([np.cos(ang), np.sin(ang)], axis=1) / np.sqrt(128.0)

    def tab(L):
        l = np.arange(L, dtype=np.int64)
        kl = np.outer(l, l) % L
        a = 2 * np.pi * kl / float(L)
        return np.stack([np.cos(a), -np.sin(a)], axis=0) / np.sqrt(float(L))

    c256 = tab(LP)
    c256s = c256.reshape(2, 2, 128, LP).transpose(2, 1, 0, 3)
    cl = tab(LS)[:, :LS // 2, :LS // 2]
    cl = cl.reshape(2, LS // 256, 128, LS // 512, 256).transpose(3, 2, 1, 0, 4)
    lidx = np.arange(LS).reshape(LS // 128, 128).T
    cny = np.where(lidx % 2 == 0, 1.0, -1.0) / np.sqrt(float(LS))
    pp = np.arange(128)
    perm = ((pp[:, None] + pp[None, :]) == 128).astype(np.float32)
    e00 = np.zeros((128, 128), np.float32)
    e00[0, 0] = 1.0
    jperm = np.stack([perm, e00], axis=1)
    kk = np.arange(LS // 2)
    cpk = (np.where(kk % 2 == 0, 1.0, -1.0) / np.sqrt(float(LS)))[None, :]
    return (_bf(cm), am.astype(np.float32), _bf(cn), _bf(np.ascontiguousarray(c256s)), _bf(cl), _bf(cny),
            _bf(jperm), _bf(cpk), _bf(np.eye(128)))


class VecPack:
    def __init__(self):
        self.cols = []
        self.off = {}
        self.n = 0

    def add(self, name, arr):
        arr = np.asarray(arr, np.float32).reshape(128, -1)
        self.off[name] = self.n
        self.cols.append(arr)
        self.n += arr.shape[1]

    def build(self):
        return np.ascontiguousarray(np.concatenate(self.cols, axis=1))


def _fm(v):
    return np.asarray(v, np.float32).reshape(-1, 128).T


def _vec_layout(depth):
    off, n = {}, 0
    for l in range(depth):
        for nm, k in (("bada", 96), ("n1", 16), ("n2", 16), ("glan", 1), ("convw", 12)):
            off[(nm, l)] = n
            n += k
    off["nf"] = n
    n += 8
    return off, n


def _pack_vecs(depth, b_ada, norm1, norm2, gla_norm, conv_w, norm_f):
    off, n = _vec_layout(depth)
    out = np.zeros((128, n), np.float32)
    for l in range(depth):
        o = off[("bada", l)]
        out[:, o:o + 96] = np.repeat(_fm(b_ada[l]), 2, axis=1)
        o = off[("n1", l)]
        out[:, o:o + 16] = np.repeat(_fm(norm1[l]), 2, axis=1)
        o = off[("n2", l)]
        out[:, o:o + 16] = np.repeat(_fm(norm2[l]), 2, axis=1)
        o = off[("glan", l)]
        out[:, o:o + 1] = np.asarray(gla_norm[l], np.float32).reshape(128, 1)
        o = off[("convw", l)]
        for i in range(3):
            out[:, o + i * 4:o + i * 4 + 4] = _fm(conv_w[l, i])
    o = off["nf"]
    out[:, o:o + 8] = _fm(norm_f)
    return out


def build(depth, LS):
    T = 2 * LP + LS
    NT = T // TT
    nc = bass.Bass("TRN2", target_bir_lowering=False)
    voff, nvec = _vec_layout(depth)

    def din(name, shape, dt=F32):
        return nc.dram_tensor(name, list(shape), dt, kind="ExternalInput").ap()

    def dscr(name, shape, dt):
        return nc.dram_tensor(name, list(shape), dt).ap()

    x0 = din("x0", [D, T])
    state = din("state", [depth, 2, NH, DK, DV])
    cond = din("cond", [128, NCH, 2])
    vecs_d = din("vecs", [128, nvec])
    w_ada = din("w_ada", [depth, D, 6 * D])
    w_in = din("w_in", [depth, D, PIN])
    w_gk = [din("w_gk_f", [depth, 16, QKW]), din("w_gk_b", [depth, 16, QKW])]
    b_gk = [din("b_gk_f", [depth, QKW]), din("b_gk_b", [depth, QKW])]
    w_a_out = din("w_a_out", [depth, VW, D])
    w_b_out = din("w_b_out", [depth, SCW, D])
    w_c_out = din("w_c_out", [depth, FNW, D])
    w_o = din("w_o", [depth, D, D])
    w_up = din("w_up", [depth, D, 2 * DFF])
    w_down = din("w_down", [depth, DFF, D])
    cm_d = din("cmask", [128, 4, 128], BF16)
    am_d = din("amask", [128, 2, 4, 128])
    cn_d = din("cn", [128, 256], BF16)
    c256_d = din("c256", [128, 2, 2, LP], BF16)
    cl_d = din("cl", [LS // 512, 128, LS // 256, 2, 256], BF16)
    jp_d = din("jperm", [128, 2, 128], BF16)
    cpk_d = din("cpk", [1, LS // 2], BF16)
    id_d = din("ident", [128, 128], BF16)
    cny_d = din("cny", [128, LS // 128], BF16)
    y_out = nc.dram_tensor("y", [D, T], F32, kind="ExternalOutput").ap()
    ns_out = nc.dram_tensor("ns", [2, depth, 2, NH, DK, DV], F32, kind="ExternalOutput").ap()

    xs = dscr("xs", [D, T], F32)
    QT = dscr("QT", [NH, DK, T], BF16)
    KT = dscr("KT", [NH, DK, T], BF16)
    KTM = dscr("KTM", [T, QKW], BF16)
    VTM = dscr("VTM", [T, VW], BF16)
    GK = [dscr("GKF", [16, T], BF16), dscr("GKB", [16, T], BF16)]
    SOG = dscr("SOG", [VW, T], BF16)
    YBIN = dscr("YBIN", [SCW, T], BF16)
    AB = dscr("AB", [T, 1024], BF16)
    OF = dscr("OF", [VW, T], F32)
    OS = dscr("OS", [VW, T], F32)
    H1 = dscr("H1", [D, T], BF16)
    GA = dscr("GA", [VW, T], BF16)
    FR = dscr("FR", [FNW, T], BF16)

    P = Prog(nc)
    G = P.outer
    banks = [P.alloc(G, [128, 512], F32, psum=True) for _ in range(8)]
    bank_ctr = [0]

    nb_mod = [8]

    def nb():
        i = bank_ctr[0] % nb_mod[0]
        bank_ctr[0] += 1
        return banks[i], f"ps{i}"

    vecs = P.alloc(G, [128, nvec], F32)
    ones_bf = P.alloc(G, [128, 128], BF16)
    ident_bf = P.alloc(G, [128, 128], BF16)
    mod = P.alloc(G, [128, depth, 96], F32)
    a1 = P.alloc(G, [128, depth, 16], F32)
    a2 = P.alloc(G, [128, depth, 16], F32)
    condt = P.alloc(G, [128, NCH, 2], F32)
    scb = P.alloc(G, [128, NCH, 2], BF16)

    def vcol(name, l, i):
        o = voff[(name, l)] + i
        return vecs[:, o:o + 1]

    def modc(l, which, c, ci):
        o = (which * 8 + c) * 2 + ci
        return mod[:, l, o:o + 1]

    wv = w_ada.rearrange("l (kc p) n -> l p kc n", p=128)

    def ada_gen(l, wa, pb, pk):
        def issue(pc):
            wk = f"wa{pc % 3}"
            P.dma("pool", wa[pc % 3][:], wv[l, :, :, pc * 512:(pc + 1) * 512], (), [wk], wk)
        issue(0)
        for pc in range(12):
            if pc + 1 < 12:
                issue(pc + 1)
            wt = wa[pc % 3]
            wk = f"wa{pc % 3}"
            for jj in range(4):
                j = pc * 4 + jj
                for kc in range(NCH):
                    P.mm(pb[:, 2 * j:2 * j + 2], wt[:, kc, jj * 128:(jj + 1) * 128], scb[:, kc, :],
                         kc == 0, kc == NCH - 1, [wk, "scb"], [pk])
            yield
        o = voff[("bada", l)]
        P.tt("dve", mod[:, l, :], pb[:, 0:96], vecs[:, o:o + 96], ALU.add, [pk, "vecs"], [f"mod{l}"])
        o1 = voff[("n1", l)]
        P.stt("dve", a1[:, l, :], mod[:, l, 16:32], 1.0, vecs[:, o1:o1 + 16], ALU.add, ALU.mult,
              [f"mod{l}", "vecs"], [f"a1{l}"])
        o2 = voff[("n2", l)]
        P.stt("dve", a2[:, l, :], mod[:, l, 64:80], 1.0, vecs[:, o2:o2 + 16], ALU.add, ALU.mult,
              [f"mod{l}", "vecs"], [f"a2{l}"])
        yield

    with contextlib.ExitStack() as S:
        P.dma("sp", vecs[:], vecs_d, (), ["vecs"], "ld0")
        P.dma("sp", condt[:], cond, (), ["condt"], "ld5")
        P.dma("sp", ident_bf[:], id_d, (), ["ident"], "ld8")
        P.memset("pool", ones_bf[:], 1.0, ["ones"])
        P.act(scb[:], condt[:], AF.Silu, ["condt"], ["scb"])
        wa0 = [P.alloc(S, [128, NCH, 512], BF16) for _ in range(3)]
        pb0, pk0 = nb()
        for _ in ada_gen(0, wa0, pb0, pk0):
            pass
        P.flush()

    def ci_of(t):
        return 0 if t == 0 else 1

    def xsrc(l):
        return x0 if l == 0 else xs

    def xview(src, t):
        return src.rearrange("(c p) t -> p c t", p=128)[:, :, t * TT:(t + 1) * TT]

    def norm_parts(parts, xt, xkeys, hbuf, hkeys, rs, rsk, tmp, tmpk, avec, akey, shw, l, ci, sq_act=False):
        if 0 in parts:
            if sq_act:
                P.act(hbuf[:], xt[:], AF.Square, sorted(set(xkeys)), hkeys)
            else:
                for c in range(NCH):
                    P.tt("pool", hbuf[:, c, :], xt[:, c, :], xt[:, c, :], ALU.mult, [xkeys[c]], [hkeys[c]])
        if 1 in parts:
            pb, pk = nb()
            for c in range(NCH):
                P.mm(pb[:], ones_bf[:], hbuf[:, c, :], c == 0, c == NCH - 1, ["ones", hkeys[c]], [pk])
            P.act(rs[:], pb[:], AF.Ln, [pk], [rsk], bias=EPS, scale=1.0 / D)
            P.act(rs[:], rs[:], AF.Exp, [rsk], [rsk], scale=-0.5)
        for part in (2, 3):
            if part in parts:
                for c in range(4 * (part - 2), 4 * (part - 1)):
                    P.tt("dve", tmp[:, c % 2, :], xt[:, c, :], rs[:], ALU.mult, [xkeys[c], rsk], [tmpk + str(c % 2)])
                    P.ts("dve", hbuf[:, c, :], tmp[:, c % 2, :], avec[:, l, c * 2 + ci:c * 2 + ci + 1], ALU.mult,
                         [tmpk + str(c % 2), akey, f"mod{l}"], [hkeys[c]], s2=modc(l, shw, c, ci), op1=ALU.add)

    wslot = [0]

    def next_w():
        wslot[0] += 1
        return f"w{wslot[0] - 1}"

    def load_w(wt, src2d, kch, ncols, name, piece=512):
        v = src2d.rearrange("(kc p) n -> p kc n", p=128)
        keys = []
        for i, c0 in enumerate(range(0, ncols, piece)):
            c1 = min(ncols, c0 + piece)
            k = f"{name}{i}"
            P.dma("pool", wt[:, :, c0:c1], v[:, :, c0:c1], (), [k], next_w())
            keys.append(k)
        return keys

    for l in range(depth):
        last = l == depth - 1
        with contextlib.ExitStack() as S:
            wslot[0] = 0
            NA = OFF_MG
            win = P.alloc(S, [128, NCH, NA], BF16)
            cn = P.alloc(S, [128, 256], BF16)
            P.dma("sp", cn[:], cn_d, (), ["cn"], "ld3")
            wkeys = load_w(win, w_in[l][:, 0:NA], NCH, NA, "win")

            def wk_of(c0, c1):
                return sorted({wkeys[c // 512] for c in (c0, c1 - 1)})

            xt = P.alloc(S, [128, NCH, TT], F32)
            hb = [P.alloc(S, [128, NCH, TT], BF16) for _ in range(2)]
            rs = P.alloc(S, [128, TT], F32)
            tmp = P.alloc(S, [128, 2, TT], F32)
            st_q = P.alloc(S, [128, 2, TT], BF16)
            st_k = P.alloc(S, [128, 2, TT], BF16)
            st_ktm = P.alloc(S, [128, 4, QKW], BF16)
            st_vtm = P.alloc(S, [128, 4, VW], BF16)
            st_gk = P.alloc(S, [32, TT], BF16)
            st_sog = P.alloc(S, [128, 4, TT], BF16)
            st_yb = P.alloc(S, [128, 4, TT], BF16)
            fxT = P.alloc(S, [128, 4, TT], BF16)
            st_ab = P.alloc(S, [128, 4, 1024], BF16)
            sbf = P.alloc(S, [128, TT], F32)
            scf = P.alloc(S, [128, TT], F32)
            uu = P.alloc(S, [128, TT], F32)
            cv = P.alloc(S, [128, TT], F32)
            src = xsrc(l)
            P.dma("sp", xt[:], xview(src, 0), (), ["xt"], "xt")
            agen = None
            if l + 1 < depth:
                waA = [P.alloc(S, [128, NCH, 512], BF16) for _ in range(3)]
                nb_mod[0] = 7
                agen = ada_gen(l + 1, waA, banks[7], "ps7")

            def ada_step():
                if agen is not None:
                    next(agen, None)

            for t in range(NT):
                ci = ci_of(t)
                h = hb[t % 2]
                hk = f"h{t % 2}"
                hkeys = [f"{hk}_{c}" for c in range(NCH)]
                if t == 0:
                    norm_parts((0, 1, 2, 3), xt, ["xt"] * NCH, h, hkeys, rs, "rs", tmp, "tmp", a1, f"a1{l}", 0, l, ci,
                               sq_act=True)
                    P.dma("sp", xview(H1, 0), h[:], hkeys, (), "hst0")
                if t + 1 < NT:
                    P.dma("sp", xt[:], xview(src, t + 1), (), ["xt"], "xt")
                cols = slice(t * TT, (t + 1) * TT)

                def fm_group(c0, m, evac):
                    pb, pk = nb()
                    for kc in range(NCH):
                        P.mm(pb[0:m, :], win[:, kc, c0:c0 + m], h[:, kc, :], kc == 0, kc == NCH - 1,
                             wk_of(c0, c0 + m) + [hkeys[kc]], [pk])
                    evac(pb, pk)

                for pr2 in range(2):
                    fm_group(OFF_Q + pr2 * 128, 128,
                             lambda pb, pk, pr2=pr2: P.act(st_q[:, pr2, :], pb[:], AF.Copy, [pk], [f"st_q{pr2}"],
                                                           scale=DK ** -0.5))
                for pr2 in range(2):
                    fm_group(OFF_K + pr2 * 128, 128,
                             lambda pb, pk, pr2=pr2: P.copy("dve", st_k[:, pr2, :], pb[:], [pk], [f"st_k{pr2}"]))
                P.dma("sp", QT[:, :, cols].rearrange("h d t -> (h d) t").rearrange("(pr p) t -> p pr t", p=128),
                      st_q[:], ["st_q0", "st_q1"], ["QT"], "st_q")
                P.dma("sp", KT[:, :, cols].rearrange("h d t -> (h d) t").rearrange("(pr p) t -> p pr t", p=128),
                      st_k[:], ["st_k0", "st_k1"], ["KT"], "st_k")
                ada_step()
                if t + 1 < NT:
                    norm_parts((0,), xt, ["xt"] * NCH, hb[(t + 1) % 2], [f"h{(t + 1) % 2}_{c}" for c in range(NCH)],
                               rs, "rs", tmp, "tmp", a1, f"a1{l}", 0, l, ci_of(t + 1))
                fm_group(OFF_GKF, 32, lambda pb, pk: P.copy("dve", st_gk[:], pb[0:32, :], [pk], ["st_gk"]))
                for dr in range(2):
                    P.dma("sp", GK[dr][:, cols], st_gk[16 * dr:16 * dr + 16, :], ["st_gk"], [f"GK{dr}"], f"st_gk{dr}")
                for c in range(4):
                    fm_group(OFF_OG + c * 128, 128,
                             lambda pb, pk, c=c: P.act(st_sog[:, c, :], pb[:], AF.Silu, [pk], [f"st_sog{c}"]))
                P.dma("sp", SOG.rearrange("(c p) t -> p c t", p=128)[:, :, cols], st_sog[:], [f"st_sog{i}" for i in range(4)], ["SOG"],
                      "st_sog")
                for sb_ in range(4):
                    tk = slice(sb_ * 128, (sb_ + 1) * 128)
                    pb, pk = nb()
                    for pr2 in range(2):
                        P.mm(pb[:, pr2 * 128:(pr2 + 1) * 128], st_k[:, pr2, tk], ident_bf[:], True, True,
                             [f"st_k{pr2}", "ident"], [pk])
                    P.copy("act", st_ktm[:, sb_, :], pb[:, 0:QKW], [pk], ["st_ktm"])
                    pb, pk = nb()
                    for kc in range(NCH):
                        P.mm(pb[:], h[:, kc, tk], win[:, kc, OFF_V:OFF_V + VW], kc == 0, kc == NCH - 1,
                             wk_of(OFF_V, OFF_V + VW) + [hkeys[kc]], [pk])
                    P.copy("dve", st_vtm[:, sb_, :], pb[:], [pk], ["st_vtm"])
                if t + 1 < NT:
                    norm_parts((1, 2, 3), xt, ["xt"] * NCH, hb[(t + 1) % 2], [f"h{(t + 1) % 2}_{c}" for c in range(NCH)],
                               rs, "rs", tmp, "tmp", a1, f"a1{l}", 0, l, ci_of(t + 1))
                    P.dma("sp", xview(H1, t + 1), hb[(t + 1) % 2][:], [f"h{(t + 1) % 2}_{c}" for c in range(NCH)], (),
                          f"hst{(t + 1) % 2}")
                P.dma("sp", KTM[cols, :].rearrange("(s p) c -> p s c", p=128), st_ktm[:], ["st_ktm"], ["KTM"], "st_ktm")
                P.dma("sp", VTM[cols, :].rearrange("(s p) c -> p s c", p=128), st_vtm[:], ["st_vtm"], ["VTM"], "st_vtm")
                R = TT // GRID_W if t > 0 else 2
                W = TT // R
                for c in range(4):
                    fm_group(OFF_SB + c * 128, 128, lambda pb, pk: P.copy("act", sbf[:], pb[:], [pk], ["sbf"]))
                    fm_group(OFF_SC + c * 128, 128, lambda pb, pk: P.copy("act", scf[:], pb[:], [pk], ["scf"]))
                    fm_group(OFF_SX + c * 128, 128,
                             lambda pb, pk: P.tt("dve", uu[:], pb[:], scf[:], ALU.mult, [pk, "scf"], ["uu"]))
                    u3 = uu[:].rearrange("p (r w) -> p r w", w=W)
                    c3 = cv[:].rearrange("p (r w) -> p r w", w=W)
                    P.act(cv[:], uu[:], AF.Copy, ["uu", "vecs"], ["cv"], scale=vcol("convw", l, 4 + c))
                    P.stt("dve", c3[:, :, 1:W], u3[:, :, 0:W - 1], vcol("convw", l, 0 + c), c3[:, :, 1:W],
                          ALU.mult, ALU.add, ["uu", "cv", "vecs"], ["cv"])
                    P.stt("dve", c3[:, :, 0:W - 1], u3[:, :, 1:W], vcol("convw", l, 8 + c), c3[:, :, 0:W - 1],
                          ALU.mult, ALU.add, ["uu", "cv", "vecs"], ["cv"])
                    P.tt("pool", st_yb[:, c, :], sbf[:], cv[:], ALU.mult, ["sbf", "cv"], ["st_yb"])
                P.dma("sp", YBIN.rearrange("(c p) t -> p c t", p=128)[:, :, cols], st_yb[:], ["st_yb"], ["YBIN"], "st_yb")
                ada_step()
                for g in range(4):
                    fm_group(OFF_FX + g * 128, 128,
                             lambda pb, pk, g=g: P.copy("act", fxT[:, g, :], pb[:], [pk], [f"fxT{g}"]))
                for sb_ in range(4):
                    tk = slice(sb_ * 128, (sb_ + 1) * 128)
                    for half in range(2):
                        pb, pk = nb()
                        for gg in range(2):
                            g = half * 2 + gg
                            P.mm(pb[:, gg * 256:(gg + 1) * 256], fxT[:, g, tk], cn[:], True, True,
                                 [f"fxT{g}", "cn"], [pk])
                        P.copy("dve" if half else "act", st_ab[:, sb_, half * 512:(half + 1) * 512], pb[:], [pk],
                               ["st_ab"])
                P.dma("sp", AB[cols, :].rearrange("(s p) c -> p s c", p=128), st_ab[:], ["st_ab"], ["AB"], "st_ab")
            if agen is not None:
                for _ in agen:
                    pass
            nb_mod[0] = 8
            P.flush()

        for dr in range(2):
            with contextlib.ExitStack() as S:
                wslot[0] = 0
                cm = P.alloc(S, [128, 4, 128], BF16)
                am = P.alloc(S, [128, 2, 4, 128], F32)
                P.dma("sp", cm[:], cm_d, (), ["cm"], "ld1")
                P.dma("sp", am[:], am_d, (), ["am"], "ld2")
                wgk = P.alloc(S, [16, QKW], BF16)
                bgk = P.alloc(S, [1, QKW], BF16)
                ones_row = P.alloc(S, [1, 128], BF16)
                P.dma("pool", wgk[:], w_gk[dr][l], (), ["wgk"], next_w())
                P.dma("pool", bgk[:], b_gk[dr][l:l + 1, :], (), ["bgk"], next_w())
                P.memset("pool", ones_row[:], 1.0, ["ones_row"])
                qTb = [P.alloc(S, [64, NH, TT], BF16) for _ in range(3)]
                kTb = [P.alloc(S, [64, NH, TT], BF16) for _ in range(3)]
                ktmb = [P.alloc(S, [128, 4, QKW], BF16) for _ in range(3)]
                vtmb = [P.alloc(S, [128, 4, VW], BF16) for _ in range(3)]
                gkb = [P.alloc(S, [16, TT], BF16) for _ in range(3)]
                qinb = [P.alloc(S, [64, NH, TT], BF16) for _ in range(2)]
                kinb = [P.alloc(S, [64, NH, TT], BF16) for _ in range(2)]
                koutb = [P.alloc(S, [128, 4, QKW], BF16) for _ in range(2)]
                gcolb = [P.alloc(S, [64, NH, 8], F32) for _ in range(2)]
                e1 = P.alloc(S, [128, QKW], F32)
                sp_ = P.alloc(S, [128, 4, QKW], BF16)
                ebT = P.alloc(S, [64, NH, TT], F32)
                eiT = P.alloc(S, [64, NH, TT], F32)
                ec = P.alloc(S, [128, 4, QKW], F32)
                attm = [P.alloc(S, [128, NH, 128], BF16) for _ in range(2)]
                Sf = P.alloc(S, [64, NH, DV], F32)
                Sb = P.alloc(S, [64, NH, DV], BF16)
                osbb = [P.alloc(S, [128, NH, TT], F32) for _ in range(2)]
                if dr == 1:
                    oflb = [P.alloc(S, [128, NH, TT], F32) for _ in range(3)]
                minc = cm[:, 0 + 2 * dr, :]
                mstr = cm[:, 1 + 2 * dr, :]
                if dr == 0:
                    visits = [(0, [("p", 0, [0, 1]), ("p", 1, [2, 3])])]
                    visits += [(t, [("s", 0, [0, 1, 2, 3])]) for t in range(1, NT)]
                else:
                    visits = [(0, [("p", 0, [1, 0]), ("p", 1, [3, 2])])]
                    visits += [(t, [("s", 0, [3, 2, 1, 0])]) for t in range(NT - 1, 0, -1)]
                NV = len(visits)
                first_s = visits[1][0]
                last_s = visits[-1][0]
                obank = [banks[i] for i in range(4)]
                obk = [f"ps{i}" for i in range(4)]
                cr, pr_ = [0], [0]

                def cb():
                    i = 4 + cr[0] % 2
                    cr[0] += 1
                    return banks[i], f"ps{i}"

                def p1b():
                    i = 6 + pr_[0] % 2
                    pr_[0] += 1
                    return banks[i], f"ps{i}"

                def g_loads(i):
                    t = visits[i][0]
                    b = i % 3
                    cols = slice(t * TT, (t + 1) * TT)
                    P.dma("sp", gkb[b][:], GK[dr][:, cols], [f"GK{dr}"], [f"gk{b}"], f"gk{b}")
                    P.dma("sp", qTb[b][:], QT[:, :, cols].rearrange("h d t -> d h t"), ["QT"], [f"qT{b}"], f"qT{b}")
                    P.dma("sp", kTb[b][:], KT[:, :, cols].rearrange("h d t -> d h t"), ["KT"], [f"kT{b}"], f"kT{b}")
                    P.dma("sp", ktmb[b][:], KTM[cols, :].rearrange("(s p) c -> p s c", p=128), ["KTM"], [f"ktm{b}"], f"ktm{b}")
                    P.dma("sp", vtmb[b][:], VTM[cols, :].rearrange("(s p) c -> p s c", p=128), ["VTM"], [f"vtm{b}"], f"vtm{b}")
                    if dr == 1:
                        P.dma("sp", oflb[b][:], OF.rearrange("(h p) t -> p h t", p=128)[:, :, cols], ["OF"], [f"ofl{b}"], f"ofl{b}")

                def p1a(i, s_):
                    b = i % 2
                    b3 = i % 3
                    tk = slice(s_ * 128, (s_ + 1) * 128)
                    pb, pk = p1b()
                    P.mm(pb[:, 0:QKW], gkb[b3][:, tk], wgk[:], True, False, [f"gk{b3}", "wgk"], [pk])
                    P.mm(pb[:, 0:QKW], ones_row[:], bgk[:], False, True, ["ones_row", "bgk"], [pk])
                    P.act(e1[:], pb[:, 0:QKW], AF.Exp, [pk], ["e1"], scale=-1.0)
                    P.act(sp_[:, s_, :], e1[:], AF.Ln, ["e1"], [f"sp{s_}"], bias=1.0)

                def p1c(i, s_):
                    b = i % 2
                    b3 = i % 3
                    tk = slice(s_ * 128, (s_ + 1) * 128)
                    pb, pk = p1b()
                    for hh in range(NH):
                        P.mm(pb[0:64, hh * 128:(hh + 1) * 128], sp_[:, s_, hh * 64:(hh + 1) * 64], minc, True, True,
                             [f"sp{s_}", "cm"], [pk])
                    pv = pb[0:64, :].rearrange("p (h t) -> p h t", h=NH)
                    P.act(ebT[:, :, tk], pv, AF.Exp, [pk], [f"ebT{s_}"], scale=-1.0 / 16)
                    P.act(eiT[:, :, tk], pv, AF.Exp, [pk], [f"eiT{s_}"], scale=1.0 / 16)
                    pb, pk = p1b()
                    P.mm(pb[:, 0:QKW], mstr, sp_[:, s_, :], True, True, [f"sp{s_}", "cm"], [pk])
                    P.act(ec[:, s_, :], pb[:, 0:QKW], AF.Exp, [pk], [f"ec{s_}"], scale=-1.0 / 16)
                    P.tt("pool", qinb[b][:, :, tk], qTb[b3][:, :, tk], ebT[:, :, tk], ALU.mult, [f"qT{b3}", f"ebT{s_}"],
                         [f"qin{b}_{s_}"])
                    P.tt("pool", kinb[b][:, :, tk], kTb[b3][:, :, tk], eiT[:, :, tk], ALU.mult, [f"kT{b3}", f"eiT{s_}"],
                         [f"kin{b}_{s_}"])
                    P.tt("pool", koutb[b][:, s_, :], ktmb[b3][:, s_, :], ec[:, s_, :], ALU.mult, [f"ktm{b3}", f"ec{s_}"],
                         [f"kout{b}_{s_}"])
                    tl = s_ * 128 + (127 if dr == 0 else 0)
                    P.copy("pool", gcolb[b][:, :, s_:s_ + 1], ebT[:, :, tl:tl + 1], [f"ebT{s_}"], [f"gcol{b}_{s_}"])

                def p2_att(i, s_):
                    b = i % 2
                    tk = slice(s_ * 128, (s_ + 1) * 128)
                    pb, pk = p1b()
                    for hh in range(NH):
                        P.mm(pb[:, hh * 128:(hh + 1) * 128], kinb[b][:, hh, tk], qinb[b][:, hh, tk], True, True,
                             [f"kin{b}_{s_}", f"qin{b}_{s_}"], [pk])
                    P.tt("dve", attm[s_ % 2][:], pb[:].rearrange("p (h t) -> p h t", h=NH), am[:, dr, :, :], ALU.mult,
                         [pk, "am"], [f"attm{s_ % 2}"])

                def p2_o(i, s_):
                    b = i % 2
                    b3 = i % 3
                    tk = slice(s_ * 128, (s_ + 1) * 128)
                    for hh in range(NH):
                        P.mm(obank[hh][:, tk], vtmb[b3][:, s_, hh * DV:(hh + 1) * DV], attm[s_ % 2][:, hh, :], s_ == 0,
                             False, [f"vtm{b3}", f"attm{s_ % 2}"], [obk[hh]])

                def chain_step(i, ch):
                    b = i % 2
                    b3 = i % 3
                    ck = slice(ch * 128, (ch + 1) * 128)
                    s_ = ch
                    pr = slice(0, 128)
                    ub = []
                    for p2_ in range(2):
                        heads = (2 * p2_, 2 * p2_ + 1)
                        for hh in heads:
                            P.mm(obank[hh][:, ck], Sb[:, hh, :], qinb[b][:, hh, ck], False, True,
                                 [f"Sb{p2_}", f"qin{b}_{s_}"], [obk[hh]])
                        pb, pk = cb()
                        for hh in heads:
                            P.mm(pb[0:64, (hh % 2) * DV:(hh % 2 + 1) * DV], koutb[b][pr, s_, hh * 64:(hh + 1) * 64],
                                 vtmb[b3][pr, s_, hh * DV:(hh + 1) * DV], True, True, [f"kout{b}_{s_}", f"vtm{b3}"], [pk])
                        ub.append((pb, pk))
                    for p2_ in range(2):
                        heads = (2 * p2_, 2 * p2_ + 1)
                        pb, pk = ub[p2_]
                        for hh in heads:
                            P.stt("dve", Sf[:, hh, :], Sf[:, hh, :], gcolb[b][:, hh, ch:ch + 1],
                                  pb[0:64, (hh % 2) * DV:(hh % 2 + 1) * DV], ALU.mult, ALU.add,
                                  [f"Sf{hh}", f"gcol{b}_{ch}", pk], [f"Sf{hh}"])
                        P.copy("dve", Sb[:, 2 * p2_:2 * p2_ + 2, :], Sf[:, 2 * p2_:2 * p2_ + 2, :],
                               [f"Sf{hh}" for hh in heads], [f"Sb{p2_}"])

                g_loads(0)
                if NV > 1:
                    g_loads(1)
                for s_ in range(4):
                    p1a(0, s_)
                for s_ in range(4):
                    p1c(0, s_)
                for i, (t, parts) in enumerate(visits):
                    b = i % 2
                    cols = slice(t * TT, (t + 1) * TT)
                    if i + 2 < NV:
                        g_loads(i + 2)
                    p2_att(i, 0)
                    p2_att(i, 1)
                    p2_o(i, 0)
                    p2_att(i, 2)
                    p2_o(i, 1)
                    p2_att(i, 3)
                    p2_o(i, 2)
                    p2_o(i, 3)
                    k = 0
                    for (kind, sidx, chunks) in parts:
                        first_visit = (kind == "p") or (t == first_s)
                        last_visit = (kind == "p") or (t == last_s)
                        if first_visit:
                            if kind == "p":
                                P.memset("dve", Sf[:], 0.0, [f"Sf{hh}" for hh in range(NH)])
                                P.memset("dve", Sb[:], 0.0, ["Sb0", "Sb1"])
                            else:
                                P.dma("sp", Sf[:], state[l, dr].rearrange("h d e -> d h e"), (),
                                      [f"Sf{hh}" for hh in range(NH)], "Sf")
                                P.copy("dve", Sb[:], Sf[:], [f"Sf{hh}" for hh in range(NH)], ["Sb0", "Sb1"])
                        for ch in chunks:
                            chain_step(i, ch)
                            if i + 1 < NV:
                                p1a(i + 1, k)
                                if k >= 1:
                                    p1c(i + 1, k - 1)
                            k += 1
                        if last_visit and kind == "p":
                            P.dma("sp", ns_out[sidx, l, dr].rearrange("h d e -> d h e"), Sf[:],
                                  [f"Sf{hh}" for hh in range(NH)], ["ns"], "Sfo")
                    osb = osbb[b]
                    if i + 1 < NV:
                        p1c(i + 1, 3)
                    if dr == 0:
                        for hh in range(NH):
                            P.copy("act" if hh % 2 else "dve", osb[:, hh, :], obank[hh][:], [obk[hh]], [f"osb{b}_{hh}"])
                        P.dma("sp", OF.rearrange("(h p) t -> p h t", p=128)[:, :, cols], osb[:],
                              [f"osb{b}_{hh}" for hh in range(NH)], ["OF"], f"osb{b}")
                    else:
                        for hh in range(NH):
                            P.tt("dve", osb[:, hh, :], obank[hh][:], oflb[i % 3][:, hh, :], ALU.add, [obk[hh], f"ofl{i % 3}"],
                                 [f"osb{b}_{hh}"])
                        P.dma("sp", OS.rearrange("(h p) t -> p h t", p=128)[:, :, cols], osb[:],
                              [f"osb{b}_{hh}" for hh in range(NH)], ["OS"], f"osb{b}")
                P.flush()

        with contextlib.ExitStack() as S:
            KTW = 256
            NLC = LS // 128
            c256 = P.alloc(S, [128, 2, 2, LP], BF16)
            P.dma("sp", c256[:], c256_d, (), ["c256"], "ld4")
            abp = P.alloc(S, [128, 2, 1024], BF16)
            frp = P.alloc(S, [128, 4, LP], BF16)
            abs_ = P.alloc(S, [128, NLC, 1024], BF16)
            AST = min(8, NLC)
            for i in range(0, NLC, AST):
                P.dma("sp", abs_[:, i:i + AST, :],
                      AB[2 * LP + i * 128:2 * LP + (i + AST) * 128, :].rearrange("(lc p) c -> p lc c", p=128),
                      ["AB"], [f"abs{i}"], f"abs{i}")
            NL2 = NLC // 2
            tabs = [P.alloc(S, [128, NL2, 2, KTW], BF16) for _ in range(2)]
            jp = P.alloc(S, [128, 2, 128], BF16)
            cpk = P.alloc(S, [1, LS // 2], BF16)
            P.dma("sp", jp[:], jp_d, (), ["jp"], "ld6")
            P.dma("sp", cpk[:], cpk_d, (), ["cpk"], "ld7")
            osl = [P.alloc(S, [128, NH, TT], F32) for _ in range(2)]
            sgl = [P.alloc(S, [128, NH, TT], BF16) for _ in range(2)]
            gal = [P.alloc(S, [128, NH, TT], BF16) for _ in range(2)]
            sq4 = P.alloc(S, [128, 2, TT], BF16)
            rs4 = P.alloc(S, [128, 2, TT], F32)

            def tail_loads(t):
                b = t % 2
                cols = slice(t * TT, (t + 1) * TT)
                P.dma("sp", osl[b][:], OS.rearrange("(h p) t -> p h t", p=128)[:, :, cols], ["OS"],
                      [f"osl{b}_{h_}" for h_ in range(NH)], f"osl{b}")
                P.dma("sp", sgl[b][:], SOG.rearrange("(h p) t -> p h t", p=128)[:, :, cols], ["SOG"], [f"sgl{b}"], f"sgl{b}")

            def tail_a(u):
                t, hh = divmod(u, NH)
                b = t % 2
                if hh == 2 and t + 1 < NT:
                    tail_loads(t + 1)
                P.act(sq4[:, u % 2, :], osl[b][:, hh, :], AF.Square, [f"osl{b}_{hh}"], [f"sq4{u % 2}"])

            def tail_b(u):
                t, hh = divmod(u, NH)
                b = t % 2
                pb, pk = nb()
                P.mm(pb[:], ones_bf[:], sq4[:, u % 2, :], True, True, ["ones", f"sq4{u % 2}"], [pk])
                P.act(rs4[:, u % 2, :], pb[:], AF.Ln, [pk], [f"rs4{u % 2}"], bias=EPS, scale=1.0 / DV)
                P.act(rs4[:, u % 2, :], rs4[:, u % 2, :], AF.Exp, [f"rs4{u % 2}"], [f"rs4{u % 2}"], scale=-0.5)
                P.tt("pool", osl[b][:, hh, :], osl[b][:, hh, :], rs4[:, u % 2, :], ALU.mult,
                     [f"osl{b}_{hh}", f"rs4{u % 2}"], [f"osl{b}_{hh}"])

            def tail_c(u):
                t, hh = divmod(u, NH)
                b = t % 2
                P.stt("dve", gal[b][:, hh, :], osl[b][:, hh, :], vcol("glan", l, 0), sgl[b][:, hh, :],
                      ALU.mult, ALU.mult, [f"osl{b}_{hh}", f"sgl{b}", "vecs"], [f"gal{b}_{hh}"])
                if hh == NH - 1:
                    cols = slice(t * TT, (t + 1) * TT)
                    P.dma("sp", GA.rearrange("(h p) t -> p h t", p=128)[:, :, cols], gal[b][:],
                          [f"gal{b}_{h_}" for h_ in range(NH)], ["GA"], f"gal{b}")

            NU = NT * NH
            tstate = [0]

            def tail_step():
                n = tstate[0]
                if n - 2 >= 0 and n - 2 < NU:
                    tail_c(n - 2)
                if n - 1 >= 0 and n - 1 < NU:
                    tail_b(n - 1)
                if n < NU:
                    tail_a(n)
                tstate[0] += 1

            tail_loads(0)
            frl = [P.alloc(S, [128, 4, KTW + 1], BF16) for _ in range(2)]
            frh = [P.alloc(S, [128, 4, KTW], BF16) for _ in range(2)]
            qsb = [P.alloc(S, [128, KTW], F32) for _ in range(2)]
            cny = P.alloc(S, [128, NLC], BF16)
            P.dma("sp", cny[:], cny_d, (), ["cny"], "ld5")
            P.dma("sp", tabs[0][:], cl_d[0], (), ["tab0"], "tab0")
            for sq_i in range(2):
                c0 = sq_i * LP
                P.dma("sp", abp[:], AB[c0:c0 + LP, :].rearrange("(lc p) c -> p lc c", p=128), ["AB"], ["abp"], "abp")
                for g in range(4):
                    pb, pk = nb()
                    n = 0
                    for lc in range(2):
                        for cs in range(2):
                            P.mm(pb[:, 0:LP], abp[:, lc, g * 256 + cs * 128:g * 256 + cs * 128 + 128],
                                 c256[:, lc, cs, :], n == 0, n == 3, ["abp", "c256"], [pk])
                            n += 1
                    P.copy("act" if g % 2 else "dve", frp[:, g, :], pb[:, 0:LP], [pk], ["frp"])
                    tail_step()
                P.dma("sp", FR.rearrange("(g p) t -> p g t", p=128)[:, :, c0:c0 + LP], frp[:], ["frp"], ["FR"], "frp")
            def akey(lc):
                return f"abs{(lc // AST) * AST}"

            rv = P.alloc(S, [128, 2, 512], BF16)

            for lc in range(NL2):
                for half in range(2):
                    cs_ = slice(half * 512, (half + 1) * 512)
                    pb, pk = nb()
                    P.mm(pb[:], jp[:, 0, :], abs_[:, NLC - 1 - lc, cs_], True, lc == 0, ["jp", akey(NLC - 1 - lc)], [pk])
                    if lc > 0:
                        P.mm(pb[:], jp[:, 1, :], abs_[:, NLC - lc, cs_], False, True, ["jp", akey(NLC - lc)], [pk])
                    ri = (2 * lc + half) % 2
                    rk = f"rv{ri}"
                    P.copy("act", rv[:, ri, :], pb[:], [pk], [rk])
                    v = rv[:, ri, :].rearrange("p (g cs m) -> p g cs m", g=2, cs=2)
                    dst = abs_[:, lc, cs_].rearrange("p (g cs m) -> p g cs m", g=2, cs=2)
                    P.tt("dve", dst[:, :, 0, :], dst[:, :, 0, :], v[:, :, 0, :], ALU.add, [rk, akey(lc)], [f"ap{lc}_{half}"])
                    P.tt("dve", dst[:, :, 1, :], dst[:, :, 1, :], v[:, :, 1, :], ALU.subtract, [rk, akey(lc)],
                         [f"bm{lc}_{half}"])
                if lc % 2 == 1:
                    tail_step()
            NKT2 = LS // (2 * KTW)
            for kt in range(NKT2):
                tb = tabs[kt % 2]
                tbk = f"tab{kt % 2}"
                if kt + 1 < NKT2:
                    P.dma("sp", tabs[(kt + 1) % 2][:], cl_d[kt + 1], (), [f"tab{(kt + 1) % 2}"], f"tab{(kt + 1) % 2}")
                lo, hi = frl[kt % 2], frh[kt % 2]
                lok, hik = f"frl{kt % 2}", f"frh{kt % 2}"
                for g in range(4):
                    pP, pPk = nb()
                    for lc in range(NL2):
                        P.mm(pP[:, 0:KTW], abs_[:, lc, g * 256:g * 256 + 128], tb[:, lc, 0, :], lc == 0, False,
                             [akey(lc), f"ap{lc}_{g // 2}", tbk], [pPk])
                    P.mm(pP[:, 0:KTW], abs_[0:1, NL2, g * 256:g * 256 + 128], cpk[0:1, kt * KTW:(kt + 1) * KTW], False, True,
                         [akey(NL2), "cpk"], [pPk])
                    pQ, pQk = nb()
                    for lc in range(NL2):
                        P.mm(pQ[:, 0:KTW], abs_[:, lc, g * 256 + 128:g * 256 + 256], tb[:, lc, 1, :], lc == 0,
                             lc == NL2 - 1, [akey(lc), f"bm{lc}_{g // 2}", tbk], [pQk])
                    q = qsb[g % 2]
                    qk = f"qsb{g % 2}"
                    P.copy("act", q[:], pQ[:, 0:KTW], [pQk], [qk])
                    P.tt("dve", lo[:, g, 0:KTW], pP[:, 0:KTW], q[:], ALU.add, [pPk, qk], [lok + f"_{g}"])
                    if kt == 0:
                        P.tt("dve", hi[:, g, KTW - 2::-1], pP[:, 1:KTW], q[:, 1:KTW], ALU.subtract, [pPk, qk], [hik + f"_{g}"])
                    else:
                        P.tt("dve", hi[:, g, ::-1], pP[:, 0:KTW], q[:], ALU.subtract, [pPk, qk], [hik + f"_{g}"])
                    tail_step()
                k0 = kt * KTW
                nlo = KTW
                if kt == NKT2 - 1:
                    pN, pNk = nb()
                    for g in range(4):
                        for lc in range(NL2):
                            P.mm(pN[:, g:g + 1], abs_[:, lc, g * 256:g * 256 + 128], cny[:, lc:lc + 1], lc == 0,
                                 False, [akey(lc), f"ap{lc}_{g // 2}", "cny"], [pNk])
                        P.mm(pN[:, g:g + 1], abs_[0:1, NL2, g * 256:g * 256 + 128], cpk[0:1, 0:1], False, True,
                             [akey(NL2), "cpk"], [pNk])
                    P.copy("act", lo[:, :, KTW:KTW + 1], pN[:, 0:4].rearrange("p (g o) -> p g o", o=1), [pNk], [lok + "_n"])
                    nlo = KTW + 1
                FRv = FR.rearrange("(g p) t -> p g t", p=128)
                P.dma("sp", FRv[:, :, 2 * LP + k0:2 * LP + k0 + nlo], lo[:, :, 0:nlo],
                      [lok + f"_{g}" for g in range(4)] + ([lok + "_n"] if nlo > KTW else []), ["FR"], lok)
                if kt == 0:
                    P.dma("sp", FRv[:, :, 2 * LP + LS - (KTW - 1):2 * LP + LS], hi[:, :, 0:KTW - 1],
                          [hik + f"_{g}" for g in range(4)], ["FR"], hik)
                else:
                    P.dma("sp", FRv[:, :, 2 * LP + LS - k0 - (KTW - 1):2 * LP + LS - k0 + 1], hi[:, :, 0:KTW],
                          [hik + f"_{g}" for g in range(4)], ["FR"], hik)
            while tstate[0] < NU + 2:
                tail_step()
            P.flush()

        with contextlib.ExitStack() as S:
            wslot[0] = 0
            wmg = P.alloc(S, [128, NCH, 3 * D], BF16)
            wao = P.alloc(S, [128, 4, D], BF16)
            wbo = P.alloc(S, [128, 4, D], BF16)
            wco = P.alloc(S, [128, 4, D], BF16)
            woo = P.alloc(S, [128, NCH, D], BF16)
            P.dma("pool", wao[:], w_a_out[l].rearrange("(kc p) n -> p kc n", p=128), (), ["wao"], next_w())
            P.dma("pool", wbo[:], w_b_out[l].rearrange("(kc p) n -> p kc n", p=128), (), ["wbo"], next_w())
            P.dma("pool", wco[:], w_c_out[l].rearrange("(kc p) n -> p kc n", p=128), (), ["wco"], next_w())
            mgk = load_w(wmg, w_in[l][:, OFF_MG:PIN], NCH, 3 * D, "wmg")
            P.dma("pool", woo[:], w_o[l].rearrange("(kc p) n -> p kc n", p=128), (), ["woo"], next_w())
            xtb = [P.alloc(S, [128, NCH, TT], F32) for _ in range(2)]
            hb2 = [P.alloc(S, [128, NCH, TT], BF16) for _ in range(2)]
            gatb = [P.alloc(S, [128, 4, TT], BF16) for _ in range(2)]
            ybtb = [P.alloc(S, [128, 4, TT], BF16) for _ in range(2)]
            frtb = [P.alloc(S, [128, 4, TT], BF16) for _ in range(2)]
            rs = P.alloc(S, [128, TT], F32)
            tmp = P.alloc(S, [128, 2, TT], F32)
            gsb = [P.alloc(S, [128, TT], BF16) for _ in range(3)]
            tb3 = [P.alloc(S, [128, TT], F32) for _ in range(3)]
            Y = P.alloc(S, [128, NCH, TT], BF16)
            src = xsrc(l)

            def b_loads(t):
                b = t % 2
                cols = slice(t * TT, (t + 1) * TT)
                P.dma("sp", hb2[b][:], xview(H1, t), (), [f"h{b}_{c}" for c in range(NCH)], f"hld{b}")
                P.dma("sp", xtb[b][:], xview(src, t), (), [f"xt{b}_{c}" for c in range(NCH)], f"xt{b}")
                P.dma("sp", gatb[b][:], GA.rearrange("(c p) t -> p c t", p=128)[:, :, cols], ["GA"], [f"gat{b}"], f"gat{b}")
                P.dma("sp", ybtb[b][:], YBIN.rearrange("(c p) t -> p c t", p=128)[:, :, cols], ["YBIN"], [f"ybt{b}"], f"ybt{b}")
                P.dma("sp", frtb[b][:], FR.rearrange("(c p) t -> p c t", p=128)[:, :, cols], ["FR"], [f"frt{b}"], f"frt{b}")

            def b_prologue(t, parts=(0, 1, 2, 3)):
                b = t % 2
                norm_parts(parts, xtb[b], [f"xt{b}_{c}" for c in range(NCH)], hb2[b], [f"h{b}_{c}" for c in range(NCH)],
                           rs, "rs", tmp, "tmp", a1, f"a1{l}", 0, l, ci_of(t), sq_act=(t == 0))

            b_loads(0)
            for t in range(NT):
                ci = ci_of(t)
                b = t % 2
                xt, h, gat, ybt, frt = xtb[b], hb2[b], gatb[b], ybtb[b], frtb[b]
                xk = [f"xt{b}_{c}" for c in range(NCH)]
                hkeys = [f"h{b}_{c}" for c in range(NCH)]
                if t + 1 < NT:
                    b_loads(t + 1)
                for j in range(NCH):
                    js = slice(j * 128, (j + 1) * 128)
                    ybanks = []
                    for (wt, wkk, it, itk) in ((wao, "wao", gat, f"gat{b}"), (wbo, "wbo", ybt, f"ybt{b}"),
                                               (wco, "wco", frt, f"frt{b}")):
                        pb, pk = nb()
                        for kc in range(4):
                            P.mm(pb[:], wt[:, kc, js], it[:, kc, :], kc == 0, kc == 3, [wkk, itk], [pk])
                        ybanks.append((pb, pk))
                    for bb in range(3):
                        pb, pk = nb()
                        c0 = bb * D + j * 128
                        for kc in range(NCH):
                            P.mm(pb[:], wmg[:, kc, c0:c0 + 128], h[:, kc, :], kc == 0, kc == NCH - 1,
                                 [mgk[c0 // 512], hkeys[kc]], [pk])
                        P.act(gsb[bb][:], pb[:], AF.Sigmoid, [pk], [f"gsb{bb}"])
                        P.tt("dve", tb3[bb][:], ybanks[bb][0][:], gsb[bb][:], ALU.mult, [ybanks[bb][1], f"gsb{bb}"],
                             [f"tb3{bb}"])
                    P.tt("pool", tb3[0][:], tb3[0][:], tb3[1][:], ALU.add, ["tb30", "tb31"], ["tb30"])
                    P.tt("pool", Y[:, j, :], tb3[0][:], tb3[2][:], ALU.add, ["tb30", "tb32"], [f"Y{j}"])
                for j in range(NCH):
                    js = slice(j * 128, (j + 1) * 128)
                    pb, pk = nb()
                    for kc in range(NCH):
                        P.mm(pb[:], woo[:, kc, js], Y[:, kc, :], kc == 0, kc == NCH - 1, ["woo", f"Y{kc}"], [pk])
                    P.stt("dve", xt[:, j, :], pb[:], modc(l, 2, j, ci), xt[:, j, :], ALU.mult, ALU.add,
                          [pk, xk[j], f"mod{l}"], [xk[j]])
                P.dma("sp", xview(xs, t), xt[:], xk, ["xs"], f"xto{b}")
            P.flush()

        with contextlib.ExitStack() as S:
            wslot[0] = 0
            wup = P.alloc(S, [128, NCH, 2 * DFF], BF16)
            upk = load_w(wup, w_up[l], NCH, 2 * DFF, "wup")
            wdn = P.alloc(S, [128, NFF, D], BF16)
            dnk = load_w(wdn, w_down[l], NFF, D, "wdn", piece=256)
            xt = P.alloc(S, [128, NCH, TT], F32)
            hb2 = [P.alloc(S, [128, NCH, TT], BF16) for _ in range(2)]
            rs = P.alloc(S, [128, TT], F32)
            hid = P.alloc(S, [128, NFF, TT], BF16)
            sgb = P.alloc(S, [128, 2, TT], BF16)
            xe = P.alloc(S, [128, 2, TT], F32)
            xo = P.alloc(S, [128, 2, TT], F32)
            xkeys = [f"xt{c}" for c in range(NCH)]
            xsv = xs.rearrange("(c p) t -> p c t", p=128)

            def c_prologue(t, parts=(0, 1, 2, 3)):
                b = t % 2
                norm_parts(parts, xt, xkeys, hb2[b], [f"h{b}_{c}" for c in range(NCH)], rs, "rs", xo, "xo",
                           a2, f"a2{l}", 3, l, ci_of(t), sq_act=(t == 0))

            P.dma("sp", xt[:], xview(xs, 0), (), xkeys, "xt")
            c_prologue(0)
            for t in range(NT):
                ci = ci_of(t)
                b = t % 2
                h = hb2[b]
                hkeys = [f"h{b}_{c}" for c in range(NCH)]
                if t + 1 < NT:
                    P.dma("sp", xt[:], xview(xs, t + 1), (), xkeys, "xt")
                for j in range(NFF):
                    pg, pgk = nb()
                    c0 = j * 128
                    for kc in range(NCH):
                        P.mm(pg[:], wup[:, kc, c0:c0 + 128], h[:, kc, :], kc == 0, kc == NCH - 1,
                             [upk[c0 // 512], hkeys[kc]], [pgk])
                    pu, puk = nb()
                    c1 = DFF + j * 128
                    for kc in range(NCH):
                        P.mm(pu[:], wup[:, kc, c1:c1 + 128], h[:, kc, :], kc == 0, kc == NCH - 1,
                             [upk[c1 // 512], hkeys[kc]], [puk])
                    P.act(sgb[:, j % 2, :], pg[:], AF.Silu, [pgk], [f"sg{j % 2}"])
                    P.tt("dve", hid[:, j, :], pu[:], sgb[:, j % 2, :], ALU.mult, [puk, f"sg{j % 2}"], [f"hid{j}"])
                    if t + 1 < NT and j in (6, 8, 10, 12):
                        c_prologue(t + 1, ((j - 6) // 2,))
                cols = slice(t * TT, (t + 1) * TT)
                P.dma("sp", xe[:, 0, :], xsv[:, 0, cols], (), ["xe0"], "xe0")
                for j in range(NCH):
                    js = slice(j * 128, (j + 1) * 128)
                    pb, pk = nb()
                    for kc in range(NFF):
                        P.mm(pb[:], wdn[:, kc, js], hid[:, kc, :], kc == 0, kc == NFF - 1,
                             [dnk[(j * 128) // 256], f"hid{kc}"], [pk])
                    if j + 1 < NCH:
                        P.dma("sp", xe[:, (j + 1) % 2, :], xsv[:, j + 1, cols], (), [f"xe{(j + 1) % 2}"], f"xe{(j + 1) % 2}")
                    P.stt("dve", xo[:, j % 2, :], pb[:], modc(l, 5, j, ci), xe[:, j % 2, :], ALU.mult, ALU.add,
                          [pk, f"xe{j % 2}", f"mod{l}"], [f"xo{j % 2}"])
                    P.dma("sp", xsv[:, j, cols], xo[:, j % 2, :], [f"xo{j % 2}"], (), f"xo{j % 2}")
            P.flush()

    with contextlib.ExitStack() as S:
        xb = [P.alloc(S, [128, NCH, TT], F32) for _ in range(2)]
        sqb = [P.alloc(S, [128, NCH, TT], BF16) for _ in range(2)]
        rsb = [P.alloc(S, [128, TT], F32) for _ in range(2)]
        o = voff["nf"]
        P.dma("sp", xb[0][:], xview(xs, 0), (), [f"x0_{c}" for c in range(NCH)], "xt0")
        for t in range(NT):
            b = t % 2
            xk = [f"x{b}_{c}" for c in range(NCH)]
            if t + 1 < NT:
                P.dma("sp", xb[1 - b][:], xview(xs, t + 1), (), [f"x{1 - b}_{c}" for c in range(NCH)], f"xt{1 - b}")
            P.act(sqb[b][:], xb[b][:], AF.Square, xk, [f"sq{b}"])
            pb, pk = nb()
            for c in range(NCH):
                P.mm(pb[:], ones_bf[:], sqb[b][:, c, :], c == 0, c == NCH - 1, ["ones", f"sq{b}"], [pk])
            P.act(rsb[b][:], pb[:], AF.Ln, [pk], [f"rs{b}"], bias=EPS, scale=1.0 / D)
            P.act(rsb[b][:], rsb[b][:], AF.Exp, [f"rs{b}"], [f"rs{b}"], scale=-0.5)
            for c in range(NCH):
                P.stt("dve", xb[b][:, c, :], xb[b][:, c, :], vecs[:, o + c:o + c + 1], rsb[b][:], ALU.mult, ALU.mult,
                      [xk[c], f"rs{b}", "vecs"], [xk[c]])
            P.dma("sp", xview(y_out, t), xb[b][:], xk, ["y"], f"xto{b}")
        P.flush()
    P.outer.close()
    return nc, P


_CACHE = {}


def run(inputs, depth, LS, n_cores, seq_of_core, sample_owner=None):
    f32 = lambda a: np.ascontiguousarray(np.asarray(a, dtype=np.float32))
    key = (depth, LS)
    if key not in _CACHE:
        _CACHE[key] = (build(depth, LS), _consts(LS))
    (nc, P), (cmk, amk, cnk, c256k, clk, cnyk, jpk, cpkk, identk) = _CACHE[key]
    xp, xsm = f32(inputs["x_prompt"]), f32(inputs["x_sample"])
    vec = _pack_vecs(depth, f32(inputs["b_ada"]), f32(inputs["norm1"]), f32(inputs["norm2"]),
                     f32(inputs["gla_norm"]), f32(inputs["conv_w"]), f32(inputs["norm_f"]))
    shared = {k: f32(inputs[k]) for k in ("w_ada", "w_in", "w_gk_f", "w_gk_b", "b_gk_f", "b_gk_b", "w_a_out",
                                           "w_b_out", "w_c_out", "w_o", "w_up", "w_down")}
    shared.update({"vecs": vec, "cmask": cmk, "amask": amk, "cn": cnk, "c256": c256k, "cl": clk, "cny": cnyk, "jperm": jpk, "cpk": cpkk, "ident": identk})
    in_maps = []
    voff, _ = _vec_layout(depth)
    for c in range(n_cores):
        s = seq_of_core(c)
        dup = (sample_owner is not None) and (not sample_owner(c))
        xs_c = np.zeros_like(xsm[s]) if dup else xsm[s]
        xall = np.concatenate([xp[2 * c], xp[2 * c + 1], xs_c], axis=0)
        c_s = np.zeros_like(f32(inputs["c"])[s]) if dup else f32(inputs["c"])[s]
        cnd = np.stack([_fm(f32(inputs["c_ctx"])), _fm(c_s)], axis=-1)
        m = dict(shared)
        if dup:
            v2 = vec.copy()
            for l in range(depth):
                o = voff[("bada", l)]
                v2[:, o + 1:o + 96:2] = 0.0
            m["vecs"] = v2
        m["x0"] = np.ascontiguousarray(xall.T)
        m["state"] = np.zeros_like(f32(inputs["state_gla"])[s]) if dup else f32(inputs["state_gla"])[s]
        m["cond"] = np.ascontiguousarray(cnd)
        in_maps.append(m)
    res = run_bass_kernel_spmd(nc, in_maps, core_ids=list(range(n_cores)))
    return res.results


def kernel(**inputs):
    depth, LS, n_cores = 4, 4096, 8
    res = run(inputs, depth, LS, n_cores, lambda c: c // 2, sample_owner=lambda c: c % 2 == 0)
    B = 2 * n_cores
    y_prompt = np.empty((B, LP, D), np.float32)
    y_sample = np.empty((n_cores // 2, LS, D), np.float32)
    ns = np.empty((B, depth, 2, NH, DK, DV), np.float32)
    for c in range(n_cores):
        y = np.asarray(res[c]["y"], np.float32)
        y_prompt[2 * c] = y[:, 0:LP].T
        y_prompt[2 * c + 1] = y[:, LP:2 * LP].T
        if c % 2 == 0:
            y_sample[c // 2] = y[:, 2 * LP:].T
        nsc = np.asarray(res[c]["ns"], np.float32)
        ns[2 * c] = nsc[0]
        ns[2 * c + 1] = nsc[1]
    return (y_prompt, y_sample, ns)
```
